# Optimizing a Trainium2 kernel written in Bass

```python
import math
import jax, jax.numpy as jnp
from jax import lax
import numpy as np

D_MODEL = 1024
BATCH = 8
SEQ = 8192
DEPTH = 1

N_HEADS = 8
HEAD_DIM = 64
V_DIM = 2 * HEAD_DIM
QK_WIDTH = N_HEADS * 2 * HEAD_DIM
D_ATTN = N_HEADS * V_DIM
Q_BLOCK = 128
ROPE_THETA = 10000.0
D_RNN = D_MODEL
N_RNN_BLOCKS = 8
RNN_BLOCK = D_RNN // N_RNN_BLOCKS
CONV_WIDTH = 4
LRU_C = 8.0
D_FF = 4 * D_MODEL
N_BRANCH = 2
IN_COLS = 2 * D_RNN + 2 * QK_WIDTH + D_ATTN + N_BRANCH * D_MODEL
N_MOD = 6
NORM_EPS = 1e-6

kernel_name = "hybrid_rglru_diffattn_gated_block"


def lambda_init(layer_idx):
    return 0.8 - 0.6 * math.exp(-0.3 * layer_idx)


def rms_norm(x):
    xf = x.astype(jnp.float32)
    return (xf * lax.rsqrt(jnp.mean(xf * xf, axis=-1, keepdims=True) + NORM_EPS)).astype(x.dtype)


def rope_tables(positions):
    inv_freq = ROPE_THETA ** (-jnp.arange(0, HEAD_DIM, 2, dtype=jnp.float32) / HEAD_DIM)
    ang = positions.astype(jnp.float32)[..., None] * inv_freq
    return jnp.cos(ang)[:, :, None, None, :], jnp.sin(ang)[:, :, None, None, :]


def apply_rope(x, cos, sin):
    xf = x.astype(jnp.float32)
    x1, x2 = xf[..., : HEAD_DIM // 2], xf[..., HEAD_DIM // 2:]
    return jnp.concatenate([x1 * cos - x2 * sin, x2 * cos + x1 * sin], axis=-1).astype(x.dtype)


def causal_depthwise_conv(x, w, b):
    y = lax.conv_general_dilated(
        x, w[:, None, :].astype(x.dtype), window_strides=(1,), padding=[(CONV_WIDTH - 1, 0)],
        dimension_numbers=("NWC", "WIO", "NWC"), feature_group_count=x.shape[-1])
    return y + b


def rg_lru(x, w_a, b_a, w_x, b_x, lam):
    B, S, C = x.shape
    xb = x.reshape(B, S, N_RNN_BLOCKS, RNN_BLOCK)
    r = jax.nn.sigmoid(jnp.einsum("bsnk,nkj->bsnj", xb, w_a).reshape(B, S, C) + b_a)
    i = jax.nn.sigmoid(jnp.einsum("bsnk,nkj->bsnj", xb, w_x).reshape(B, S, C) + b_x)
    log_a = -LRU_C * r.astype(jnp.float32) * jax.nn.softplus(-lam.astype(jnp.float32))
    a = jnp.exp(log_a)
    u = jnp.sqrt(-jnp.expm1(2.0 * log_a)) * (i * x).astype(jnp.float32)

    def combine(left, right):
        a1, b1 = left
        a2, b2 = right
        return a1 * a2, a2 * b1 + b2

    _, h = lax.associative_scan(combine, (a, u), axis=1)
    return h.astype(x.dtype)


def diff_attention(q, k, v, lam, subln_gain, lam_init):
    B, S = q.shape[0], q.shape[1]
    nb = S // Q_BLOCK
    scale = HEAD_DIM ** -0.5
    qb = q.reshape(B, nb, Q_BLOCK, N_HEADS, 2, HEAD_DIM).transpose(1, 0, 3, 4, 2, 5)
    kt = k.transpose(0, 2, 3, 1, 4)
    vt = v.transpose(0, 2, 1, 3)
    key_pos = jnp.arange(S)

    def one_block(args):
        qi, blk = args
        s = jnp.einsum("bhcqd,bhckd->bhcqk", qi, kt).astype(jnp.float32) * scale
        q_pos = blk * Q_BLOCK + jnp.arange(Q_BLOCK)
        mask = key_pos[None, :] <= q_pos[:, None]
        p = jax.nn.softmax(jnp.where(mask, s, -jnp.inf), axis=-1)
        attn = p[:, :, 0] - lam * p[:, :, 1]
        return jnp.einsum("bhqk,bhkv->bhqv", attn.astype(vt.dtype), vt)

    o = lax.map(one_block, (qb, jnp.arange(nb)))
    o = o.transpose(1, 0, 3, 2, 4).reshape(B, S, N_HEADS, V_DIM)
    o = rms_norm(o) * subln_gain * (1.0 - lam_init)
    return o.reshape(B, S, D_ATTN)


def setup_inputs(seed: int = 0) -> dict:
    key = jax.random.key(seed)
    ks = jax.random.split(key, 26)
    f32 = jnp.float32
    nrm = lambda k, shape, s: jax.random.normal(k, shape, f32) * s
    a_c = jax.random.uniform(ks[11], (DEPTH, D_RNN), f32, 0.9, 0.999)
    p = a_c ** (1.0 / LRU_C)
    return {
        "x": nrm(ks[0], (BATCH, SEQ, D_MODEL), 1.0),
        "c": nrm(ks[1], (BATCH, D_MODEL), 1.0),
        "positions": (jnp.arange(SEQ, dtype=jnp.int32)[None, :]
                      + jax.random.randint(ks[2], (BATCH, 1), 0, 1024, dtype=jnp.int32)),
        "w_ada": nrm(ks[3], (DEPTH, D_MODEL, N_MOD * D_MODEL), 0.5 * D_MODEL ** -0.5),
        "b_ada": nrm(ks[4], (DEPTH, N_MOD * D_MODEL), 0.01),
        "w_in": nrm(ks[5], (DEPTH, D_MODEL, IN_COLS), D_MODEL ** -0.5),
        "conv_w": nrm(ks[6], (DEPTH, CONV_WIDTH, D_RNN), CONV_WIDTH ** -0.5),
        "conv_b": nrm(ks[7], (DEPTH, D_RNN), 0.01),
        "rglru_wa": nrm(ks[8], (DEPTH, N_RNN_BLOCKS, RNN_BLOCK, RNN_BLOCK), RNN_BLOCK ** -0.5),
        "rglru_ba": nrm(ks[9], (DEPTH, D_RNN), 0.01),
        "rglru_wx": nrm(ks[10], (DEPTH, N_RNN_BLOCKS, RNN_BLOCK, RNN_BLOCK), RNN_BLOCK ** -0.5),
        "rglru_bx": nrm(ks[12], (DEPTH, D_RNN), 0.01),
        "rglru_lambda": jnp.log(p) - jnp.log1p(-p),
        "q_norm_gain": 1.0 + nrm(ks[13], (DEPTH, HEAD_DIM), 0.02),
        "k_norm_gain": 1.0 + nrm(ks[14], (DEPTH, HEAD_DIM), 0.02),
        "lambda_q1": nrm(ks[15], (DEPTH, HEAD_DIM), 0.1),
        "lambda_k1": nrm(ks[16], (DEPTH, HEAD_DIM), 0.1),
        "lambda_q2": nrm(ks[17], (DEPTH, HEAD_DIM), 0.1),
        "lambda_k2": nrm(ks[18], (DEPTH, HEAD_DIM), 0.1),
        "subln_gain": 1.0 + nrm(ks[19], (DEPTH, V_DIM), 0.02),
        "w_proj_rnn": nrm(ks[20], (DEPTH, D_RNN, D_MODEL), D_RNN ** -0.5),
        "w_proj_attn": nrm(ks[21], (DEPTH, D_ATTN, D_MODEL), D_ATTN ** -0.5),
        "w_out": nrm(ks[22], (DEPTH, D_MODEL, D_MODEL), D_MODEL ** -0.5),
        "w_ff1": nrm(ks[23], (DEPTH, D_MODEL, D_FF), D_MODEL ** -0.5),
        "w_ff2": nrm(ks[24], (DEPTH, D_FF, D_MODEL), D_FF ** -0.5),
    }


def reference(x, c, positions, w_ada, b_ada, w_in, conv_w, conv_b, rglru_wa, rglru_ba, rglru_wx,
              rglru_bx, rglru_lambda, q_norm_gain, k_norm_gain, lambda_q1, lambda_k1, lambda_q2,
              lambda_k2, subln_gain, w_proj_rnn, w_proj_attn, w_out, w_ff1, w_ff2):
    B, S, _ = x.shape
    cos, sin = rope_tables(positions)
    c_act = jax.nn.silu(c)
    o1 = D_RNN
    o2 = o1 + D_RNN
    o3 = o2 + QK_WIDTH
    o4 = o3 + QK_WIDTH
    o5 = o4 + D_ATTN
    for l in range(DEPTH):
        lam_init = lambda_init(l)
        mod = (c_act @ w_ada[l] + b_ada[l])[:, None, :]
        shift1, scale1, gate1, shift2, scale2, gate2 = jnp.split(mod, N_MOD, axis=-1)

        h = rms_norm(x) * (1.0 + scale1) + shift1
        proj = h @ w_in[l]
        xr, gr, q, k, v, gm = jnp.split(proj, [o1, o2, o3, o4, o5], axis=-1)

        xr = causal_depthwise_conv(xr, conv_w[l], conv_b[l])
        y_rnn = rg_lru(xr, rglru_wa[l], rglru_ba[l], rglru_wx[l], rglru_bx[l], rglru_lambda[l]) * jax.nn.gelu(gr)

        q = apply_rope(rms_norm(q.reshape(B, S, N_HEADS, 2, HEAD_DIM)) * q_norm_gain[l], cos, sin)
        k = apply_rope(rms_norm(k.reshape(B, S, N_HEADS, 2, HEAD_DIM)) * k_norm_gain[l], cos, sin)
        v = v.reshape(B, S, N_HEADS, V_DIM)
        lam = (jnp.exp(jnp.sum(lambda_q1[l] * lambda_k1[l]).astype(jnp.float32))
               - jnp.exp(jnp.sum(lambda_q2[l] * lambda_k2[l]).astype(jnp.float32)) + lam_init)
        y_attn = diff_attention(q, k, v, lam, subln_gain[l], lam_init)

        gates = jax.nn.sigmoid(gm).reshape(B, S, N_BRANCH, D_MODEL)
        merged = gates[:, :, 0] * (y_rnn @ w_proj_rnn[l]) + gates[:, :, 1] * (y_attn @ w_proj_attn[l])
        x = x + gate1 * (merged @ w_out[l])

        h2 = rms_norm(x) * (1.0 + scale2) + shift2
        ff = jnp.square(jax.nn.relu(h2 @ w_ff1[l])) @ w_ff2[l]
        x = x + gate2 * ff
    return x
```

```python
import math
from contextlib import ExitStack
import numpy as np
import concourse.bass as bass
import concourse.mybir as mybir
from concourse.bass_utils import run_bass_kernel_spmd

F32 = mybir.dt.float32
BF16 = mybir.dt.bfloat16
I32 = mybir.dt.int32
AF = mybir.ActivationFunctionType
ALU = mybir.AluOpType
AX = mybir.AxisListType

D = 1024
NCH = 8
NH = 8
DFF = 4096
EPS = 1e-6
LAM_INIT = 0.8 - 0.6 * math.exp(-0.3 * 0)
TWO_PI = 2.0 * math.pi
C1_2PI = 6.28125
C2_2PI = TWO_PI - C1_2PI
PI_SAFE = 3.1415925


class K:
    def __init__(self, nc):
        self.nc = nc
        self.engs = {"pe": nc.tensor, "act": nc.scalar, "dve": nc.vector, "pool": nc.gpsimd, "sp": nc.sync}
        self.sem = {}
        self.cnt = {}
        self.waited = {e: {} for e in self.engs}
        self._ctx = []
        for e in self.engs:
            cm = nc.semaphore("s_" + e)
            self.sem[e] = cm.__enter__()
            self._ctx.append(cm)
            self.cnt[e] = 0
        self.dsem = {}
        self.dcnt = {}
        self.lastw = {}
        self.readers = {}
        self.nins = 0

    def _wait(self, e, deps):
        for d in deps:
            if d is None:
                continue
            kind, key, val = d
            if kind == "e" and key == e and e == "pe":
                continue
            wk = (kind, key)
            if self.waited[e].get(wk, 0) >= val:
                continue
            self.waited[e][wk] = val
            self.engs[e].wait_ge(self.sem[key] if kind == "e" else self.dsem[key], val)

    def _deps(self, reads, writes):
        deps = []
        for r in reads:
            if r in self.lastw:
                deps.append(self.lastw[r])
        for w in writes:
            if w in self.lastw:
                deps.append(self.lastw[w])
            deps.extend(self.readers.get(w, ()))
        return deps

    def _commit(self, tok, reads, writes):
        for r in reads:
            self.readers.setdefault(r, []).append(tok)
        for w in writes:
            self.lastw[w] = tok
            self.readers[w] = []

    def op(self, e, ins, reads=(), writes=(), sig=True):
        self._wait(e, self._deps(reads, writes))
        i = ins(self.engs[e])
        self.nins += 1
        if sig:
            self.cnt[e] += 1
            i.then_inc(self.sem[e], 1)
            tok = ("e", e, self.cnt[e])
        else:
            tok = ("e", e, self.cnt[e] + 1)
        self._commit(tok, reads, writes)
        return tok

    def dma(self, e, slot, out, in_, reads=(), writes=(), **kw):
        if slot not in self.dsem:
            cm = self.nc.semaphore("d_" + slot)
            self.dsem[slot] = cm.__enter__()
            self._ctx.append(cm)
            self.dcnt[slot] = 0
        self._wait(e, self._deps(reads, writes))
        self.engs[e].dma_start(out=out, in_=in_, **kw).then_inc(self.dsem[slot], 16)
        self.nins += 1
        self.dcnt[slot] += 16
        tok = ("d", slot, self.dcnt[slot])
        self._commit(tok, reads, writes)
        return tok

    def dma_group(self, e, slot, pairs, keys):
        deps = self._deps((), keys)
        tok = None
        for (o, i) in pairs:
            tok = self.dma(e, slot, o, i, reads=(), writes=())
            if deps:
                pass
        for kk in keys:
            self.lastw[kk] = tok
            self.readers[kk] = []
        return tok

    def barrier(self, engines=("pe", "act", "dve", "pool", "sp")):
        toks = [("e", e, self.cnt[e]) for e in self.engs if self.cnt[e] > 0]
        toks += [("d", s, c) for s, c in self.dcnt.items()]
        for e in engines:
            self._wait(e, toks)


def build(S, upto=5):
    NT = S // 128
    NG = S // 512
    nc = bass.Bass("TRN2", target_bir_lowering=False)
    k = K(nc)

    def din(name, shape, dt=F32):
        return nc.dram_tensor(name, shape, dt, kind="ExternalInput").ap()

    x_d = din("x", [S, D])
    cT_d = din("cT", [128, 8])
    pos_d = din("posT", [128, NT], I32)
    invf_d = din("invf", [128, 32])
    wada_d = din("w_ada", [D, 6 * D])
    bada_d = din("b_adaT", [128, 48])
    win_d = din("w_in", [D, 7168])
    cw_d = din("conv_wT", [128, 32])
    cb_d = din("conv_bT", [128, 8])
    wa_d = din("rglru_wa", [8, 128, 128])
    wx_d = din("rglru_wx", [8, 128, 128])
    ba_d = din("baT", [128, 8])
    bx_d = din("bxT", [128, 8])
    lam_d = din("lamT", [128, 8])
    gq_d = din("gq", [1, 64])
    gk_d = din("gk", [1, 64])
    lq1_d = din("lq1", [1, 64])
    lk1_d = din("lk1", [1, 64])
    lq2_d = din("lq2", [1, 64])
    lk2_d = din("lk2", [1, 64])
    subg_d = din("subg", [1, 128])
    wpr_d = din("w_proj_rnn", [D, D])
    wpa_d = din("w_proj_attn", [D, D])
    wo_d = din("w_out", [D, D])
    wf1_d = din("w_ff1", [D, DFF])
    wf2_d = din("w_ff2", [DFF, D])
    out_d = nc.dram_tensor("out", [S, D], F32, kind="ExternalOutput").ap()

    def dscr(name, shape, dt):
        return nc.dram_tensor(name, shape, dt, kind="Internal").ap()

    qT_s = dscr("qT_s", [NH, 128, S], BF16)
    kT_s = dscr("kT_s", [NH, 128, S], BF16)
    v_s = dscr("v_s", [S, D], BF16)
    yrT_s = dscr("yrT_s", [NCH, 128, S], BF16)
    yaT_s = dscr("yaT_s", [NH, 128, S], BF16)
    x1_s = dscr("x1_s", [S, D], F32)
    gate_s = dscr("gate_s", [1, 2 * D], F32)

    sb = lambda n, s, d=F32: nc.alloc_sbuf_tensor("s_" + n, s, d)

    ident = sb("ident", [128, 128], BF16)
    trimask = sb("trimask", [128, 128], BF16)
    cT = sb("cT", [128, 8])
    badaT = sb("badaT", [128, 48])
    modT = sb("modT", [128, 48])
    sc1p = sb("sc1p", [128, 8])
    sc2p = sb("sc2p", [128, 8])
    cw = sb("cw", [128, 32])
    cb = sb("cb", [128, 8])
    hba = sb("hba", [128, 8])
    hbx = sb("hbx", [128, 8])
    lam = sb("lam", [128, 8])
    c1 = sb("c1", [128, 8])
    c2 = sb("c2", [128, 8])
    gq_b = sb("gq_b", [128, 64])
    gk_b = sb("gk_b", [128, 64])
    ngq_b = sb("ngq_b", [128, 64])
    ngk_b = sb("ngk_b", [128, 64])
    lq1_b = sb("lq1_b", [128, 64]); lk1_b = sb("lk1_b", [128, 64])
    lq2_b = sb("lq2_b", [128, 64]); lk2_b = sb("lk2_b", [128, 64])
    neg_lam = sb("neg_lam", [128, 1])
    subg_b = sb("subg_b", [128, 128])
    g1_b = sb("g1_b", [128, D])
    g2_b = sb("g2_b", [128, D])
    mhalf = sb("mhalf", [128, 64])
    hstate = sb("hstate", [128, 8])

    for nm, t, d in [("cT", cT, cT_d), ("badaT", badaT, bada_d), ("cw", cw, cw_d), ("cb", cb, cb_d),
                     ("hba", hba, ba_d), ("hbx", hbx, bx_d), ("lam", lam, lam_d)]:
        k.dma("sp", "c_" + nm, t[:], d, writes=[nm])
    for nm, t, d in [("gq_b", gq_b, gq_d), ("gk_b", gk_b, gk_d), ("lq1_b", lq1_b, lq1_d), ("lk1_b", lk1_b, lk1_d),
                     ("lq2_b", lq2_b, lq2_d), ("lk2_b", lk2_b, lk2_d), ("subg_b", subg_b, subg_d)]:
        k.dma("sp", "c_" + nm, t[:], d.partition_broadcast(128), writes=[nm])

    idf = sb("idf", [128, 128])
    k.op("pool", lambda e: e.memset(idf[:], 1.0), writes=["idf"])
    k.op("pool", lambda e: e.affine_select(out=idf[:], in_=idf[:], pattern=[[-1, 128]], compare_op=ALU.is_equal,
                                           fill=0.0, base=0, channel_multiplier=1), reads=["idf"], writes=["idf"])
    k.op("pool", lambda e: e.tensor_copy(out=ident[:], in_=idf[:]), reads=["idf"], writes=["ident"])
    k.op("pool", lambda e: e.memset(idf[:], 1.0), writes=["idf"])
    k.op("pool", lambda e: e.affine_select(out=idf[:], in_=idf[:], pattern=[[1, 128]], compare_op=ALU.is_ge,
                                           fill=0.0, base=0, channel_multiplier=-1), reads=["idf"], writes=["idf"])
    k.op("pool", lambda e: e.tensor_copy(out=trimask[:], in_=idf[:]), reads=["idf"], writes=["trimask"])
    k.op("pool", lambda e: e.memset(mhalf[:], -0.5), writes=["mhalf"])
    k.op("pool", lambda e: e.memset(hstate[:], 0.0), writes=["hstate"])

    k.op("dve", lambda e: e.tensor_scalar(out=hba[:], in0=hba[:], scalar1=0.5, scalar2=None, op0=ALU.mult), reads=["hba"], writes=["hba"])
    k.op("dve", lambda e: e.tensor_scalar(out=hbx[:], in0=hbx[:], scalar1=0.5, scalar2=None, op0=ALU.mult), reads=["hbx"], writes=["hbx"])
    k.op("dve", lambda e: e.tensor_scalar(out=ngq_b[:], in0=gq_b[:], scalar1=-1.0, scalar2=None, op0=ALU.mult), reads=["gq_b"], writes=["ngq_b"])
    k.op("dve", lambda e: e.tensor_scalar(out=ngk_b[:], in0=gk_b[:], scalar1=-1.0, scalar2=None, op0=ALU.mult), reads=["gk_b"], writes=["ngk_b"])
    k.op("dve", lambda e: e.tensor_scalar(out=subg_b[:], in0=subg_b[:], scalar1=1.0 - LAM_INIT, scalar2=None, op0=ALU.mult), reads=["subg_b"], writes=["subg_b"])
    t_ab = sb("t_ab", [128, 8]); t_mx = sb("t_mx", [128, 8]); t_e = sb("t_e", [128, 8])
    k.op("dve", lambda e: e.tensor_scalar(out=t_mx[:], in0=lam[:], scalar1=-1.0, scalar2=None, op0=ALU.mult), reads=["lam"], writes=["t_mx"])
    k.op("dve", lambda e: e.tensor_tensor(out=t_ab[:], in0=lam[:], in1=t_mx[:], op=ALU.max), reads=["lam", "t_mx"], writes=["t_ab"])
    k.op("dve", lambda e: e.tensor_scalar(out=t_mx[:], in0=t_mx[:], scalar1=0.0, scalar2=None, op0=ALU.max), reads=["t_mx", "t_ab"], writes=["t_mx"])
    k.op("act", lambda e: e.activation(out=t_e[:], in_=t_ab[:], func=AF.Exp, scale=-1.0), reads=["t_ab"], writes=["t_e"])
    k.op("act", lambda e: e.activation(out=t_e[:], in_=t_e[:], func=AF.Ln, bias=1.0), reads=["t_e"], writes=["t_e"])
    k.op("dve", lambda e: e.tensor_tensor(out=t_mx[:], in0=t_mx[:], in1=t_e[:], op=ALU.add), reads=["t_mx", "t_e"], writes=["t_mx"])
    k.op("dve", lambda e: e.tensor_scalar(out=c1[:], in0=t_mx[:], scalar1=-4.0, scalar2=None, op0=ALU.mult), reads=["t_mx"], writes=["c1"])
    k.op("dve", lambda e: e.tensor_scalar(out=c2[:], in0=t_mx[:], scalar1=-8.0, scalar2=None, op0=ALU.mult), reads=["t_mx"], writes=["c2"])
    t_p = sb("t_p", [128, 64]); t_s = sb("t_s", [128, 2])
    k.op("dve", lambda e: e.tensor_tensor(out=t_p[:], in0=lq1_b[:], in1=lk1_b[:], op=ALU.mult), reads=["lq1_b", "lk1_b"], writes=["t_p"])
    k.op("dve", lambda e: e.tensor_reduce(out=t_s[:, 0:1], in_=t_p[:], axis=AX.X, op=ALU.add), reads=["t_p"], writes=["t_s0"])
    k.op("dve", lambda e: e.tensor_tensor(out=t_p[:], in0=lq2_b[:], in1=lk2_b[:], op=ALU.mult), reads=["lq2_b", "lk2_b", "t_s0"], writes=["t_p"])
    k.op("dve", lambda e: e.tensor_reduce(out=t_s[:, 1:2], in_=t_p[:], axis=AX.X, op=ALU.add), reads=["t_p"], writes=["t_s1"])
    k.op("act", lambda e: e.activation(out=t_s[:], in_=t_s[:], func=AF.Exp), reads=["t_s0", "t_s1"], writes=["t_s"])
    k.op("dve", lambda e: e.scalar_tensor_tensor(out=neg_lam[:], in0=t_s[:, 1:2], scalar=-LAM_INIT, in1=t_s[:, 0:1],
                                                 op0=ALU.add, op1=ALU.subtract), reads=["t_s"], writes=["neg_lam"])

    c_act2 = sb("c_act2", [128, 8, 2])
    k.op("act", lambda e: e.activation(out=c_act2[:, :, 0], in_=cT[:], func=AF.Silu), reads=["cT"], writes=["ca0"])
    k.op("act", lambda e: e.activation(out=c_act2[:, :, 1], in_=cT[:], func=AF.Silu), reads=["cT"], writes=["ca1"])
    psall = nc.alloc_psum_tensor("psall", [128, 8, 512], F32)
    banks = [psall[:, i, :] for i in range(8)]
    ps_mod = banks[7][:, 0:96].rearrange("p (j t) -> p j t", t=2)
    wada_v = wada_d.rearrange("(c p) n -> p c n", p=128)
    wq_cm = nc.sbuf_tensor("s_wqkv", [128, 8, 3072], BF16)
    wqkv = wq_cm.__enter__()
    k.dma_group("pool", "w_wqkv", [(wqkv[:, kc, c0:c0 + 1024], win_d[kc * 128:(kc + 1) * 128, 2048 + c0:2048 + c0 + 1024])
                                   for kc in range(8) for c0 in range(0, 3072, 1024)], ["wqkv"])
    with ExitStack() as es:
     wada0 = es.enter_context(nc.sbuf_tensor("s_wada0", [128, 8, 1024], F32))
     wada1 = es.enter_context(nc.sbuf_tensor("s_wada1", [128, 8, 1024], F32))
     if True:
        wadas = [wada0, wada1]
        for jg in range(2):
            wsb = wadas[jg % 2]
            k._wait("sp", k._deps((), [("wada", jg % 2)]))
            k.dma_group("sp", "wada%d" % (jg % 2), [(wsb[:, kc, :], wada_d[kc * 128:(kc + 1) * 128, jg * 1024:(jg + 1) * 1024]) for kc in range(8)],
                        [("wada", jg % 2)])
            for jj in range(8):
                j = jg * 8 + jj
                for kc in range(8):
                    k.op("pe", lambda e, wsb=wsb, kc=kc, jj=jj, j=j: e.matmul(
                        ps_mod[:, j, :], lhsT=wsb[:, kc, jj * 128:(jj + 1) * 128], rhs=c_act2[:, kc, :],
                        start=(kc == 0), stop=(kc == 7)),
                        reads=[("wada", jg % 2), "ca0", "ca1"], writes=["ps_mod"], sig=(kc == 7))
        k.op("dve", lambda e: e.tensor_tensor(out=modT[:, 0:16], in0=ps_mod[:, 0:16, 0], in1=badaT[:, 0:16], op=ALU.add),
             reads=["ps_mod", "badaT"], writes=["modT"])
        k.barrier()
    k.op("dve", lambda e: e.tensor_scalar(out=sc1p[:], in0=modT[:, 8:16], scalar1=1.0, scalar2=None, op0=ALU.add), reads=["modT"], writes=["sc1p"])


    def bank_bf(i):
        return banks[i].bitcast(BF16)

    def norm_p1(tag, src, srckey, nsub, xn, junk, ssq, rstd, rstd_mode="pool"):
        for s in range(nsub):
            k.op("act", lambda e, s=s: e.activation(out=junk[:], in_=src[:, s, :], func=AF.Square, accum_out=ssq[:, s:s + 1]),
                 reads=[srckey], writes=[(tag, "ssq", s), (tag, "junk")])
        if rstd_mode == "act":
            k.op("act", lambda e: e.activation(out=rstd[:, 0:nsub], in_=ssq[:, 0:nsub], func=AF.Ln, scale=1.0 / D, bias=EPS),
                 reads=[(tag, "ssq", s) for s in range(nsub)], writes=[(tag, "rstd")])
            k.op("act", lambda e: e.activation(out=rstd[:, 0:nsub], in_=rstd[:, 0:nsub], func=AF.Exp, scale=-0.5),
                 reads=[(tag, "rstd")], writes=[(tag, "rstd")])
        else:
            k.op("pool", lambda e: e.tensor_scalar(out=rstd[:, 0:nsub], in0=ssq[:, 0:nsub], scalar1=1.0 / D, scalar2=EPS, op0=ALU.mult, op1=ALU.add),
                 reads=[(tag, "ssq", s) for s in range(nsub)], writes=[(tag, "rstd")])
            k.op("pool", lambda e: e.tensor_tensor(out=rstd[:, 0:nsub], in0=rstd[:, 0:nsub], in1=mhalf[:, 0:nsub], op=ALU.pow),
                 reads=[(tag, "rstd"), "mhalf"], writes=[(tag, "rstd")])
        for s in range(nsub):
            eng = "dve" if s % 2 == 0 else "pool"
            k.op(eng, lambda e, s=s: e.tensor_scalar(out=xn[:, s, :], in0=src[:, s, :], scalar1=rstd[:, s:s + 1], scalar2=0.0, op0=ALU.mult, op1=ALU.add),
                 reads=[srckey, (tag, "rstd")], writes=[(tag, "xn", s)])

    def norm_p2(tag, scp, shift, hT, hTkey, nsub, xn, tbanks, evac_eng):
        for c in range(8):
            bi = tbanks[c % len(tbanks)]
            pT = bank_bf(bi)
            for s in range(nsub):
                k.op("pe", lambda e, s=s, c=c, pT=pT: e.transpose(pT[:, s * 128:(s + 1) * 128], xn[:, s, c * 128:(c + 1) * 128], ident[:]),
                     reads=[(tag, "xn", s), "ident"], writes=[("bank", bi)], sig=(s == nsub - 1))
            ee = evac_eng if evac_eng != "alt" else ("act" if c % 2 == 0 else "dve")
            if ee == "act":
                k.op("act", lambda e, c=c, pT=pT: e.activation(out=hT[:, c, :], in_=pT[:, 0:nsub * 128], func=AF.Identity,
                                                               scale=scp[:, c:c + 1], bias=shift[:, c:c + 1]),
                     reads=[("bank", bi), "modT", "sc1p", "sc2p"], writes=[(hTkey, c)])
            else:
                k.op("dve", lambda e, c=c, pT=pT: e.tensor_scalar(out=hT[:, c, :], in0=pT[:, 0:nsub * 128], scalar1=scp[:, c:c + 1],
                                                                  scalar2=shift[:, c:c + 1], op0=ALU.mult, op1=ALU.add),
                     reads=[("bank", bi), "modT", "sc1p", "sc2p"], writes=[(hTkey, c)])

    def norm_T(tag, src, srckey, scp, shift, hT, hTkey, nsub, xn, junk, ssq, rstd, tbanks, evac_eng):
        norm_p1(tag, src, srckey, nsub, xn, junk, ssq, rstd)
        norm_p2(tag, scp, shift, hT, hTkey, nsub, xn, tbanks, evac_eng)

    def load_w_cast(name, wsb, wd, row0, col0, ncols, nk, piece=1024):
        for c0 in range(0, ncols, piece):
            c1_ = min(ncols, c0 + piece)
            pairs = [(wsb[:, kc, c0:c1_], wd[row0 + kc * 128:row0 + (kc + 1) * 128, col0 + c0:col0 + c1_]) for kc in range(nk)]
            k.dma_group("pool", "w_%s_%d" % (name, c0), pairs, [(name, c0)])

    def wkeys(name, lo, hi, piece=1024):
        return [(name, c0) for c0 in range((lo // piece) * piece, hi, piece)]

    with ExitStack() as es:
     cos_t = es.enter_context(nc.sbuf_tensor("s_cos_t", [128, NT, 32], F32))
     sin_t = es.enter_context(nc.sbuf_tensor("s_sin_t", [128, NT, 32], F32))
     if True:
      with ExitStack() as es:
       pos_i = es.enter_context(nc.sbuf_tensor("s_pos_i", [128, NT], I32))
       pos_f = es.enter_context(nc.sbuf_tensor("s_pos_f", [128, NT], F32))
       invf = es.enter_context(nc.sbuf_tensor("s_invf", [128, 32], F32))
       ang = es.enter_context(nc.sbuf_tensor("s_ang", [128, NT, 32], F32))
       rk = es.enter_context(nc.sbuf_tensor("s_rk", [128, NT, 32], F32))
       ki = es.enter_context(nc.sbuf_tensor("s_ki", [128, NT, 32], I32))
       dd = es.enter_context(nc.sbuf_tensor("s_dd", [128, NT, 32], F32))
       dc = es.enter_context(nc.sbuf_tensor("s_dc", [128, NT, 32], F32))
       if True:
            k.dma("sp", "pos", pos_i[:], pos_d, writes=["pos_i"])
            k.dma("sp", "invf", invf[:], invf_d, writes=["invf"])
            k.op("dve", lambda e: e.tensor_copy(out=pos_f[:], in_=pos_i[:]), reads=["pos_i"], writes=["pos_f"])
            k.op("dve", lambda e: e.tensor_tensor(out=ang[:], in0=pos_f[:].unsqueeze(2).to_broadcast([128, NT, 32]),
                                                  in1=invf[:].unsqueeze(1).to_broadcast([128, NT, 32]), op=ALU.mult),
                 reads=["pos_f", "invf"], writes=["ang"])
            k.op("dve", lambda e: e.tensor_scalar(out=rk[:], in0=ang[:], scalar1=1.0 / TWO_PI, scalar2=None, op0=ALU.mult), reads=["ang"], writes=["rk"])
            k.op("dve", lambda e: e.tensor_copy(out=ki[:], in_=rk[:]), reads=["rk"], writes=["ki"])
            k.op("dve", lambda e: e.tensor_copy(out=rk[:], in_=ki[:]), reads=["ki"], writes=["rk"])
            k.op("dve", lambda e: e.scalar_tensor_tensor(out=dd[:], in0=rk[:], scalar=-C1_2PI, in1=ang[:], op0=ALU.mult, op1=ALU.add),
                 reads=["rk", "ang"], writes=["dd"])
            k.op("dve", lambda e: e.scalar_tensor_tensor(out=dd[:], in0=rk[:], scalar=-C2_2PI, in1=dd[:], op0=ALU.mult, op1=ALU.add),
                 reads=["rk", "dd"], writes=["dd"])
            k.op("dve", lambda e: e.tensor_scalar(out=dd[:], in0=dd[:], scalar1=PI_SAFE, scalar2=-PI_SAFE, op0=ALU.min, op1=ALU.max),
                 reads=["dd"], writes=["dd"])
            k.op("act", lambda e: e.activation(out=sin_t[:], in_=dd[:], func=AF.Sin), reads=["dd"], writes=["sin_t"])
            k.op("dve", lambda e: e.tensor_scalar(out=dc[:], in0=dd[:], scalar1=math.pi / 2, scalar2=None, op0=ALU.add), reads=["dd"], writes=["dc"])
            k.op("dve", lambda e: e.tensor_scalar(out=rk[:], in0=dc[:], scalar1=math.pi, scalar2=None, op0=ALU.is_gt), reads=["dc"], writes=["rk"])
            k.op("dve", lambda e: e.scalar_tensor_tensor(out=dc[:], in0=rk[:], scalar=-TWO_PI, in1=dc[:], op0=ALU.mult, op1=ALU.add),
                 reads=["rk", "dc"], writes=["dc"])
            k.op("dve", lambda e: e.tensor_scalar(out=dc[:], in0=dc[:], scalar1=PI_SAFE, scalar2=-PI_SAFE, op0=ALU.min, op1=ALU.max),
                 reads=["dc"], writes=["dc"])
            k.op("act", lambda e: e.activation(out=cos_t[:], in_=dc[:], func=AF.Sin), reads=["dc"], writes=["cos_t"])
            k.barrier()
      with ExitStack() as es:
       xt1 = es.enter_context(nc.sbuf_tensor("s_b_xt", [128, 4, D], F32))
       xn = es.enter_context(nc.sbuf_tensor("s_b_xn", [128, 4, D], BF16))
       junk = es.enter_context(nc.sbuf_tensor("s_b_junk", [128, D], BF16))
       hTd = es.enter_context(nc.sbuf_tensor("s_b_hT", [128, 2, 8, 512], BF16))
       ssq = es.enter_context(nc.sbuf_tensor("s_b_ssq", [128, 4], F32))
       rstd = es.enter_context(nc.sbuf_tensor("s_b_rstd", [128, 4], F32))
       Tq = es.enter_context(nc.sbuf_tensor("s_Tq", [128, 4, 2, 64], F32))
       Tk = es.enter_context(nc.sbuf_tensor("s_Tk", [128, 4, 2, 64], F32))
       sqj = es.enter_context(nc.sbuf_tensor("s_sqj", [128, 2, D], BF16))
       gss = es.enter_context(nc.sbuf_tensor("s_gss", [128, 2, 16], F32))
       grs = es.enter_context(nc.sbuf_tensor("s_grs", [128, 2, 16], F32))
       m1 = es.enter_context(nc.sbuf_tensor("s_m1", [128, 2, 2, D], F32))
       m2 = es.enter_context(nc.sbuf_tensor("s_m2", [128, 2, 2, D], F32))
       ob = es.enter_context(nc.sbuf_tensor("s_ob", [128, 2, 2, D], BF16))
       qTst = es.enter_context(nc.sbuf_tensor("s_qTst", [128, 2, 8, 512], BF16))
       kTst = es.enter_context(nc.sbuf_tensor("s_kTst", [128, 2, 8, 512], BF16))
       vst = es.enter_context(nc.sbuf_tensor("s_vst", [128, 2, D], BF16))
       if True:
        def load_x(g):
            k.dma("sp", "b_xt", xt1[:], x_d[g * 512:(g + 1) * 512, :].rearrange("(s p) d -> p s d", p=128), writes=[("xt", 0)])

        def np1(g):
            norm_p1("n1b", xt1, ("xt", 0), 4, xn, junk, ssq, rstd, rstd_mode="act")

        def np2(g):
            norm_p2("n1b", sc1p, modT[:, 0:8], hTd[:, g % 2], ("hT", g % 2), 4, xn, [0, 1], "alt")

        def tables(g):
            for (T, gb, ngb, nm) in ((Tq, gq_b, ngq_b, "Tq"), (Tk, gk_b, ngk_b, "Tk")):
                cs = cos_t[:, 4 * g:4 * g + 4, :]
                sn = sin_t[:, 4 * g:4 * g + 4, :]
                bc = lambda a, lo: a[:, lo:lo + 32].unsqueeze(1).to_broadcast([128, 4, 32])
                k.op("pool", lambda e: e.tensor_tensor(out=T[:, :, 0, 0:32], in0=cs, in1=bc(gb, 0), op=ALU.mult), reads=["cos_t", "gq_b", "gk_b"], writes=[(nm, 0)])
                k.op("pool", lambda e: e.tensor_tensor(out=T[:, :, 0, 32:64], in0=cs, in1=bc(gb, 32), op=ALU.mult), reads=["cos_t", "gq_b", "gk_b"], writes=[(nm, 1)])
                k.op("pool", lambda e: e.tensor_tensor(out=T[:, :, 1, 0:32], in0=sn, in1=bc(ngb, 32), op=ALU.mult), reads=["sin_t", "ngq_b", "ngk_b"], writes=[(nm, 2)])
                k.op("pool", lambda e: e.tensor_tensor(out=T[:, :, 1, 32:64], in0=sn, in1=bc(gb, 0), op=ALU.mult), reads=["sin_t", "gq_b", "gk_b"], writes=[(nm, 3)])

        def mm_sub(g, s):
            hT = hTd[:, g % 2]
            pairs = {}
            for wi, (nm, col0) in enumerate((("q", 0), ("k", 1024), ("v", 2048))):
                idx = (s * 3 + wi) % 3
                b0 = 2 + 2 * idx
                pairs[nm] = b0
                for half in (0, 1):
                    bi = b0 + half
                    for kc in range(8):
                        k.op("pe", lambda e, bi=bi, kc=kc, col0=col0, half=half: e.matmul(
                            banks[bi], lhsT=hT[:, kc, s * 128:(s + 1) * 128],
                            rhs=wqkv[:, kc, col0 + half * 512:col0 + (half + 1) * 512], start=(kc == 0), stop=(kc == 7)),
                            reads=[(("hT", g % 2), kc), "wqkv"], writes=[("bank", bi)], sig=(kc == 7))
            return pairs

        def chains(g, s, pairs):
            gp = g % 2
            sp_ = s % 2
            info = []
            for qi, nm in enumerate(("q", "k")):
                b0 = pairs[nm]
                bkeys = [("bank", b0), ("bank", b0 + 1)]
                px = psall[:, b0:b0 + 2, :]
                k.op("act", lambda e: e.activation(out=sqj[:, qi, :].rearrange("p (b n) -> p b n", b=2), in_=px, func=AF.Square),
                     reads=(), writes=[("sqj", qi)] + bkeys)
                info.append((qi, nm, b0, bkeys, px))
            for (qi, nm, b0, bkeys, px) in info:
                T = Tq if nm == "q" else Tk
                Tn = "Tq" if nm == "q" else "Tk"
                px3 = px.rearrange("p b (g d) -> p (b g) d", d=64)
                px4 = px.rearrange("p b (g t d) -> p (b g) t d", t=2, d=32)
                m1v = m1[:, qi, sp_, :].rearrange("p (g d) -> p g d", d=64)
                m2v = m2[:, qi, sp_, :].rearrange("p (g t d) -> p g t d", t=2, d=32)
                tb = lambda t, lo: T[:, s, t, lo:lo + 32].unsqueeze(1).to_broadcast([128, 16, 32])
                tkeys = [(Tn, i) for i in range(4)]
                k.op("dve", lambda e: e.tensor_tensor(out=m1v, in0=px3, in1=T[:, s, 0, :].unsqueeze(1).to_broadcast([128, 16, 64]), op=ALU.mult),
                     reads=bkeys + tkeys, writes=[("m1", qi, sp_)])
                k.op("dve", lambda e: e.tensor_tensor(out=m2v[:, :, 0, :], in0=px4[:, :, 1, :], in1=tb(1, 0), op=ALU.mult),
                     reads=bkeys + tkeys, writes=[("m2a", qi, sp_)])
                k.op("dve", lambda e: e.tensor_tensor(out=m2v[:, :, 1, :], in0=px4[:, :, 0, :], in1=tb(1, 32), op=ALU.mult),
                     reads=bkeys + tkeys, writes=[("m2b", qi, sp_)])
                k.op("dve", lambda e: e.tensor_reduce(out=gss[:, qi, :], in_=sqj[:, qi, :].rearrange("p (g d) -> p g d", d=64), axis=AX.X, op=ALU.add),
                     reads=[("sqj", qi)], writes=[("gss", qi)])
                k.op("pool", lambda e: e.tensor_tensor(out=m1[:, qi, sp_, :], in0=m1[:, qi, sp_, :], in1=m2[:, qi, sp_, :], op=ALU.add),
                     reads=[("m1", qi, sp_), ("m2a", qi, sp_), ("m2b", qi, sp_)], writes=[("m1", qi, sp_)])
            b0 = pairs["v"]
            k.op("act", lambda e: e.activation(out=vst[:, s % 2, :].rearrange("p (b n) -> p b n", b=2), in_=psall[:, b0:b0 + 2, :], func=AF.Copy),
                 reads=[("bank", b0), ("bank", b0 + 1)], writes=[("vst", s % 2)])
            k.dma("sp", "vst%d" % (s % 2), v_s[g * 512 + s * 128:g * 512 + (s + 1) * 128, :], vst[:, s % 2, :],
                  reads=[("vst", s % 2)], writes=[("v_s", g, s)])
            for (qi, nm, b0, bkeys, px) in info:
                m1v = m1[:, qi, sp_, :].rearrange("p (g d) -> p g d", d=64)
                k.op("act", lambda e: e.activation(out=grs[:, qi, :], in_=gss[:, qi, :], func=AF.Ln, scale=1.0 / 64, bias=EPS),
                     reads=[("gss", qi)], writes=[("grs", qi)])
                k.op("act", lambda e: e.activation(out=grs[:, qi, :], in_=grs[:, qi, :], func=AF.Exp, scale=-0.5),
                     reads=[("grs", qi)], writes=[("grs", qi)])
                k.op("pool", lambda e: e.tensor_tensor(out=ob[:, qi, s % 2, :].rearrange("p (g d) -> p g d", d=64), in0=m1v,
                                                       in1=grs[:, qi, :].unsqueeze(2).to_broadcast([128, 16, 64]), op=ALU.mult),
                     reads=[("m1", qi, sp_), ("grs", qi)], writes=[("ob", qi, s % 2)])

        def make_deferred(g, s):
            gp = g % 2

            def run():
                for qi, nm in enumerate(("q", "k")):
                    pT = bank_bf(qi)
                    for h in range(8):
                        k.op("pe", lambda e, h=h: e.transpose(pT[:, h * 128:(h + 1) * 128], ob[:, qi, s % 2, h * 128:(h + 1) * 128], ident[:]),
                             reads=[("ob", qi, s % 2), "ident"], writes=[("bank", qi)], sig=(h == 7))
                    st = qTst if nm == "q" else kTst
                    k.op("act", lambda e: e.activation(out=st[:, gp, :, s * 128:(s + 1) * 128],
                                                       in_=pT.rearrange("p (h t) -> p h t", t=128), func=AF.Copy),
                         reads=[("bank", qi)], writes=[(nm + "st", gp, s)])
                if s == 3:
                    k.dma("sp", "qst%d" % gp, qT_s.rearrange("h p t -> p h t")[:, :, g * 512:(g + 1) * 512], qTst[:, gp, :, :],
                          reads=[("qst", gp, s_) for s_ in range(4)], writes=[("qT_s", g)])
                    k.dma("sp", "kst%d" % gp, kT_s.rearrange("h p t -> p h t")[:, :, g * 512:(g + 1) * 512], kTst[:, gp, :, :],
                          reads=[("kst", gp, s_) for s_ in range(4)], writes=[("kT_s", g)])
            return run

        load_x(0)
        np1(0)
        if NG > 1:
            load_x(1)
        np2(0)
        dq = []
        for g in range(NG):
            tables(g)
            for s in range(4):
                if len(dq) >= 2:
                    dq.pop(0)()
                pairs = mm_sub(g, s)
                chains(g, s, pairs)
                if s == 0 and g + 1 < NG:
                    np1(g + 1)
                    if g + 2 < NG:
                        load_x(g + 2)
                if s == 1 and g + 1 < NG:
                    np2(g + 1)
                dq.append(make_deferred(g, s))
        while dq:
            dq.pop(0)()
        k.barrier()

    wq_cm.__exit__(None, None, None)
    if upto <= 1:
        k.barrier()
        return nc, k
    JB = 4
    with ExitStack() as es:
     wrg = es.enter_context(nc.sbuf_tensor("s_wrg", [128, 8, 2048], BF16))
     wab = es.enter_context(nc.sbuf_tensor("s_wab", [128, 8, 128], BF16))
     wxb = es.enter_context(nc.sbuf_tensor("s_wxb", [128, 8, 128], BF16))
     xt = es.enter_context(nc.sbuf_tensor("s_a_xt", [128, 4, D], F32))
     xn = es.enter_context(nc.sbuf_tensor("s_a_xn", [128, 4, D], BF16))
     junk = es.enter_context(nc.sbuf_tensor("s_a_junk", [128, D], BF16))
     hTd = es.enter_context(nc.sbuf_tensor("s_a_hT", [128, 2, 8, 512], BF16))
     ssq = es.enter_context(nc.sbuf_tensor("s_a_ssq", [128, 4], F32))
     rstd = es.enter_context(nc.sbuf_tensor("s_a_rstd", [128, 4], F32))
     xrb = es.enter_context(nc.sbuf_tensor("s_xrb", [128, 8, 516], BF16))
     Wd = es.enter_context(nc.sbuf_tensor("s_Wd", [128, 8, 4, 128], BF16))
     xcb = es.enter_context(nc.sbuf_tensor("s_xcb", [128, 2, JB, 512], BF16))
     gg = es.enter_context(nc.sbuf_tensor("s_gg", [128, 2, JB, 512], BF16))
     trr = es.enter_context(nc.sbuf_tensor("s_trr", [128, 2, JB, 512], BF16))
     tii = es.enter_context(nc.sbuf_tensor("s_tii", [128, 2, JB, 512], BF16))
     aa = es.enter_context(nc.sbuf_tensor("s_aa", [128, 2, JB, 512], F32))
     sqv = es.enter_context(nc.sbuf_tensor("s_sqv", [128, 2, JB, 512], F32))
     uu = es.enter_context(nc.sbuf_tensor("s_uu", [128, JB, 512], F32))
     hs = es.enter_context(nc.sbuf_tensor("s_hs", [128, 2, 512], F32))
     yst = es.enter_context(nc.sbuf_tensor("s_yst", [128, 8, 512], BF16))
     if True:
        load_w_cast("wrg", wrg, win_d, 0, 0, 2048, 8)
        k.dma("pool", "w_wab", wab[:], wa_d.rearrange("n k j -> k n j"), writes=["wab"])
        k.dma("pool", "w_wxb", wxb[:], wx_d.rearrange("n k j -> k n j"), writes=["wxb"])
        k.op("pool", lambda e: e.memset(xrb[:, :, 0:3], 0.0), writes=[("xrh", j) for j in range(8)])
        for j in range(8):
            for tap in range(4):
                k.op("pool" if (j * 4 + tap) % 2 else "dve", lambda e, j=j, tap=tap: e.tensor_scalar(
                    out=Wd[:, j, tap, :], in0=ident[:], scalar1=cw[:, j * 4 + tap:j * 4 + tap + 1], scalar2=0.0, op0=ALU.mult, op1=ALU.add),
                    reads=["ident", "cw"], writes=[("Wd", j, tap)])

        def load_x(g):
            k.dma("sp", "a_xt", xt[:], x_d[g * 512:(g + 1) * 512, :].rearrange("(s p) d -> p s d", p=128), writes=[("xt", 0)])

        def np1(g):
            norm_p1("n1a", xt, ("xt", 0), 4, xn, junk, ssq, rstd)

        def np2(g):
            norm_p2("n1a", sc1p, modT[:, 0:8], hTd[:, g % 2], ("hT", g % 2), 4, xn, [0, 1], "act")

        def stageA1(b):
            g, jb, par = b // 2, (b % 2) * JB, b % 2
            hT = hTd[:, g % 2]
            for jj in range(JB):
                j = jb + jj
                bx_, bg_ = 2 + (jj % 2), 4 + (jj % 2)
                for (bi, col0) in ((bx_, 0), (bg_, 1024)):
                    for kc in range(8):
                        k.op("pe", lambda e, bi=bi, kc=kc, col0=col0: e.matmul(
                            banks[bi], lhsT=wrg[:, kc, col0 + j * 128:col0 + (j + 1) * 128], rhs=hT[:, kc, :],
                            start=(kc == 0), stop=(kc == 7)),
                            reads=[(("hT", g % 2), kc)] + wkeys("wrg", col0 + j * 128, col0 + (j + 1) * 128), writes=[("bank", bi)], sig=(kc == 7))
                k.op("act", lambda e: e.activation(out=xrb[:, j, 3:515], in_=banks[bx_], func=AF.Copy),
                     reads=[("bank", bx_)], writes=[("xr", j)])
                k.op("act", lambda e: e.activation(out=gg[:, par, jj, :], in_=banks[bg_], func=AF.Gelu_apprx_tanh),
                     reads=[("bank", bg_)], writes=[("gg", par, jj)])

        def stageA2(b):
            g, jb, par = b // 2, (b % 2) * JB, b % 2
            for jj in range(JB):
                j = jb + jj
                cvb = 6 + (jj % 2)
                for tap in range(4):
                    k.op("pe", lambda e, tap=tap: e.matmul(banks[cvb], lhsT=Wd[:, j, tap, :], rhs=xrb[:, j, tap:tap + 512],
                                                          start=(tap == 0), stop=(tap == 3)),
                         reads=[("xr", j), ("xrh", j), ("Wd", j, tap)], writes=[("bank", cvb)], sig=(tap == 3))
                k.op("dve", lambda e: e.tensor_scalar(out=xcb[:, par, jj, :], in0=banks[cvb], scalar1=cb[:, j:j + 1], scalar2=None, op0=ALU.add),
                     reads=[("bank", cvb), "cb"], writes=[("xcb", par, jj)])
                k.op("pool", lambda e: e.tensor_copy(out=xrb[:, j, 0:3], in_=xrb[:, j, 512:515]), reads=[("xr", j)], writes=[("xrh", j)])
            for jj in range(JB):
                j = jb + jj
                bx_ = 2 + (jj % 2)
                br_ = 4 + (jj % 2)
                k.op("pe", lambda e: e.matmul(banks[br_], lhsT=wab[:, j, :], rhs=xcb[:, par, jj, :], start=True, stop=True),
                     reads=[("xcb", par, jj), "wab"], writes=[("bank", br_)])
                k.op("act", lambda e: e.activation(out=trr[:, par, jj, :], in_=banks[br_], func=AF.Tanh, scale=0.5, bias=hba[:, j:j + 1]),
                     reads=[("bank", br_), "hba"], writes=[("trr", par, jj)])
                k.op("pe", lambda e: e.matmul(banks[bx_], lhsT=wxb[:, j, :], rhs=xcb[:, par, jj, :], start=True, stop=True),
                     reads=[("xcb", par, jj), "wxb"], writes=[("bank", bx_)])
                k.op("act", lambda e: e.activation(out=tii[:, par, jj, :], in_=banks[bx_], func=AF.Tanh, scale=0.5, bias=hbx[:, j:j + 1]),
                     reads=[("bank", bx_), "hbx"], writes=[("tii", par, jj)])

        def stageBC(b):
            g, jb, par = b // 2, (b % 2) * JB, b % 2
            for jj in range(JB):
                j = jb + jj
                k.op("act", lambda e: e.activation(out=aa[:, par, jj, :], in_=trr[:, par, jj, :], func=AF.Exp, scale=c1[:, j:j + 1], bias=c1[:, j:j + 1]),
                     reads=[("trr", par, jj), "c1"], writes=[("aa", par, jj)])
                k.op("act", lambda e: e.activation(out=sqv[:, par, jj, :], in_=trr[:, par, jj, :], func=AF.Exp, scale=c2[:, j:j + 1], bias=c2[:, j:j + 1]),
                     reads=[("trr", par, jj), "c2"], writes=[("sqv", par, jj)])
            for jj in range(JB):
                k.op("act", lambda e: e.activation(out=sqv[:, par, jj, :], in_=sqv[:, par, jj, :], func=AF.Sqrt, scale=-0.25, bias=0.25),
                     reads=[("sqv", par, jj)], writes=[("sqv", par, jj)])

        def stageD(b):
            g, jb, par = b // 2, (b % 2) * JB, b % 2
            for jj in range(JB):
                j = jb + jj
                ui = jj % 2
                k.op("dve", lambda e: e.scalar_tensor_tensor(out=uu[:, jj, :], in0=tii[:, par, jj, :], scalar=1.0, in1=xcb[:, par, jj, :],
                                                             op0=ALU.add, op1=ALU.mult),
                     reads=[("tii", par, jj), ("xcb", par, jj)], writes=[("uu", jj)])
                k.op("pool", lambda e: e.tensor_tensor(out=uu[:, jj, :], in0=uu[:, jj, :], in1=sqv[:, par, jj, :], op=ALU.mult),
                     reads=[("uu", jj), ("sqv", par, jj)], writes=[("uu", jj)])
            for jj in range(JB):
                j = jb + jj
                ui = jj % 2
                k.op("dve", lambda e: e.tensor_tensor_scan(out=hs[:, ui, :], data0=aa[:, par, jj, :], data1=uu[:, jj, :],
                                                           initial=hstate[:, j:j + 1], op0=ALU.mult, op1=ALU.add),
                     reads=[("aa", par, jj), ("uu", jj), ("hstate", j), "hstate"], writes=[("hs", ui)])
                k.op("dve", lambda e: e.tensor_copy(out=hstate[:, j:j + 1], in_=hs[:, ui, 511:512]),
                     reads=[("hs", ui)], writes=[("hstate", j)])
                k.op("pool", lambda e: e.tensor_tensor(out=yst[:, j, :], in0=hs[:, ui, :], in1=gg[:, par, jj, :], op=ALU.mult),
                     reads=[("hs", ui), ("gg", par, jj)], writes=[("yst", j)])
            if b % 2 == 1:
                k.dma("sp", "yst", yrT_s.rearrange("j p t -> p j t")[:, :, g * 512:(g + 1) * 512], yst[:],
                      reads=[("yst", j) for j in range(8)], writes=[("yrT_s", g)])

        NB2 = 2 * NG
        load_x(0)
        np1(0)
        if NG > 1:
            load_x(1)
        np2(0)
        stageA1(0)
        stageA2(0)
        for b in range(NB2):
            g = b // 2
            nxt = (b % 2 == 0 and g + 1 < NG)
            if nxt:
                np1(g + 1)
                if g + 2 < NG:
                    load_x(g + 2)
            if b + 1 < NB2:
                stageA1(b + 1)
            if nxt:
                np2(g + 1)
            stageBC(b)
            if b + 1 < NB2:
                stageA2(b + 1)
            stageD(b)
        k.barrier()

    if upto <= 2:
        k.barrier()
        return nc, k
    with ExitStack() as es:
     kTh = es.enter_context(nc.sbuf_tensor("s_kTh", [128, 2, S], BF16))
     qTh = es.enter_context(nc.sbuf_tensor("s_qTh", [128, 2, S], BF16))
     vh = es.enter_context(nc.sbuf_tensor("s_vh", [128, 2, NT, 130], BF16))
     Pb = es.enter_context(nc.sbuf_tensor("s_Pb", [128, 3, 2, 512], BF16))
     accs = es.enter_context(nc.sbuf_tensor("s_accs", [128, 2, 3, 390], F32))
     rc = es.enter_context(nc.sbuf_tensor("s_rc", [128, 2, 8], F32))
     t0 = es.enter_context(nc.sbuf_tensor("s_t0", [128, 2, 128], F32))
     yv = es.enter_context(nc.sbuf_tensor("s_yv", [128, 4, 128], F32))
     yj = es.enter_context(nc.sbuf_tensor("s_yj", [128, 128], F32))
     ss2 = es.enter_context(nc.sbuf_tensor("s_ss2", [128, 4], F32))
     rs2 = es.enter_context(nc.sbuf_tensor("s_rs2", [128, 4], F32))
     ynb = es.enter_context(nc.sbuf_tensor("s_ynb", [128, 2, 4, 128], BF16))
     yTst = es.enter_context(nc.sbuf_tensor("s_yTst", [128, 2, 512], BF16))
     wadc = es.enter_context(nc.sbuf_tensor("s_wadc", [128, 2, 8, 128], F32))
     if True:
        k.op("pool", lambda e: e.memset(vh[:, :, :, 128:130], 1.0), writes=[("vones",)])

        def load_head(h):
            hp = h % 2
            k.dma("sp", "kTh%d" % hp, kTh[:, hp, :], kT_s[h], writes=[("kTh", hp)])
            k.dma("sp", "qTh%d" % hp, qTh[:, hp, :], qT_s[h], writes=[("qTh", hp)])
            k.dma("sp", "vh%d" % hp, vh[:, hp, :, 0:128], v_s[:, h * 128:(h + 1) * 128].rearrange("(t p) d -> p t d", p=128),
                  writes=[("vh", hp)])

        def acc_loc(c, i):
            a = c * 4 + i
            return a // 3, (a % 3) * 130

        tb7 = banks[7].bitcast(BF16)
        ps_mod2 = banks[7][:, 256:352].rearrange("p (j t) -> p j t", t=2)
        NMC = 32
        mc_state = {"next": 0}

        def mod_chunk_load(i):
            j = 16 + i
            k.dma("sp", "wadc%d" % (i % 2), wadc[:, i % 2], wada_d[:, j * 128:(j + 1) * 128].rearrange("(c p) n -> p c n", p=128),
                  writes=[("wadc", i % 2)])

        def mod_chunk_mm(i):
            j = 16 + i
            for kc in range(8):
                k.op("pe", lambda e, kc=kc: e.matmul(ps_mod2[:, j, :], lhsT=wadc[:, i % 2, kc, :], rhs=c_act2[:, kc, :],
                                                     start=(kc == 0), stop=(kc == 7)),
                     reads=[("wadc", i % 2), "ca0", "ca1"], writes=[("bank", 7)], sig=(kc == 7))

        def mod_chunk_step():
            i = mc_state["next"]
            if i >= NMC:
                return
            mod_chunk_mm(i)
            if i + 2 < NMC:
                mod_chunk_load(i + 2)
            mc_state["next"] = i + 1

        mod_chunk_load(0)
        mod_chunk_load(1)
        steps = [(h, qg, kt) for h in range(NH) for qg in range(NG) for kt in range(4 * qg + 4)]
        NSTEP = len(steps)

        def emit_qk(n):
            h, qg, kt = steps[n]
            hp = h % 2
            q0 = qg * 512
            j = kt - 4 * qg
            lo = max(j, 0) * 128
            sb_ = n % 2
            pb_ = n % 3
            sbanks = (2 * sb_, 2 * sb_ + 1)
            for c in (0, 1):
                k.op("pe", lambda e, c=c: e.matmul(
                    banks[sbanks[c]][:, lo:512], lhsT=kTh[c * 64:(c + 1) * 64, hp, kt * 128:(kt + 1) * 128],
                    rhs=qTh[c * 64:(c + 1) * 64, hp, q0 + lo:q0 + 512], start=True, stop=True),
                    reads=[("kTh", hp), ("qTh", hp)], writes=[("bank", sbanks[c])], sig=(c == 1))
            k.op("act", lambda e: e.activation(
                out=Pb[:, pb_, :, lo:512], in_=psall[:, 2 * sb_:2 * sb_ + 2, lo:512], func=AF.Exp, scale=0.125),
                reads=[("bank", sbanks[0]), ("bank", sbanks[1])], writes=[("Pb", pb_)])
            if j >= 0:
                k.op("pool", lambda e: e.tensor_tensor(
                    out=Pb[:, pb_, :, lo:lo + 128], in0=Pb[:, pb_, :, lo:lo + 128],
                    in1=trimask[:].unsqueeze(1).to_broadcast([128, 2, 128]), op=ALU.mult),
                    reads=[("Pb", pb_), "trimask"], writes=[("Pb", pb_)])

        started = set()

        def emit_pv(n):
            h, qg, kt = steps[n]
            hp = h % 2
            j = kt - 4 * qg
            pb_ = n % 3
            if kt == 0:
                started.clear()
            for c in (0, 1):
                for i in range(max(j, 0), 4):
                    bo, off = acc_loc(c, i)
                    bk = 4 + bo
                    st = bk not in started
                    started.add(bk)
                    last = (c == 1 and i == 3)
                    k.op("pe", lambda e, bk=bk, off=off, c=c, i=i, st=st: e.matmul(
                        banks[bk][:, off:off + 129], lhsT=Pb[:, pb_, c, i * 128:(i + 1) * 128], rhs=vh[:, hp, kt, 0:129],
                        start=st, stop=(kt == 4 * qg + i), skip_group_check=True),
                        reads=[("Pb", pb_), ("vh", hp), ("vones",)], writes=[("bank", bk)], sig=last)

        def epilogue(h, qg):
            q0 = qg * 512
            gpar = (h * NG + qg) % 2
            for bo in range(3):
                k.op("dve", lambda e, bo=bo: e.tensor_copy(out=accs[:, gpar, bo, :], in_=banks[4 + bo][:, 0:390]),
                     reads=[("bank", 4 + bo)], writes=[("accs", gpar, bo)])
            for i in range(4):
                b0_, o0 = acc_loc(0, i)
                b1_, o1 = acc_loc(1, i)
                ip = i % 2
                k.op("dve", lambda e, ip=ip, b0_=b0_, o0=o0: e.reciprocal(out=rc[:, ip, 0:1], in_=accs[:, gpar, b0_, o0 + 128:o0 + 129]),
                     reads=[("accs", gpar, b0_)], writes=[("rc0", ip)])
                k.op("dve", lambda e, ip=ip, b1_=b1_, o1=o1: e.reciprocal(out=rc[:, ip, 1:2], in_=accs[:, gpar, b1_, o1 + 128:o1 + 129]),
                     reads=[("accs", gpar, b1_)], writes=[("rc1", ip)])
                k.op("dve", lambda e, ip=ip: e.tensor_tensor(out=rc[:, ip, 2:3], in0=rc[:, ip, 1:2], in1=neg_lam[:], op=ALU.mult),
                     reads=[("rc1", ip), "neg_lam"], writes=[("rc2", ip)])
                k.op("dve", lambda e, ip=ip, b0_=b0_, o0=o0: e.tensor_scalar(out=t0[:, ip, :], in0=accs[:, gpar, b0_, o0:o0 + 128], scalar1=rc[:, ip, 0:1],
                                                                           scalar2=None, op0=ALU.mult),
                     reads=[("accs", gpar, b0_), ("rc0", ip)], writes=[("t0", ip)])
                k.op("dve", lambda e, i=i, ip=ip, b1_=b1_, o1=o1: e.scalar_tensor_tensor(out=yv[:, i, :], in0=accs[:, gpar, b1_, o1:o1 + 128], scalar=rc[:, ip, 2:3],
                                                                                     in1=t0[:, ip, :], op0=ALU.mult, op1=ALU.add),
                     reads=[("accs", gpar, b1_), ("rc2", ip), ("t0", ip)], writes=[("yv", i)])
                k.op("dve", lambda e, i=i: e.scalar_tensor_tensor(out=yj[:], in0=yv[:, i, :], scalar=1.0, in1=yv[:, i, :], op0=ALU.mult, op1=ALU.mult,
                                                                  accum_out=ss2[:, i:i + 1]),
                     reads=[("yv", i)], writes=[("ss2", i), "yj"])
            k.op("pool", lambda e: e.tensor_scalar(out=rs2[:], in0=ss2[:], scalar1=1.0 / 128, scalar2=EPS, op0=ALU.mult, op1=ALU.add),
                 reads=[("ss2", i) for i in range(4)], writes=["rs2"])
            k.op("pool", lambda e: e.tensor_tensor(out=rs2[:], in0=rs2[:], in1=mhalf[:, 0:4], op=ALU.pow), reads=["rs2", "mhalf"], writes=["rs2"])
            for i in range(4):
                k.op("dve", lambda e, i=i: e.scalar_tensor_tensor(out=ynb[:, gpar, i, :], in0=yv[:, i, :], scalar=rs2[:, i:i + 1], in1=subg_b[:],
                                                                  op0=ALU.mult, op1=ALU.mult),
                     reads=[("yv", i), "rs2", "subg_b"], writes=[("ynb", gpar, i)])

            def fin():
                for i in range(4):
                    k.op("pe", lambda e, i=i: e.transpose(tb7[:, i * 128:(i + 1) * 128], ynb[:, gpar, i, :], ident[:]),
                         reads=[("ynb", gpar, i), "ident"], writes=[("bank", 7)], sig=(i == 3))
                k.op("dve", lambda e: e.tensor_copy(out=yTst[:, gpar, :], in_=tb7[:, 0:512]), reads=[("bank", 7)], writes=[("yTst", gpar)])
                k.dma("sp", "yTst%d" % gpar, yaT_s[h][:, q0:q0 + 512], yTst[:, gpar, :], reads=[("yTst", gpar)], writes=[("yaT_s", h, qg)])
                mod_chunk_step()

            return fin

        load_head(0)
        emit_qk(0)
        if NSTEP > 1:
            emit_qk(1)
        pending = None
        age = 0
        for n in range(NSTEP):
            h, qg, kt = steps[n]
            if qg == 0 and kt == 0 and h + 1 < NH:
                load_head(h + 1)
            if n + 2 < NSTEP:
                emit_qk(n + 2)
            emit_pv(n)
            age += 1
            last = (kt == 4 * qg + 3)
            if pending is not None and (age >= 10 or last):
                pending()
                pending = None
            if last:
                pending = epilogue(h, qg)
                age = 0
        if pending is not None:
            pending()
            pending = None
        while mc_state["next"] < NMC:
            mod_chunk_step()
        k.op("dve", lambda e: e.tensor_tensor(out=modT[:, 16:48], in0=ps_mod2[:, 16:48, 0], in1=badaT[:, 16:48], op=ALU.add),
             reads=[("bank", 7), "badaT"], writes=["modT2"])
        k.op("dve", lambda e: e.tensor_scalar(out=sc2p[:], in0=modT[:, 32:40], scalar1=1.0, scalar2=None, op0=ALU.add), reads=["modT2"], writes=["sc2p"])
        for j in range(8):
            k.dma("sp", "gst", gate_s[0:1, j * 128:(j + 1) * 128].rearrange("o p -> p o"), modT[:, 16 + j:17 + j], reads=["modT2"], writes=[("gate_s", j)])
            k.dma("sp", "gst", gate_s[0:1, D + j * 128:D + (j + 1) * 128].rearrange("o p -> p o"), modT[:, 40 + j:41 + j], reads=["modT2"], writes=[("gate_s", 8 + j)])
        k.barrier(engines=("sp",))
        k.dma("sp", "g1b", g1_b[:], gate_s[0:1, 0:D].partition_broadcast(128), writes=["g1_b"])
        k.dma("sp", "g2b", g2_b[:], gate_s[0:1, D:2 * D].partition_broadcast(128), writes=["g2_b"])
        k.barrier()

    if upto <= 3:
        k.barrier()
        return nc, k
    with ExitStack() as es:
     wgm = es.enter_context(nc.sbuf_tensor("s_wgm", [128, 8, 2048], BF16))
     wpr = es.enter_context(nc.sbuf_tensor("s_wpr", [128, 8, D], BF16))
     wpa = es.enter_context(nc.sbuf_tensor("s_wpa", [128, 8, D], BF16))
     wo = es.enter_context(nc.sbuf_tensor("s_wo", [128, 8, D], BF16))
     xtA = es.enter_context(nc.sbuf_tensor("s_c_xtA", [128, 4, D], F32))
     xtB = es.enter_context(nc.sbuf_tensor("s_c_xtB", [128, 4, D], F32))
     xn = es.enter_context(nc.sbuf_tensor("s_c_xn", [128, 4, D], BF16))
     junk = es.enter_context(nc.sbuf_tensor("s_c_junk", [128, D], BF16))
     hTd = es.enter_context(nc.sbuf_tensor("s_c_hT", [128, 2, 8, 512], BF16))
     ssq = es.enter_context(nc.sbuf_tensor("s_c_ssq", [128, 4], F32))
     rstd = es.enter_context(nc.sbuf_tensor("s_c_rstd", [128, 4], F32))
     yr = es.enter_context(nc.sbuf_tensor("s_yr", [128, 2, 8, 512], BF16))
     ya = es.enter_context(nc.sbuf_tensor("s_ya", [128, 2, 8, 512], BF16))
     sg = es.enter_context(nc.sbuf_tensor("s_sg", [128, 2, 2, 512], BF16))
     mm = es.enter_context(nc.sbuf_tensor("s_mm", [128, 2, 2, 512], F32))
     mg = es.enter_context(nc.sbuf_tensor("s_mg", [128, 8, 512], BF16))
     tt = es.enter_context(nc.sbuf_tensor("s_c_tt", [128, 2, 512], F32))
     if True:
        load_w_cast("wgm", wgm, win_d, 0, 5120, 2048, 8)
        load_w_cast("wpr", wpr, wpr_d, 0, 0, D, 8)
        load_w_cast("wpa", wpa, wpa_d, 0, 0, D, 8)
        load_w_cast("wo", wo, wo_d, 0, 0, D, 8)
        xts = [xtA, xtB]

        def load_g(g):
            gp = g % 2
            k.dma("sp", "xt%d" % gp, xts[gp][:], x_d[g * 512:(g + 1) * 512, :].rearrange("(s p) d -> p s d", p=128), writes=[("xt", gp)])
            k.dma("sp", "yr%d" % gp, yr[:, gp, :, :], yrT_s.rearrange("j p t -> p j t")[:, :, g * 512:(g + 1) * 512], writes=[("yr", gp)])
            k.dma("sp", "ya%d" % gp, ya[:, gp, :, :], yaT_s.rearrange("j p t -> p j t")[:, :, g * 512:(g + 1) * 512], writes=[("ya", gp)])

        def np1_3a(g):
            norm_p1("n3a", xts[g % 2], ("xt", g % 2), 4, xn, junk, ssq, rstd)

        def np2_3a(g):
            norm_p2("n3a", sc1p, modT[:, 0:8], hTd[:, g % 2], ("hT", g % 2), 4, xn, [0, 1], "alt")

        load_g(0)
        np1_3a(0)
        np2_3a(0)
        for g in range(NG):
            if g + 1 < NG:
                load_g(g + 1)
            gp = g % 2
            xt = xts[gp]
            hT = hTd[:, gp]
            for j in range(8):
                if j == 4 and g + 1 < NG:
                    np1_3a(g + 1)
                jp = j % 2
                for br, (col0, bi) in enumerate(((0, 2), (1024, 3))):
                    for kc in range(8):
                        k.op("pe", lambda e, bi=bi, kc=kc, col0=col0, j=j: e.matmul(
                            banks[bi], lhsT=wgm[:, kc, col0 + j * 128:col0 + (j + 1) * 128], rhs=hT[:, kc, :], start=(kc == 0), stop=(kc == 7)),
                            reads=[(("hT", gp), kc)] + wkeys("wgm", col0 + j * 128, col0 + (j + 1) * 128), writes=[("bank", bi)], sig=(kc == 7))
                    k.op("act", lambda e, bi=bi, br=br, jp=jp: e.activation(out=sg[:, jp, br, :], in_=banks[bi], func=AF.Sigmoid),
                         reads=[("bank", bi)], writes=[("sg", jp, br)])
                for br, (wsb, wn, src, sk, bi) in enumerate(((wpr, "wpr", yr, "yr", 4), (wpa, "wpa", ya, "ya", 5))):
                    for kc in range(8):
                        k.op("pe", lambda e, bi=bi, kc=kc, j=j, wsb=wsb, src=src: e.matmul(
                            banks[bi], lhsT=wsb[:, kc, j * 128:(j + 1) * 128], rhs=src[:, gp, kc, :], start=(kc == 0), stop=(kc == 7)),
                            reads=[(sk, gp)] + wkeys(wn, j * 128, (j + 1) * 128), writes=[("bank", bi)], sig=(kc == 7))
                    k.op("dve", lambda e, bi=bi, br=br, jp=jp: e.tensor_tensor(out=mm[:, jp, br, :], in0=banks[bi], in1=sg[:, jp, br, :], op=ALU.mult),
                         reads=[("bank", bi), ("sg", jp, br)], writes=[("mm", jp, br)])
                k.op("pool", lambda e, j=j, jp=jp: e.tensor_tensor(out=mg[:, j, :], in0=mm[:, jp, 0, :], in1=mm[:, jp, 1, :], op=ALU.add),
                     reads=[("mm", jp, 0), ("mm", jp, 1)], writes=[("mg", j)])
            if g + 1 < NG:
                np2_3a(g + 1)
            for s in range(4):
                for half in range(2):
                    bi = 6 + half
                    hsl = slice(half * 512, (half + 1) * 512)
                    for kc in range(8):
                        k.op("pe", lambda e, bi=bi, kc=kc, s=s, hsl=hsl: e.matmul(
                            banks[bi], lhsT=mg[:, kc, s * 128:(s + 1) * 128], rhs=wo[:, kc, hsl], start=(kc == 0), stop=(kc == 7)),
                            reads=[("mg", kc)] + wkeys("wo", half * 512, (half + 1) * 512), writes=[("bank", bi)], sig=(kc == 7))
                    k.op("dve", lambda e, bi=bi, half=half, hsl=hsl: e.tensor_tensor(out=tt[:, half, :], in0=banks[bi], in1=g1_b[:, hsl], op=ALU.mult),
                         reads=[("bank", bi), "g1_b"], writes=[("tt", half)])
                    k.op("pool", lambda e, s=s, half=half, hsl=hsl, xt=xt: e.tensor_tensor(out=xt[:, s, hsl], in0=tt[:, half, :], in1=xt[:, s, hsl], op=ALU.add),
                         reads=[("tt", half), ("xt", gp)], writes=[("xt", gp)])
            k.dma("sp", "x1t%d" % gp, x1_s[g * 512:(g + 1) * 512, :].rearrange("(s p) d -> p s d", p=128), xt[:],
                  reads=[("xt", gp)], writes=[("x1_s", g)])
        k.barrier()

    if upto <= 4:
        k.barrier()
        return nc, k
    NG3 = S // 256
    with ExitStack() as es:
     wf1 = es.enter_context(nc.sbuf_tensor("s_wf1", [128, 8, DFF], BF16))
     wf2 = es.enter_context(nc.sbuf_tensor("s_wf2", [128, 32, D], BF16))
     x1A = es.enter_context(nc.sbuf_tensor("s_x1A", [128, 2, D], F32))
     x1B = es.enter_context(nc.sbuf_tensor("s_x1B", [128, 2, D], F32))
     xn = es.enter_context(nc.sbuf_tensor("s_d_xn", [128, 2, D], BF16))
     junk = es.enter_context(nc.sbuf_tensor("s_d_junk", [128, D], BF16))
     h2Td = es.enter_context(nc.sbuf_tensor("s_h2T", [128, 2, 8, 256], BF16))
     ssq = es.enter_context(nc.sbuf_tensor("s_d_ssq", [128, 4], F32))
     rstd = es.enter_context(nc.sbuf_tensor("s_d_rstd", [128, 4], F32))
     sq3 = es.enter_context(nc.sbuf_tensor("s_sq3", [128, 2, 256], F32))
     uT = es.enter_context(nc.sbuf_tensor("s_uT", [128, 32, 256], BF16))
     tt = es.enter_context(nc.sbuf_tensor("s_d_tt", [128, 2, 512], F32))
     if True:
        load_w_cast("wf1", wf1, wf1_d, 0, 0, DFF, 8)
        for fg in range(4):
            k.dma_group("pool", "w_wf2_%d" % fg, [(wf2[:, fk, :], wf2_d[fk * 128:(fk + 1) * 128, :]) for fk in range(fg * 8, fg * 8 + 8)], [("wf2", fg)])
        x1s = [x1A, x1B]

        def load_x1(g):
            k.dma("sp", "x1l%d" % (g % 2), x1s[g % 2][:], x1_s[g * 256:(g + 1) * 256, :].rearrange("(s p) d -> p s d", p=128), writes=[("x1", g % 2)])

        def np1_3b(g):
            norm_p1("n3b", x1s[g % 2], ("x1", g % 2), 2, xn, junk, ssq, rstd)

        def np2_3b(g):
            norm_p2("n3b", sc2p, modT[:, 24:32], h2Td[:, g % 2], ("h2T", g % 2), 2, xn, [0, 1, 7], "alt")

        load_x1(0)
        np1_3b(0)
        np2_3b(0)
        if NG3 > 1:
            load_x1(1)
        for g in range(NG3):
            gp = g % 2
            x1 = x1s[gp]
            h2T = h2Td[:, gp]
            for f in range(32):
                bi = 2 + (f % 3)
                fp = f % 2
                for kc in range(8):
                    k.op("pe", lambda e, bi=bi, kc=kc, f=f: e.matmul(
                        banks[bi][:, 0:256], lhsT=wf1[:, kc, f * 128:(f + 1) * 128], rhs=h2T[:, kc, :], start=(kc == 0), stop=(kc == 7)),
                        reads=[(("h2T", gp), kc)] + wkeys("wf1", f * 128, (f + 1) * 128), writes=[("bank", bi)], sig=(kc == 7))
                k.op("act", lambda e, bi=bi, fp=fp: e.activation(out=sq3[:, fp, :], in_=banks[bi][:, 0:256], func=AF.Square),
                     reads=[("bank", bi)], writes=[("sq3", fp)])
                k.op("dve", lambda e, bi=bi, fp=fp, f=f: e.scalar_tensor_tensor(out=uT[:, f, :], in0=banks[bi][:, 0:256], scalar=0.0, in1=sq3[:, fp, :],
                                                                              op0=ALU.is_gt, op1=ALU.mult),
                     reads=[("bank", bi), ("sq3", fp)], writes=[("uT", f)])
                if f == 15 and g + 1 < NG3:
                    np1_3b(g + 1)
            if g + 1 < NG3:
                np2_3b(g + 1)
            for s in range(2):
                for half in range(2):
                    bi = 5 + ((s * 2 + half) % 2)
                    hsl = slice(half * 512, (half + 1) * 512)
                    for f in range(32):
                        k.op("pe", lambda e, bi=bi, f=f, s=s, hsl=hsl: e.matmul(
                            banks[bi], lhsT=uT[:, f, s * 128:(s + 1) * 128], rhs=wf2[:, f, hsl], start=(f == 0), stop=(f == 31)),
                            reads=[("uT", f), ("wf2", f // 8)], writes=[("bank", bi)], sig=(f == 31))
                    k.op("dve", lambda e, bi=bi, half=half, hsl=hsl: e.tensor_tensor(out=tt[:, half, :], in0=banks[bi], in1=g2_b[:, hsl], op=ALU.mult),
                         reads=[("bank", bi), "g2_b"], writes=[("tt", half)])
                    k.op("pool", lambda e, s=s, half=half, hsl=hsl, x1=x1: e.tensor_tensor(out=x1[:, s, hsl], in0=tt[:, half, :], in1=x1[:, s, hsl], op=ALU.add),
                         reads=[("tt", half), ("x1", gp)], writes=[("x1", gp)])
            k.dma("sp", "outst%d" % gp, out_d[g * 256:(g + 1) * 256, :].rearrange("(s p) d -> p s d", p=128), x1[:],
                  reads=[("x1", gp)], writes=[("out", g)])
            if g + 2 < NG3:
                load_x1(g + 2)
        k.barrier()
    return nc, k


_CACHE = {}


def _prep_inputs(S, x, c, positions, w_ada, b_ada, w_in, conv_w, conv_b, rglru_wa, rglru_ba, rglru_wx, rglru_bx,
                 rglru_lambda, q_norm_gain, k_norm_gain, lambda_q1, lambda_k1, lambda_q2, lambda_k2, subln_gain,
                 w_proj_rnn, w_proj_attn, w_out, w_ff1, w_ff2):
    f = lambda a: np.ascontiguousarray(np.asarray(a, dtype=np.float32))
    B = x.shape[0]
    NT = S // 128
    col8 = lambda v: f(np.asarray(v).reshape(8, 128).T)
    invf = (10000.0 ** (-np.arange(0, 64, 2, dtype=np.float32) / 64.0)).astype(np.float32)
    shared = {
        "invf": f(np.broadcast_to(invf[None, :], (128, 32))),
        "w_ada": f(w_ada[0]),
        "b_adaT": f(np.asarray(b_ada[0]).reshape(48, 128).T),
        "w_in": f(w_in[0]),
        "conv_wT": f(np.asarray(conv_w[0]).reshape(4, 8, 128).transpose(2, 1, 0).reshape(128, 32)),
        "conv_bT": col8(conv_b[0]),
        "rglru_wa": f(rglru_wa[0]),
        "rglru_wx": f(rglru_wx[0]),
        "baT": col8(rglru_ba[0]),
        "bxT": col8(rglru_bx[0]),
        "lamT": col8(rglru_lambda[0]),
        "gq": f(np.asarray(q_norm_gain[0]).reshape(1, 64)),
        "gk": f(np.asarray(k_norm_gain[0]).reshape(1, 64)),
        "lq1": f(np.asarray(lambda_q1[0]).reshape(1, 64)),
        "lk1": f(np.asarray(lambda_k1[0]).reshape(1, 64)),
        "lq2": f(np.asarray(lambda_q2[0]).reshape(1, 64)),
        "lk2": f(np.asarray(lambda_k2[0]).reshape(1, 64)),
        "subg": f(np.asarray(subln_gain[0]).reshape(1, 128)),
        "w_proj_rnn": f(w_proj_rnn[0]),
        "w_proj_attn": f(w_proj_attn[0]),
        "w_out": f(w_out[0]),
        "w_ff1": f(w_ff1[0]),
        "w_ff2": f(w_ff2[0]),
    }
    in_maps = []
    for b in range(B):
        m = dict(shared)
        m["x"] = f(x[b])
        m["cT"] = f(np.asarray(c[b]).reshape(8, 128).T)
        m["posT"] = np.ascontiguousarray(np.asarray(positions[b], dtype=np.int32).reshape(NT, 128).T)
        in_maps.append(m)
    return in_maps


def kernel(**inputs):
    x = np.asarray(inputs["x"])
    B, S, _ = x.shape
    if S not in _CACHE:
        _CACHE[S] = build(S)[0]
    nc = _CACHE[S]
    in_maps = _prep_inputs(S, **inputs)
    res = run_bass_kernel_spmd(nc, in_maps, core_ids=list(range(B)))
    return np.stack([np.asarray(r["out"], dtype=np.float32).reshape(S, D) for r in res.results], axis=0)
```

```python
import math
from contextlib import ExitStack
import numpy as np
import concourse.bass as bass
import concourse.mybir as mybir
from concourse.bass_utils import run_bass_kernel_spmd

F32 = mybir.dt.float32
BF16 = mybir.dt.bfloat16
I32 = mybir.dt.int32
AF = mybir.ActivationFunctionType
ALU = mybir.AluOpType
AX = mybir.AxisListType

D = 1024
NCH = 8
NH = 8
DFF = 4096
EPS = 1e-6
LAM_INIT = 0.8 - 0.6 * math.exp(-0.3 * 0)
TWO_PI = 2.0 * math.pi
C1_2PI = 6.28125
C2_2PI = TWO_PI - C1_2PI
PI_SAFE = 3.1415925


class K:
    def __init__(self, nc):
        self.nc = nc
        self.engs = {"pe": nc.tensor, "act": nc.scalar, "dve": nc.vector, "pool": nc.gpsimd, "sp": nc.sync}
        self.sem = {}
        self.cnt = {}
        self.waited = {e: {} for e in self.engs}
        self._ctx = []
        for e in self.engs:
            cm = nc.semaphore("s_" + e)
            self.sem[e] = cm.__enter__()
            self._ctx.append(cm)
            self.cnt[e] = 0
        self.dsem = {}
        self.dcnt = {}
        self.lastw = {}
        self.readers = {}
        self.nins = 0

    def _wait(self, e, deps):
        for d in deps:
            if d is None:
                continue
            kind, key, val = d
            if kind == "e" and key == e and e == "pe":
                continue
            wk = (kind, key)
            if self.waited[e].get(wk, 0) >= val:
                continue
            self.waited[e][wk] = val
            self.engs[e].wait_ge(self.sem[key] if kind == "e" else self.dsem[key], val)

    def _deps(self, reads, writes):
        deps = []
        for r in reads:
            if r in self.lastw:
                deps.append(self.lastw[r])
        for w in writes:
            if w in self.lastw:
                deps.append(self.lastw[w])
            deps.extend(self.readers.get(w, ()))
        return deps

    def _commit(self, tok, reads, writes):
        for r in reads:
            self.readers.setdefault(r, []).append(tok)
        for w in writes:
            self.lastw[w] = tok
            self.readers[w] = []

    def op(self, e, ins, reads=(), writes=(), sig=True):
        self._wait(e, self._deps(reads, writes))
        i = ins(self.engs[e])
        self.nins += 1
        if sig:
            self.cnt[e] += 1
            i.then_inc(self.sem[e], 1)
            tok = ("e", e, self.cnt[e])
        else:
            tok = ("e", e, self.cnt[e] + 1)
        self._commit(tok, reads, writes)
        return tok

    def dma(self, e, slot, out, in_, reads=(), writes=(), **kw):
        if slot not in self.dsem:
            cm = self.nc.semaphore("d_" + slot)
            self.dsem[slot] = cm.__enter__()
            self._ctx.append(cm)
            self.dcnt[slot] = 0
        self._wait(e, self._deps(reads, writes))
        self.engs[e].dma_start(out=out, in_=in_, **kw).then_inc(self.dsem[slot], 16)
        self.nins += 1
        self.dcnt[slot] += 16
        tok = ("d", slot, self.dcnt[slot])
        self._commit(tok, reads, writes)
        return tok

    def dma_group(self, e, slot, pairs, keys):
        deps = self._deps((), keys)
        tok = None
        for (o, i) in pairs:
            tok = self.dma(e, slot, o, i, reads=(), writes=())
            if deps:
                pass
        for kk in keys:
            self.lastw[kk] = tok
            self.readers[kk] = []
        return tok

    def barrier(self, engines=("pe", "act", "dve", "pool", "sp")):
        toks = [("e", e, self.cnt[e]) for e in self.engs if self.cnt[e] > 0]
        toks += [("d", s, c) for s, c in self.dcnt.items()]
        for e in engines:
            self._wait(e, toks)


def build(S, upto=5):
    NT = S // 128
    NG = S // 512
    nc = bass.Bass("TRN2", target_bir_lowering=False)
    k = K(nc)

    def din(name, shape, dt=F32):
        return nc.dram_tensor(name, shape, dt, kind="ExternalInput").ap()

    x_d = din("x", [S, D])
    cT_d = din("cT", [128, 8])
    pos_d = din("posT", [128, NT], I32)
    invf_d = din("invf", [128, 32])
    wada_d = din("w_ada", [D, 6 * D])
    bada_d = din("b_adaT", [128, 48])
    win_d = din("w_in", [D, 7168])
    cw_d = din("conv_wT", [128, 32])
    cb_d = din("conv_bT", [128, 8])
    wa_d = din("rglru_wa", [8, 128, 128])
    wx_d = din("rglru_wx", [8, 128, 128])
    ba_d = din("baT", [128, 8])
    bx_d = din("bxT", [128, 8])
    lam_d = din("lamT", [128, 8])
    gq_d = din("gq", [1, 64])
    gk_d = din("gk", [1, 64])
    lq1_d = din("lq1", [1, 64])
    lk1_d = din("lk1", [1, 64])
    lq2_d = din("lq2", [1, 64])
    lk2_d = din("lk2", [1, 64])
    subg_d = din("subg", [1, 128])
    wpr_d = din("w_proj_rnn", [D, D])
    wpa_d = din("w_proj_attn", [D, D])
    wo_d = din("w_out", [D, D])
    wf1_d = din("w_ff1", [D, DFF])
    wf2_d = din("w_ff2", [DFF, D])
    out_d = nc.dram_tensor("out", [S, D], F32, kind="ExternalOutput").ap()

    def dscr(name, shape, dt):
        return nc.dram_tensor(name, shape, dt, kind="Internal").ap()

    qT_s = dscr("qT_s", [NH, 128, S], BF16)
    kT_s = dscr("kT_s", [NH, 128, S], BF16)
    v_s = dscr("v_s", [S, D], BF16)
    yrT_s = dscr("yrT_s", [NCH, 128, S], BF16)
    yaT_s = dscr("yaT_s", [NH, 128, S], BF16)
    x1_s = dscr("x1_s", [S, D], F32)
    gate_s = dscr("gate_s", [1, 2 * D], F32)

    sb = lambda n, s, d=F32: nc.alloc_sbuf_tensor("s_" + n, s, d)

    ident = sb("ident", [128, 128], BF16)
    trimask = sb("trimask", [128, 128], BF16)
    cT = sb("cT", [128, 8])
    badaT = sb("badaT", [128, 48])
    modT = sb("modT", [128, 48])
    sc1p = sb("sc1p", [128, 8])
    sc2p = sb("sc2p", [128, 8])
    cw = sb("cw", [128, 32])
    cb = sb("cb", [128, 8])
    hba = sb("hba", [128, 8])
    hbx = sb("hbx", [128, 8])
    lam = sb("lam", [128, 8])
    c1 = sb("c1", [128, 8])
    c2 = sb("c2", [128, 8])
    gq_b = sb("gq_b", [128, 64])
    gk_b = sb("gk_b", [128, 64])
    ngq_b = sb("ngq_b", [128, 64])
    ngk_b = sb("ngk_b", [128, 64])
    lq1_b = sb("lq1_b", [128, 64]); lk1_b = sb("lk1_b", [128, 64])
    lq2_b = sb("lq2_b", [128, 64]); lk2_b = sb("lk2_b", [128, 64])
    neg_lam = sb("neg_lam", [128, 1])
    subg_b = sb("subg_b", [128, 128])
    g1_b = sb("g1_b", [128, D])
    g2_b = sb("g2_b", [128, D])
    mhalf = sb("mhalf", [128, 64])
    hstate = sb("hstate", [128, 8])

    for nm, t, d in [("cT", cT, cT_d), ("badaT", badaT, bada_d), ("cw", cw, cw_d), ("cb", cb, cb_d),
                     ("hba", hba, ba_d), ("hbx", hbx, bx_d), ("lam", lam, lam_d)]:
        k.dma("sp", "c_" + nm, t[:], d, writes=[nm])
    for nm, t, d in [("gq_b", gq_b, gq_d), ("gk_b", gk_b, gk_d), ("lq1_b", lq1_b, lq1_d), ("lk1_b", lk1_b, lk1_d),
                     ("lq2_b", lq2_b, lq2_d), ("lk2_b", lk2_b, lk2_d), ("subg_b", subg_b, subg_d)]:
        k.dma("sp", "c_" + nm, t[:], d.partition_broadcast(128), writes=[nm])

    idf = sb("idf", [128, 128])
    k.op("pool", lambda e: e.memset(idf[:], 1.0), writes=["idf"])
    k.op("pool", lambda e: e.affine_select(out=idf[:], in_=idf[:], pattern=[[-1, 128]], compare_op=ALU.is_equal,
                                           fill=0.0, base=0, channel_multiplier=1), reads=["idf"], writes=["idf"])
    k.op("pool", lambda e: e.tensor_copy(out=ident[:], in_=idf[:]), reads=["idf"], writes=["ident"])
    k.op("pool", lambda e: e.memset(idf[:], 1.0), writes=["idf"])
    k.op("pool", lambda e: e.affine_select(out=idf[:], in_=idf[:], pattern=[[1, 128]], compare_op=ALU.is_ge,
                                           fill=0.0, base=0, channel_multiplier=-1), reads=["idf"], writes=["idf"])
    k.op("pool", lambda e: e.tensor_copy(out=trimask[:], in_=idf[:]), reads=["idf"], writes=["trimask"])
    k.op("pool", lambda e: e.memset(mhalf[:], -0.5), writes=["mhalf"])
    k.op("pool", lambda e: e.memset(hstate[:], 0.0), writes=["hstate"])

    k.op("dve", lambda e: e.tensor_scalar(out=hba[:], in0=hba[:], scalar1=0.5, scalar2=None, op0=ALU.mult), reads=["hba"], writes=["hba"])
    k.op("dve", lambda e: e.tensor_scalar(out=hbx[:], in0=hbx[:], scalar1=0.5, scalar2=None, op0=ALU.mult), reads=["hbx"], writes=["hbx"])
    k.op("dve", lambda e: e.tensor_scalar(out=ngq_b[:], in0=gq_b[:], scalar1=-1.0, scalar2=None, op0=ALU.mult), reads=["gq_b"], writes=["ngq_b"])
    k.op("dve", lambda e: e.tensor_scalar(out=ngk_b[:], in0=gk_b[:], scalar1=-1.0, scalar2=None, op0=ALU.mult), reads=["gk_b"], writes=["ngk_b"])
    k.op("dve", lambda e: e.tensor_scalar(out=subg_b[:], in0=subg_b[:], scalar1=1.0 - LAM_INIT, scalar2=None, op0=ALU.mult), reads=["subg_b"], writes=["subg_b"])
    t_ab = sb("t_ab", [128, 8]); t_mx = sb("t_mx", [128, 8]); t_e = sb("t_e", [128, 8])
    k.op("dve", lambda e: e.tensor_scalar(out=t_mx[:], in0=lam[:], scalar1=-1.0, scalar2=None, op0=ALU.mult), reads=["lam"], writes=["t_mx"])
    k.op("dve", lambda e: e.tensor_tensor(out=t_ab[:], in0=lam[:], in1=t_mx[:], op=ALU.max), reads=["lam", "t_mx"], writes=["t_ab"])
    k.op("dve", lambda e: e.tensor_scalar(out=t_mx[:], in0=t_mx[:], scalar1=0.0, scalar2=None, op0=ALU.max), reads=["t_mx", "t_ab"], writes=["t_mx"])
    k.op("act", lambda e: e.activation(out=t_e[:], in_=t_ab[:], func=AF.Exp, scale=-1.0), reads=["t_ab"], writes=["t_e"])
    k.op("act", lambda e: e.activation(out=t_e[:], in_=t_e[:], func=AF.Ln, bias=1.0), reads=["t_e"], writes=["t_e"])
    k.op("dve", lambda e: e.tensor_tensor(out=t_mx[:], in0=t_mx[:], in1=t_e[:], op=ALU.add), reads=["t_mx", "t_e"], writes=["t_mx"])
    k.op("dve", lambda e: e.tensor_scalar(out=c1[:], in0=t_mx[:], scalar1=-4.0, scalar2=None, op0=ALU.mult), reads=["t_mx"], writes=["c1"])
    k.op("dve", lambda e: e.tensor_scalar(out=c2[:], in0=t_mx[:], scalar1=-8.0, scalar2=None, op0=ALU.mult), reads=["t_mx"], writes=["c2"])
    t_p = sb("t_p", [128, 64]); t_s = sb("t_s", [128, 2])
    k.op("dve", lambda e: e.tensor_tensor(out=t_p[:], in0=lq1_b[:], in1=lk1_b[:], op=ALU.mult), reads=["lq1_b", "lk1_b"], writes=["t_p"])
    k.op("dve", lambda e: e.tensor_reduce(out=t_s[:, 0:1], in_=t_p[:], axis=AX.X, op=ALU.add), reads=["t_p"], writes=["t_s0"])
    k.op("dve", lambda e: e.tensor_tensor(out=t_p[:], in0=lq2_b[:], in1=lk2_b[:], op=ALU.mult), reads=["lq2_b", "lk2_b", "t_s0"], writes=["t_p"])
    k.op("dve", lambda e: e.tensor_reduce(out=t_s[:, 1:2], in_=t_p[:], axis=AX.X, op=ALU.add), reads=["t_p"], writes=["t_s1"])
    k.op("act", lambda e: e.activation(out=t_s[:], in_=t_s[:], func=AF.Exp), reads=["t_s0", "t_s1"], writes=["t_s"])
    k.op("dve", lambda e: e.scalar_tensor_tensor(out=neg_lam[:], in0=t_s[:, 1:2], scalar=-LAM_INIT, in1=t_s[:, 0:1],
                                                 op0=ALU.add, op1=ALU.subtract), reads=["t_s"], writes=["neg_lam"])

    c_act2 = sb("c_act2", [128, 8, 2])
    k.op("act", lambda e: e.activation(out=c_act2[:, :, 0], in_=cT[:], func=AF.Silu), reads=["cT"], writes=["ca0"])
    k.op("act", lambda e: e.activation(out=c_act2[:, :, 1], in_=cT[:], func=AF.Silu), reads=["cT"], writes=["ca1"])
    psall = nc.alloc_psum_tensor("psall", [128, 8, 512], F32)
    banks = [psall[:, i, :] for i in range(8)]
    ps_mod = banks[7][:, 0:96].rearrange("p (j t) -> p j t", t=2)
    wada_v = wada_d.rearrange("(c p) n -> p c n", p=128)
    wq_cm = nc.sbuf_tensor("s_wqkv", [128, 8, 3072], BF16)
    wqkv = wq_cm.__enter__()
    k.dma_group("pool", "w_wqkv", [(wqkv[:, kc, c0:c0 + 1024], win_d[kc * 128:(kc + 1) * 128, 2048 + c0:2048 + c0 + 1024])
                                   for kc in range(8) for c0 in range(0, 3072, 1024)], ["wqkv"])
    with ExitStack() as es:
     wada0 = es.enter_context(nc.sbuf_tensor("s_wada0", [128, 8, 1024], F32))
     wada1 = es.enter_context(nc.sbuf_tensor("s_wada1", [128, 8, 1024], F32))
     if True:
        wadas = [wada0, wada1]
        for jg in range(2):
            wsb = wadas[jg % 2]
            k._wait("sp", k._deps((), [("wada", jg % 2)]))
            k.dma_group("sp", "wada%d" % (jg % 2), [(wsb[:, kc, :], wada_d[kc * 128:(kc + 1) * 128, jg * 1024:(jg + 1) * 1024]) for kc in range(8)],
                        [("wada", jg % 2)])
            for jj in range(8):
                j = jg * 8 + jj
                for kc in range(8):
                    k.op("pe", lambda e, wsb=wsb, kc=kc, jj=jj, j=j: e.matmul(
                        ps_mod[:, j, :], lhsT=wsb[:, kc, jj * 128:(jj + 1) * 128], rhs=c_act2[:, kc, :],
                        start=(kc == 0), stop=(kc == 7)),
                        reads=[("wada", jg % 2), "ca0", "ca1"], writes=["ps_mod"], sig=(kc == 7))
        k.op("dve", lambda e: e.tensor_tensor(out=modT[:, 0:16], in0=ps_mod[:, 0:16, 0], in1=badaT[:, 0:16], op=ALU.add),
             reads=["ps_mod", "badaT"], writes=["modT"])
        k.barrier()
    k.op("dve", lambda e: e.tensor_scalar(out=sc1p[:], in0=modT[:, 8:16], scalar1=1.0, scalar2=None, op0=ALU.add), reads=["modT"], writes=["sc1p"])


    def bank_bf(i):
        return banks[i].bitcast(BF16)

    def norm_p1(tag, src, srckey, nsub, xn, junk, ssq, rstd, rstd_mode="pool"):
        for s in range(nsub):
            k.op("act", lambda e, s=s: e.activation(out=junk[:], in_=src[:, s, :], func=AF.Square, accum_out=ssq[:, s:s + 1]),
                 reads=[srckey], writes=[(tag, "ssq", s), (tag, "junk")])
        if rstd_mode == "act":
            k.op("act", lambda e: e.activation(out=rstd[:, 0:nsub], in_=ssq[:, 0:nsub], func=AF.Ln, scale=1.0 / D, bias=EPS),
                 reads=[(tag, "ssq", s) for s in range(nsub)], writes=[(tag, "rstd")])
            k.op("act", lambda e: e.activation(out=rstd[:, 0:nsub], in_=rstd[:, 0:nsub], func=AF.Exp, scale=-0.5),
                 reads=[(tag, "rstd")], writes=[(tag, "rstd")])
        else:
            k.op("pool", lambda e: e.tensor_scalar(out=rstd[:, 0:nsub], in0=ssq[:, 0:nsub], scalar1=1.0 / D, scalar2=EPS, op0=ALU.mult, op1=ALU.add),
                 reads=[(tag, "ssq", s) for s in range(nsub)], writes=[(tag, "rstd")])
            k.op("pool", lambda e: e.tensor_tensor(out=rstd[:, 0:nsub], in0=rstd[:, 0:nsub], in1=mhalf[:, 0:nsub], op=ALU.pow),
                 reads=[(tag, "rstd"), "mhalf"], writes=[(tag, "rstd")])
        for s in range(nsub):
            eng = "dve" if s % 2 == 0 else "pool"
            k.op(eng, lambda e, s=s: e.tensor_scalar(out=xn[:, s, :], in0=src[:, s, :], scalar1=rstd[:, s:s + 1], scalar2=0.0, op0=ALU.mult, op1=ALU.add),
                 reads=[srckey, (tag, "rstd")], writes=[(tag, "xn", s)])

    def norm_p2(tag, scp, shift, hT, hTkey, nsub, xn, tbanks, evac_eng):
        for c in range(8):
            bi = tbanks[c % len(tbanks)]
            pT = bank_bf(bi)
            for s in range(nsub):
                k.op("pe", lambda e, s=s, c=c, pT=pT: e.transpose(pT[:, s * 128:(s + 1) * 128], xn[:, s, c * 128:(c + 1) * 128], ident[:]),
                     reads=[(tag, "xn", s), "ident"], writes=[("bank", bi)], sig=(s == nsub - 1))
            ee = evac_eng if evac_eng != "alt" else ("act" if c % 2 == 0 else "dve")
            if ee == "act":
                k.op("act", lambda e, c=c, pT=pT: e.activation(out=hT[:, c, :], in_=pT[:, 0:nsub * 128], func=AF.Identity,
                                                               scale=scp[:, c:c + 1], bias=shift[:, c:c + 1]),
                     reads=[("bank", bi), "modT", "sc1p", "sc2p"], writes=[(hTkey, c)])
            else:
                k.op("dve", lambda e, c=c, pT=pT: e.tensor_scalar(out=hT[:, c, :], in0=pT[:, 0:nsub * 128], scalar1=scp[:, c:c + 1],
                                                                  scalar2=shift[:, c:c + 1], op0=ALU.mult, op1=ALU.add),
                     reads=[("bank", bi), "modT", "sc1p", "sc2p"], writes=[(hTkey, c)])

    def norm_T(tag, src, srckey, scp, shift, hT, hTkey, nsub, xn, junk, ssq, rstd, tbanks, evac_eng):
        norm_p1(tag, src, srckey, nsub, xn, junk, ssq, rstd)
        norm_p2(tag, scp, shift, hT, hTkey, nsub, xn, tbanks, evac_eng)

    def load_w_cast(name, wsb, wd, row0, col0, ncols, nk, piece=1024):
        for c0 in range(0, ncols, piece):
            c1_ = min(ncols, c0 + piece)
            pairs = [(wsb[:, kc, c0:c1_], wd[row0 + kc * 128:row0 + (kc + 1) * 128, col0 + c0:col0 + c1_]) for kc in range(nk)]
            k.dma_group("pool", "w_%s_%d" % (name, c0), pairs, [(name, c0)])

    def wkeys(name, lo, hi, piece=1024):
        return [(name, c0) for c0 in range((lo // piece) * piece, hi, piece)]

    with ExitStack() as es:
     cos_t = es.enter_context(nc.sbuf_tensor("s_cos_t", [128, NT, 32], F32))
     sin_t = es.enter_context(nc.sbuf_tensor("s_sin_t", [128, NT, 32], F32))
     if True:
      with ExitStack() as es:
       pos_i = es.enter_context(nc.sbuf_tensor("s_pos_i", [128, NT], I32))
       pos_f = es.enter_context(nc.sbuf_tensor("s_pos_f", [128, NT], F32))
       invf = es.enter_context(nc.sbuf_tensor("s_invf", [128, 32], F32))
       ang = es.enter_context(nc.sbuf_tensor("s_ang", [128, NT, 32], F32))
       rk = es.enter_context(nc.sbuf_tensor("s_rk", [128, NT, 32], F32))
       ki = es.enter_context(nc.sbuf_tensor("s_ki", [128, NT, 32], I32))
       dd = es.enter_context(nc.sbuf_tensor("s_dd", [128, NT, 32], F32))
       dc = es.enter_context(nc.sbuf_tensor("s_dc", [128, NT, 32], F32))
       if True:
            k.dma("sp", "pos", pos_i[:], pos_d, writes=["pos_i"])
            k.dma("sp", "invf", invf[:], invf_d, writes=["invf"])
            k.op("dve", lambda e: e.tensor_copy(out=pos_f[:], in_=pos_i[:]), reads=["pos_i"], writes=["pos_f"])
            k.op("dve", lambda e: e.tensor_tensor(out=ang[:], in0=pos_f[:].unsqueeze(2).to_broadcast([128, NT, 32]),
                                                  in1=invf[:].unsqueeze(1).to_broadcast([128, NT, 32]), op=ALU.mult),
                 reads=["pos_f", "invf"], writes=["ang"])
            k.op("dve", lambda e: e.tensor_scalar(out=rk[:], in0=ang[:], scalar1=1.0 / TWO_PI, scalar2=None, op0=ALU.mult), reads=["ang"], writes=["rk"])
            k.op("dve", lambda e: e.tensor_copy(out=ki[:], in_=rk[:]), reads=["rk"], writes=["ki"])
            k.op("dve", lambda e: e.tensor_copy(out=rk[:], in_=ki[:]), reads=["ki"], writes=["rk"])
            k.op("dve", lambda e: e.scalar_tensor_tensor(out=dd[:], in0=rk[:], scalar=-C1_2PI, in1=ang[:], op0=ALU.mult, op1=ALU.add),
                 reads=["rk", "ang"], writes=["dd"])
            k.op("dve", lambda e: e.scalar_tensor_tensor(out=dd[:], in0=rk[:], scalar=-C2_2PI, in1=dd[:], op0=ALU.mult, op1=ALU.add),
                 reads=["rk", "dd"], writes=["dd"])
            k.op("dve", lambda e: e.tensor_scalar(out=dd[:], in0=dd[:], scalar1=PI_SAFE, scalar2=-PI_SAFE, op0=ALU.min, op1=ALU.max),
                 reads=["dd"], writes=["dd"])
            k.op("act", lambda e: e.activation(out=sin_t[:], in_=dd[:], func=AF.Sin), reads=["dd"], writes=["sin_t"])
            k.op("dve", lambda e: e.tensor_scalar(out=dc[:], in0=dd[:], scalar1=math.pi / 2, scalar2=None, op0=ALU.add), reads=["dd"], writes=["dc"])
            k.op("dve", lambda e: e.tensor_scalar(out=rk[:], in0=dc[:], scalar1=math.pi, scalar2=None, op0=ALU.is_gt), reads=["dc"], writes=["rk"])
            k.op("dve", lambda e: e.scalar_tensor_tensor(out=dc[:], in0=rk[:], scalar=-TWO_PI, in1=dc[:], op0=ALU.mult, op1=ALU.add),
                 reads=["rk", "dc"], writes=["dc"])
            k.op("dve", lambda e: e.tensor_scalar(out=dc[:], in0=dc[:], scalar1=PI_SAFE, scalar2=-PI_SAFE, op0=ALU.min, op1=ALU.max),
                 reads=["dc"], writes=["dc"])
            k.op("act", lambda e: e.activation(out=cos_t[:], in_=dc[:], func=AF.Sin), reads=["dc"], writes=["cos_t"])
            k.barrier()
      with ExitStack() as es:
       xt1 = es.enter_context(nc.sbuf_tensor("s_b_xt", [128, 4, D], F32))
       xn = es.enter_context(nc.sbuf_tensor("s_b_xn", [128, 4, D], BF16))
       junk = es.enter_context(nc.sbuf_tensor("s_b_junk", [128, D], BF16))
       hTd = es.enter_context(nc.sbuf_tensor("s_b_hT", [128, 2, 8, 512], BF16))
       ssq = es.enter_context(nc.sbuf_tensor("s_b_ssq", [128, 4], F32))
       rstd = es.enter_context(nc.sbuf_tensor("s_b_rstd", [128, 4], F32))
       Tq = es.enter_context(nc.sbuf_tensor("s_Tq", [128, 4, 2, 64], F32))
       Tk = es.enter_context(nc.sbuf_tensor("s_Tk", [128, 4, 2, 64], F32))
       sqj = es.enter_context(nc.sbuf_tensor("s_sqj", [128, 2, D], BF16))
       gss = es.enter_context(nc.sbuf_tensor("s_gss", [128, 2, 16], F32))
       grs = es.enter_context(nc.sbuf_tensor("s_grs", [128, 2, 16], F32))
       m1 = es.enter_context(nc.sbuf_tensor("s_m1", [128, 2, 2, D], F32))
       m2 = es.enter_context(nc.sbuf_tensor("s_m2", [128, 2, 2, D], F32))
       ob = es.enter_context(nc.sbuf_tensor("s_ob", [128, 2, 2, D], BF16))
       qTst = es.enter_context(nc.sbuf_tensor("s_qTst", [128, 2, 8, 512], BF16))
       kTst = es.enter_context(nc.sbuf_tensor("s_kTst", [128, 2, 8, 512], BF16))
       vst = es.enter_context(nc.sbuf_tensor("s_vst", [128, 2, D], BF16))
       if True:
        def load_x(g):
            k.dma("sp", "b_xt", xt1[:], x_d[g * 512:(g + 1) * 512, :].rearrange("(s p) d -> p s d", p=128), writes=[("xt", 0)])

        def np1(g):
            norm_p1("n1b", xt1, ("xt", 0), 4, xn, junk, ssq, rstd, rstd_mode="act")

        def np2(g):
            norm_p2("n1b", sc1p, modT[:, 0:8], hTd[:, g % 2], ("hT", g % 2), 4, xn, [0, 1], "alt")

        def tables(g):
            for (T, gb, ngb, nm) in ((Tq, gq_b, ngq_b, "Tq"), (Tk, gk_b, ngk_b, "Tk")):
                cs = cos_t[:, 4 * g:4 * g + 4, :]
                sn = sin_t[:, 4 * g:4 * g + 4, :]
                bc = lambda a, lo: a[:, lo:lo + 32].unsqueeze(1).to_broadcast([128, 4, 32])
                k.op("pool", lambda e: e.tensor_tensor(out=T[:, :, 0, 0:32], in0=cs, in1=bc(gb, 0), op=ALU.mult), reads=["cos_t", "gq_b", "gk_b"], writes=[(nm, 0)])
                k.op("pool", lambda e: e.tensor_tensor(out=T[:, :, 0, 32:64], in0=cs, in1=bc(gb, 32), op=ALU.mult), reads=["cos_t", "gq_b", "gk_b"], writes=[(nm, 1)])
                k.op("pool", lambda e: e.tensor_tensor(out=T[:, :, 1, 0:32], in0=sn, in1=bc(ngb, 32), op=ALU.mult), reads=["sin_t", "ngq_b", "ngk_b"], writes=[(nm, 2)])
                k.op("pool", lambda e: e.tensor_tensor(out=T[:, :, 1, 32:64], in0=sn, in1=bc(gb, 0), op=ALU.mult), reads=["sin_t", "gq_b", "gk_b"], writes=[(nm, 3)])

        def mm_sub(g, s):
            hT = hTd[:, g % 2]
            pairs = {}
            for wi, (nm, col0) in enumerate((("q", 0), ("k", 1024), ("v", 2048))):
                idx = (s * 3 + wi) % 3
                b0 = 2 + 2 * idx
                pairs[nm] = b0
                for half in (0, 1):
                    bi = b0 + half
                    for kc in range(8):
                        k.op("pe", lambda e, bi=bi, kc=kc, col0=col0, half=half: e.matmul(
                            banks[bi], lhsT=hT[:, kc, s * 128:(s + 1) * 128],
                            rhs=wqkv[:, kc, col0 + half * 512:col0 + (half + 1) * 512], start=(kc == 0), stop=(kc == 7)),
                            reads=[(("hT", g % 2), kc), "wqkv"], writes=[("bank", bi)], sig=(kc == 7))
            return pairs

        def chains(g, s, pairs):
            gp = g % 2
            sp_ = s % 2
            info = []
            for qi, nm in enumerate(("q", "k")):
                b0 = pairs[nm]
                bkeys = [("bank", b0), ("bank", b0 + 1)]
                px = psall[:, b0:b0 + 2, :]
                k.op("act", lambda e: e.activation(out=sqj[:, qi, :].rearrange("p (b n) -> p b n", b=2), in_=px, func=AF.Square),
                     reads=(), writes=[("sqj", qi)] + bkeys)
                info.append((qi, nm, b0, bkeys, px))
            for (qi, nm, b0, bkeys, px) in info:
                T = Tq if nm == "q" else Tk
                Tn = "Tq" if nm == "q" else "Tk"
                px3 = px.rearrange("p b (g d) -> p (b g) d", d=64)
                px4 = px.rearrange("p b (g t d) -> p (b g) t d", t=2, d=32)
                m1v = m1[:, qi, sp_, :].rearrange("p (g d) -> p g d", d=64)
                m2v = m2[:, qi, sp_, :].rearrange("p (g t d) -> p g t d", t=2, d=32)
                tb = lambda t, lo: T[:, s, t, lo:lo + 32].unsqueeze(1).to_broadcast([128, 16, 32])
                tkeys = [(Tn, i) for i in range(4)]
                k.op("dve", lambda e: e.tensor_tensor(out=m1v, in0=px3, in1=T[:, s, 0, :].unsqueeze(1).to_broadcast([128, 16, 64]), op=ALU.mult),
                     reads=bkeys + tkeys, writes=[("m1", qi, sp_)])
                k.op("dve", lambda e: e.tensor_tensor(out=m2v[:, :, 0, :], in0=px4[:, :, 1, :], in1=tb(1, 0), op=ALU.mult),
                     reads=bkeys + tkeys, writes=[("m2a", qi, sp_)])
                k.op("dve", lambda e: e.tensor_tensor(out=m2v[:, :, 1, :], in0=px4[:, :, 0, :], in1=tb(1, 32), op=ALU.mult),
                     reads=bkeys + tkeys, writes=[("m2b", qi, sp_)])
                k.op("dve", lambda e: e.tensor_reduce(out=gss[:, qi, :], in_=sqj[:, qi, :].rearrange("p (g d) -> p g d", d=64), axis=AX.X, op=ALU.add),
                     reads=[("sqj", qi)], writes=[("gss", qi)])
                k.op("pool", lambda e: e.tensor_tensor(out=m1[:, qi, sp_, :], in0=m1[:, qi, sp_, :], in1=m2[:, qi, sp_, :], op=ALU.add),
                     reads=[("m1", qi, sp_), ("m2a", qi, sp_), ("m2b", qi, sp_)], writes=[("m1", qi, sp_)])
            b0 = pairs["v"]
            k.op("act", lambda e: e.activation(out=vst[:, s % 2, :].rearrange("p (b n) -> p b n", b=2), in_=psall[:, b0:b0 + 2, :], func=AF.Copy),
                 reads=[("bank", b0), ("bank", b0 + 1)], writes=[("vst", s % 2)])
            k.dma("sp", "vst%d" % (s % 2), v_s[g * 512 + s * 128:g * 512 + (s + 1) * 128, :], vst[:, s % 2, :],
                  reads=[("vst", s % 2)], writes=[("v_s", g, s)])
            for (qi, nm, b0, bkeys, px) in info:
                m1v = m1[:, qi, sp_, :].rearrange("p (g d) -> p g d", d=64)
                k.op("act", lambda e: e.activation(out=grs[:, qi, :], in_=gss[:, qi, :], func=AF.Ln, scale=1.0 / 64, bias=EPS),
                     reads=[("gss", qi)], writes=[("grs", qi)])
                k.op("act", lambda e: e.activation(out=grs[:, qi, :], in_=grs[:, qi, :], func=AF.Exp, scale=-0.5),
                     reads=[("grs", qi)], writes=[("grs", qi)])
                k.op("pool", lambda e: e.tensor_tensor(out=ob[:, qi, s % 2, :].rearrange("p (g d) -> p g d", d=64), in0=m1v,
                                                       in1=grs[:, qi, :].unsqueeze(2).to_broadcast([128, 16, 64]), op=ALU.mult),
                     reads=[("m1", qi, sp_), ("grs", qi)], writes=[("ob", qi, s % 2)])

        def make_deferred(g, s):
            gp = g % 2

            def run():
                for qi, nm in enumerate(("q", "k")):
                    pT = bank_bf(qi)
                    for h in range(8):
                        k.op("pe", lambda e, h=h: e.transpose(pT[:, h * 128:(h + 1) * 128], ob[:, qi, s % 2, h * 128:(h + 1) * 128], ident[:]),
                             reads=[("ob", qi, s % 2), "ident"], writes=[("bank", qi)], sig=(h == 7))
                    st = qTst if nm == "q" else kTst
                    k.op("act", lambda e: e.activation(out=st[:, gp, :, s * 128:(s + 1) * 128],
                                                       in_=pT.rearrange("p (h t) -> p h t", t=128), func=AF.Copy),
                         reads=[("bank", qi)], writes=[(nm + "st", gp, s)])
                if s == 3:
                    k.dma("sp", "qst%d" % gp, qT_s.rearrange("h p t -> p h t")[:, :, g * 512:(g + 1) * 512], qTst[:, gp, :, :],
                          reads=[("qst", gp, s_) for s_ in range(4)], writes=[("qT_s", g)])
                    k.dma("sp", "kst%d" % gp, kT_s.rearrange("h p t -> p h t")[:, :, g * 512:(g + 1) * 512], kTst[:, gp, :, :],
                          reads=[("kst", gp, s_) for s_ in range(4)], writes=[("kT_s", g)])
            return run

        load_x(0)
        np1(0)
        if NG > 1:
            load_x(1)
        np2(0)
        dq = []
        for g in range(NG):
            tables(g)
            for s in range(4):
                if len(dq) >= 2:
                    dq.pop(0)()
                pairs = mm_sub(g, s)
                chains(g, s, pairs)
                if s == 0 and g + 1 < NG:
                    np1(g + 1)
                    if g + 2 < NG:
                        load_x(g + 2)
                if s == 1 and g + 1 < NG:
                    np2(g + 1)
                dq.append(make_deferred(g, s))
        while dq:
            dq.pop(0)()
        k.barrier()

    wq_cm.__exit__(None, None, None)
    if upto <= 1:
        k.barrier()
        return nc, k
    JB = 4
    with ExitStack() as es:
     wrg = es.enter_context(nc.sbuf_tensor("s_wrg", [128, 8, 2048], BF16))
     wab = es.enter_context(nc.sbuf_tensor("s_wab", [128, 8, 128], BF16))
     wxb = es.enter_context(nc.sbuf_tensor("s_wxb", [128, 8, 128], BF16))
     xt = es.enter_context(nc.sbuf_tensor("s_a_xt", [128, 4, D], F32))
     xn = es.enter_context(nc.sbuf_tensor("s_a_xn", [128, 4, D], BF16))
     junk = es.enter_context(nc.sbuf_tensor("s_a_junk", [128, D], BF16))
     hTd = es.enter_context(nc.sbuf_tensor("s_a_hT", [128, 2, 8, 512], BF16))
     ssq = es.enter_context(nc.sbuf_tensor("s_a_ssq", [128, 4], F32))
     rstd = es.enter_context(nc.sbuf_tensor("s_a_rstd", [128, 4], F32))
     xrb = es.enter_context(nc.sbuf_tensor("s_xrb", [128, 8, 516], BF16))
     Wd = es.enter_context(nc.sbuf_tensor("s_Wd", [128, 8, 4, 128], BF16))
     xcb = es.enter_context(nc.sbuf_tensor("s_xcb", [128, 2, JB, 512], BF16))
     gg = es.enter_context(nc.sbuf_tensor("s_gg", [128, 2, JB, 512], BF16))
     trr = es.enter_context(nc.sbuf_tensor("s_trr", [128, 2, JB, 512], BF16))
     tii = es.enter_context(nc.sbuf_tensor("s_tii", [128, 2, JB, 512], BF16))
     aa = es.enter_context(nc.sbuf_tensor("s_aa", [128, 2, JB, 512], F32))
     sqv = es.enter_context(nc.sbuf_tensor("s_sqv", [128, 2, JB, 512], F32))
     uu = es.enter_context(nc.sbuf_tensor("s_uu", [128, JB, 512], F32))
     hs = es.enter_context(nc.sbuf_tensor("s_hs", [128, 2, 512], F32))
     yst = es.enter_context(nc.sbuf_tensor("s_yst", [128, 8, 512], BF16))
     if True:
        load_w_cast("wrg", wrg, win_d, 0, 0, 2048, 8)
        k.dma("pool", "w_wab", wab[:], wa_d.rearrange("n k j -> k n j"), writes=["wab"])
        k.dma("pool", "w_wxb", wxb[:], wx_d.rearrange("n k j -> k n j"), writes=["wxb"])
        k.op("pool", lambda e: e.memset(xrb[:, :, 0:3], 0.0), writes=[("xrh", j) for j in range(8)])
        for j in range(8):
            for tap in range(4):
                k.op("pool" if (j * 4 + tap) % 2 else "dve", lambda e, j=j, tap=tap: e.tensor_scalar(
                    out=Wd[:, j, tap, :], in0=ident[:], scalar1=cw[:, j * 4 + tap:j * 4 + tap + 1], scalar2=0.0, op0=ALU.mult, op1=ALU.add),
                    reads=["ident", "cw"], writes=[("Wd", j, tap)])

        def load_x(g):
            k.dma("sp", "a_xt", xt[:], x_d[g * 512:(g + 1) * 512, :].rearrange("(s p) d -> p s d", p=128), writes=[("xt", 0)])

        def np1(g):
            norm_p1("n1a", xt, ("xt", 0), 4, xn, junk, ssq, rstd)

        def np2(g):
            norm_p2("n1a", sc1p, modT[:, 0:8], hTd[:, g % 2], ("hT", g % 2), 4, xn, [0, 1], "dve")

        def stageA1(b):
            g, jb, par = b // 2, (b % 2) * JB, b % 2
            hT = hTd[:, g % 2]
            for jj in range(JB):
                j = jb + jj
                bx_, bg_ = 2 + (jj % 2), 4 + (jj % 2)
                for (bi, col0) in ((bx_, 0), (bg_, 1024)):
                    for kc in range(8):
                        k.op("pe", lambda e, bi=bi, kc=kc, col0=col0: e.matmul(
                            banks[bi], lhsT=wrg[:, kc, col0 + j * 128:col0 + (j + 1) * 128], rhs=hT[:, kc, :],
                            start=(kc == 0), stop=(kc == 7)),
                            reads=[(("hT", g % 2), kc)] + wkeys("wrg", col0 + j * 128, col0 + (j + 1) * 128), writes=[("bank", bi)], sig=(kc == 7))
                k.op("act", lambda e: e.activation(out=xrb[:, j, 3:515], in_=banks[bx_], func=AF.Copy),
                     reads=[("bank", bx_)], writes=[("xr", j)])
                k.op("act", lambda e: e.activation(out=gg[:, par, jj, :], in_=banks[bg_], func=AF.Gelu_apprx_tanh),
                     reads=[("bank", bg_)], writes=[("gg", par, jj)])

        def stageA2(b):
            g, jb, par = b // 2, (b % 2) * JB, b % 2
            for jj in range(JB):
                j = jb + jj
                cvb = 6 + (jj % 2)
                for tap in range(4):
                    k.op("pe", lambda e, tap=tap: e.matmul(banks[cvb], lhsT=Wd[:, j, tap, :], rhs=xrb[:, j, tap:tap + 512],
                                                          start=(tap == 0), stop=(tap == 3)),
                         reads=[("xr", j), ("xrh", j), ("Wd", j, tap)], writes=[("bank", cvb)], sig=(tap == 3))
                k.op("dve", lambda e: e.tensor_scalar(out=xcb[:, par, jj, :], in0=banks[cvb], scalar1=cb[:, j:j + 1], scalar2=None, op0=ALU.add),
                     reads=[("bank", cvb), "cb"], writes=[("xcb", par, jj)])
                k.op("pool", lambda e: e.tensor_copy(out=xrb[:, j, 0:3], in_=xrb[:, j, 512:515]), reads=[("xr", j)], writes=[("xrh", j)])
            for jj in range(JB):
                j = jb + jj
                bx_ = 2 + (jj % 2)
                br_ = 4 + (jj % 2)
                k.op("pe", lambda e: e.matmul(banks[br_], lhsT=wab[:, j, :], rhs=xcb[:, par, jj, :], start=True, stop=True),
                     reads=[("xcb", par, jj), "wab"], writes=[("bank", br_)])
                k.op("act", lambda e: e.activation(out=trr[:, par, jj, :], in_=banks[br_], func=AF.Tanh, scale=0.5, bias=hba[:, j:j + 1]),
                     reads=[("bank", br_), "hba"], writes=[("trr", par, jj)])
                k.op("pe", lambda e: e.matmul(banks[bx_], lhsT=wxb[:, j, :], rhs=xcb[:, par, jj, :], start=True, stop=True),
                     reads=[("xcb", par, jj), "wxb"], writes=[("bank", bx_)])
                k.op("act", lambda e: e.activation(out=tii[:, par, jj, :], in_=banks[bx_], func=AF.Tanh, scale=0.5, bias=hbx[:, j:j + 1]),
                     reads=[("bank", bx_), "hbx"], writes=[("tii", par, jj)])

        def stageBC(b):
            g, jb, par = b // 2, (b % 2) * JB, b % 2
            for jj in range(JB):
                j = jb + jj
                k.op("act", lambda e: e.activation(out=aa[:, par, jj, :], in_=trr[:, par, jj, :], func=AF.Exp, scale=c1[:, j:j + 1], bias=c1[:, j:j + 1]),
                     reads=[("trr", par, jj), "c1"], writes=[("aa", par, jj)])
                k.op("act", lambda e: e.activation(out=sqv[:, par, jj, :], in_=trr[:, par, jj, :], func=AF.Exp, scale=c2[:, j:j + 1], bias=c2[:, j:j + 1]),
                     reads=[("trr", par, jj), "c2"], writes=[("sqv", par, jj)])
            for jj in range(JB):
                k.op("act", lambda e: e.activation(out=sqv[:, par, jj, :], in_=sqv[:, par, jj, :], func=AF.Sqrt, scale=-0.25, bias=0.25),
                     reads=[("sqv", par, jj)], writes=[("sqv", par, jj)])

        def stageD(b):
            g, jb, par = b // 2, (b % 2) * JB, b % 2
            for jj in range(JB):
                j = jb + jj
                ui = jj % 2
                k.op("dve", lambda e: e.scalar_tensor_tensor(out=uu[:, jj, :], in0=tii[:, par, jj, :], scalar=1.0, in1=xcb[:, par, jj, :],
                                                             op0=ALU.add, op1=ALU.mult),
                     reads=[("tii", par, jj), ("xcb", par, jj)], writes=[("uu", jj)])
                k.op("pool", lambda e: e.tensor_tensor(out=uu[:, jj, :], in0=uu[:, jj, :], in1=sqv[:, par, jj, :], op=ALU.mult),
                     reads=[("uu", jj), ("sqv", par, jj)], writes=[("uu", jj)])
            for jj in range(JB):
                j = jb + jj
                ui = jj % 2
                k.op("dve", lambda e: e.tensor_tensor_scan(out=hs[:, ui, :], data0=aa[:, par, jj, :], data1=uu[:, jj, :],
                                                           initial=hstate[:, j:j + 1], op0=ALU.mult, op1=ALU.add),
                     reads=[("aa", par, jj), ("uu", jj), ("hstate", j), "hstate"], writes=[("hs", ui)])
                k.op("dve", lambda e: e.tensor_copy(out=hstate[:, j:j + 1], in_=hs[:, ui, 511:512]),
                     reads=[("hs", ui)], writes=[("hstate", j)])
                k.op("pool", lambda e: e.tensor_tensor(out=yst[:, j, :], in0=hs[:, ui, :], in1=gg[:, par, jj, :], op=ALU.mult),
                     reads=[("hs", ui), ("gg", par, jj)], writes=[("yst", j)])
            if b % 2 == 1:
                k.dma("sp", "yst", yrT_s.rearrange("j p t -> p j t")[:, :, g * 512:(g + 1) * 512], yst[:],
                      reads=[("yst", j) for j in range(8)], writes=[("yrT_s", g)])

        NB2 = 2 * NG
        load_x(0)
        np1(0)
        if NG > 1:
            load_x(1)
        np2(0)
        stageA1(0)
        stageA2(0)
        for b in range(NB2):
            g = b // 2
            nxt = (b % 2 == 0 and g + 1 < NG)
            if nxt:
                np1(g + 1)
                if g + 2 < NG:
                    load_x(g + 2)
            if b + 1 < NB2:
                stageA1(b + 1)
            stageBC(b)
            if b + 1 < NB2:
                stageA2(b + 1)
            if nxt:
                np2(g + 1)
            stageD(b)
        k.barrier()

    if upto <= 2:
        k.barrier()
        return nc, k
    with ExitStack() as es:
     kTh = es.enter_context(nc.sbuf_tensor("s_kTh", [128, 2, S], BF16))
     qTh = es.enter_context(nc.sbuf_tensor("s_qTh", [128, 2, S], BF16))
     vh = es.enter_context(nc.sbuf_tensor("s_vh", [128, 2, NT, 130], BF16))
     Pb = es.enter_context(nc.sbuf_tensor("s_Pb", [128, 3, 2, 512], BF16))
     accs = es.enter_context(nc.sbuf_tensor("s_accs", [128, 2, 3, 390], F32))
     rc = es.enter_context(nc.sbuf_tensor("s_rc", [128, 2, 8], F32))
     t0 = es.enter_context(nc.sbuf_tensor("s_t0", [128, 2, 128], F32))
     yv = es.enter_context(nc.sbuf_tensor("s_yv", [128, 4, 128], F32))
     yj = es.enter_context(nc.sbuf_tensor("s_yj", [128, 128], F32))
     ss2 = es.enter_context(nc.sbuf_tensor("s_ss2", [128, 4], F32))
     rs2 = es.enter_context(nc.sbuf_tensor("s_rs2", [128, 4], F32))
     ynb = es.enter_context(nc.sbuf_tensor("s_ynb", [128, 2, 4, 128], BF16))
     yTst = es.enter_context(nc.sbuf_tensor("s_yTst", [128, 2, 512], BF16))
     wadc = es.enter_context(nc.sbuf_tensor("s_wadc", [128, 2, 8, 128], F32))
     if True:
        k.op("pool", lambda e: e.memset(vh[:, :, :, 128:130], 1.0), writes=[("vones",)])

        def load_head(h):
            hp = h % 2
            k.dma("sp", "kTh%d" % hp, kTh[:, hp, :], kT_s[h], writes=[("kTh", hp)])
            k.dma("sp", "qTh%d" % hp, qTh[:, hp, :], qT_s[h], writes=[("qTh", hp)])
            k.dma("sp", "vh%d" % hp, vh[:, hp, :, 0:128], v_s[:, h * 128:(h + 1) * 128].rearrange("(t p) d -> p t d", p=128),
                  writes=[("vh", hp)])

        def acc_loc(c, i):
            a = c * 4 + i
            return a // 3, (a % 3) * 130

        tb7 = banks[7].bitcast(BF16)
        ps_mod2 = banks[7][:, 256:352].rearrange("p (j t) -> p j t", t=2)
        NMC = 32
        mc_state = {"next": 0}

        def mod_chunk_load(i):
            j = 16 + i
            k.dma("sp", "wadc%d" % (i % 2), wadc[:, i % 2], wada_d[:, j * 128:(j + 1) * 128].rearrange("(c p) n -> p c n", p=128),
                  writes=[("wadc", i % 2)])

        def mod_chunk_mm(i):
            j = 16 + i
            for kc in range(8):
                k.op("pe", lambda e, kc=kc: e.matmul(ps_mod2[:, j, :], lhsT=wadc[:, i % 2, kc, :], rhs=c_act2[:, kc, :],
                                                     start=(kc == 0), stop=(kc == 7)),
                     reads=[("wadc", i % 2), "ca0", "ca1"], writes=[("bank", 7)], sig=(kc == 7))

        def mod_chunk_step():
            i = mc_state["next"]
            if i >= NMC:
                return
            mod_chunk_mm(i)
            if i + 2 < NMC:
                mod_chunk_load(i + 2)
            mc_state["next"] = i + 1

        mod_chunk_load(0)
        mod_chunk_load(1)
        steps = [(h, qg, kt) for h in range(NH) for qg in range(NG) for kt in range(4 * qg + 4)]
        NSTEP = len(steps)

        def emit_qk(n):
            h, qg, kt = steps[n]
            hp = h % 2
            q0 = qg * 512
            j = kt - 4 * qg
            lo = max(j, 0) * 128
            sb_ = n % 2
            pb_ = n % 3
            sbanks = (2 * sb_, 2 * sb_ + 1)
            for c in (0, 1):
                k.op("pe", lambda e, c=c: e.matmul(
                    banks[sbanks[c]][:, lo:512], lhsT=kTh[c * 64:(c + 1) * 64, hp, kt * 128:(kt + 1) * 128],
                    rhs=qTh[c * 64:(c + 1) * 64, hp, q0 + lo:q0 + 512], start=True, stop=True),
                    reads=[("kTh", hp), ("qTh", hp)], writes=[("bank", sbanks[c])], sig=(c == 1))
            k.op("act", lambda e: e.activation(
                out=Pb[:, pb_, :, lo:512], in_=psall[:, 2 * sb_:2 * sb_ + 2, lo:512], func=AF.Exp, scale=0.125),
                reads=[("bank", sbanks[0]), ("bank", sbanks[1])], writes=[("Pb", pb_)])
            if j >= 0:
                k.op("pool", lambda e: e.tensor_tensor(
                    out=Pb[:, pb_, :, lo:lo + 128], in0=Pb[:, pb_, :, lo:lo + 128],
                    in1=trimask[:].unsqueeze(1).to_broadcast([128, 2, 128]), op=ALU.mult),
                    reads=[("Pb", pb_), "trimask"], writes=[("Pb", pb_)])

        started = set()

        def emit_pv(n):
            h, qg, kt = steps[n]
            hp = h % 2
            j = kt - 4 * qg
            pb_ = n % 3
            if kt == 0:
                started.clear()
            for c in (0, 1):
                for i in range(max(j, 0), 4):
                    bo, off = acc_loc(c, i)
                    bk = 4 + bo
                    st = bk not in started
                    started.add(bk)
                    last = (c == 1 and i == 3)
                    k.op("pe", lambda e, bk=bk, off=off, c=c, i=i, st=st: e.matmul(
                        banks[bk][:, off:off + 129], lhsT=Pb[:, pb_, c, i * 128:(i + 1) * 128], rhs=vh[:, hp, kt, 0:129],
                        start=st, stop=(kt == 4 * qg + i), skip_group_check=True),
                        reads=[("Pb", pb_), ("vh", hp), ("vones",)], writes=[("bank", bk)], sig=last)

        def epilogue(h, qg):
            q0 = qg * 512
            gpar = (h * NG + qg) % 2
            for bo in range(3):
                k.op("dve", lambda e, bo=bo: e.tensor_copy(out=accs[:, gpar, bo, :], in_=banks[4 + bo][:, 0:390]),
                     reads=[("bank", 4 + bo)], writes=[("accs", gpar, bo)])
            for i in range(4):
                b0_, o0 = acc_loc(0, i)
                b1_, o1 = acc_loc(1, i)
                ip = i % 2
                k.op("dve", lambda e, ip=ip, b0_=b0_, o0=o0: e.reciprocal(out=rc[:, ip, 0:1], in_=accs[:, gpar, b0_, o0 + 128:o0 + 129]),
                     reads=[("accs", gpar, b0_)], writes=[("rc0", ip)])
                k.op("dve", lambda e, ip=ip, b1_=b1_, o1=o1: e.reciprocal(out=rc[:, ip, 1:2], in_=accs[:, gpar, b1_, o1 + 128:o1 + 129]),
                     reads=[("accs", gpar, b1_)], writes=[("rc1", ip)])
                k.op("dve", lambda e, ip=ip: e.tensor_tensor(out=rc[:, ip, 2:3], in0=rc[:, ip, 1:2], in1=neg_lam[:], op=ALU.mult),
                     reads=[("rc1", ip), "neg_lam"], writes=[("rc2", ip)])
                k.op("dve", lambda e, ip=ip, b0_=b0_, o0=o0: e.tensor_scalar(out=t0[:, ip, :], in0=accs[:, gpar, b0_, o0:o0 + 128], scalar1=rc[:, ip, 0:1],
                                                                           scalar2=None, op0=ALU.mult),
                     reads=[("accs", gpar, b0_), ("rc0", ip)], writes=[("t0", ip)])
                k.op("dve", lambda e, i=i, ip=ip, b1_=b1_, o1=o1: e.scalar_tensor_tensor(out=yv[:, i, :], in0=accs[:, gpar, b1_, o1:o1 + 128], scalar=rc[:, ip, 2:3],
                                                                                     in1=t0[:, ip, :], op0=ALU.mult, op1=ALU.add),
                     reads=[("accs", gpar, b1_), ("rc2", ip), ("t0", ip)], writes=[("yv", i)])
                k.op("dve", lambda e, i=i: e.scalar_tensor_tensor(out=yj[:], in0=yv[:, i, :], scalar=1.0, in1=yv[:, i, :], op0=ALU.mult, op1=ALU.mult,
                                                                  accum_out=ss2[:, i:i + 1]),
                     reads=[("yv", i)], writes=[("ss2", i), "yj"])
            k.op("pool", lambda e: e.tensor_scalar(out=rs2[:], in0=ss2[:], scalar1=1.0 / 128, scalar2=EPS, op0=ALU.mult, op1=ALU.add),
                 reads=[("ss2", i) for i in range(4)], writes=["rs2"])
            k.op("pool", lambda e: e.tensor_tensor(out=rs2[:], in0=rs2[:], in1=mhalf[:, 0:4], op=ALU.pow), reads=["rs2", "mhalf"], writes=["rs2"])
            for i in range(4):
                k.op("dve", lambda e, i=i: e.scalar_tensor_tensor(out=ynb[:, gpar, i, :], in0=yv[:, i, :], scalar=rs2[:, i:i + 1], in1=subg_b[:],
                                                                  op0=ALU.mult, op1=ALU.mult),
                     reads=[("yv", i), "rs2", "subg_b"], writes=[("ynb", gpar, i)])

            def fin():
                for i in range(4):
                    k.op("pe", lambda e, i=i: e.transpose(tb7[:, i * 128:(i + 1) * 128], ynb[:, gpar, i, :], ident[:]),
                         reads=[("ynb", gpar, i), "ident"], writes=[("bank", 7)], sig=(i == 3))
                k.op("dve", lambda e: e.tensor_copy(out=yTst[:, gpar, :], in_=tb7[:, 0:512]), reads=[("bank", 7)], writes=[("yTst", gpar)])
                k.dma("sp", "yTst%d" % gpar, yaT_s[h][:, q0:q0 + 512], yTst[:, gpar, :], reads=[("yTst", gpar)], writes=[("yaT_s", h, qg)])
                mod_chunk_step()

            return fin

        load_head(0)
        emit_qk(0)
        if NSTEP > 1:
            emit_qk(1)
        pending = None
        age = 0
        for n in range(NSTEP):
            h, qg, kt = steps[n]
            if qg == 0 and kt == 0 and h + 1 < NH:
                load_head(h + 1)
            if n + 2 < NSTEP:
                emit_qk(n + 2)
            emit_pv(n)
            age += 1
            last = (kt == 4 * qg + 3)
            if pending is not None and (age >= 10 or last):
                pending()
                pending = None
            if last:
                pending = epilogue(h, qg)
                age = 0
        if pending is not None:
            pending()
            pending = None
        while mc_state["next"] < NMC:
            mod_chunk_step()
        k.op("dve", lambda e: e.tensor_tensor(out=modT[:, 16:48], in0=ps_mod2[:, 16:48, 0], in1=badaT[:, 16:48], op=ALU.add),
             reads=[("bank", 7), "badaT"], writes=["modT2"])
        k.op("dve", lambda e: e.tensor_scalar(out=sc2p[:], in0=modT[:, 32:40], scalar1=1.0, scalar2=None, op0=ALU.add), reads=["modT2"], writes=["sc2p"])
        for j in range(8):
            k.dma("sp", "gst", gate_s[0:1, j * 128:(j + 1) * 128].rearrange("o p -> p o"), modT[:, 16 + j:17 + j], reads=["modT2"], writes=[("gate_s", j)])
            k.dma("sp", "gst", gate_s[0:1, D + j * 128:D + (j + 1) * 128].rearrange("o p -> p o"), modT[:, 40 + j:41 + j], reads=["modT2"], writes=[("gate_s", 8 + j)])
        k.barrier(engines=("sp",))
        k.dma("sp", "g1b", g1_b[:], gate_s[0:1, 0:D].partition_broadcast(128), writes=["g1_b"])
        k.dma("sp", "g2b", g2_b[:], gate_s[0:1, D:2 * D].partition_broadcast(128), writes=["g2_b"])
        k.barrier()

    if upto <= 3:
        k.barrier()
        return nc, k
    with ExitStack() as es:
     wgm = es.enter_context(nc.sbuf_tensor("s_wgm", [128, 8, 2048], BF16))
     wpr = es.enter_context(nc.sbuf_tensor("s_wpr", [128, 8, D], BF16))
     wpa = es.enter_context(nc.sbuf_tensor("s_wpa", [128, 8, D], BF16))
     wo = es.enter_context(nc.sbuf_tensor("s_wo", [128, 8, D], BF16))
     xtA = es.enter_context(nc.sbuf_tensor("s_c_xtA", [128, 4, D], F32))
     xtB = es.enter_context(nc.sbuf_tensor("s_c_xtB", [128, 4, D], F32))
     xn = es.enter_context(nc.sbuf_tensor("s_c_xn", [128, 4, D], BF16))
     junk = es.enter_context(nc.sbuf_tensor("s_c_junk", [128, D], BF16))
     hTd = es.enter_context(nc.sbuf_tensor("s_c_hT", [128, 2, 8, 512], BF16))
     ssq = es.enter_context(nc.sbuf_tensor("s_c_ssq", [128, 4], F32))
     rstd = es.enter_context(nc.sbuf_tensor("s_c_rstd", [128, 4], F32))
     yr = es.enter_context(nc.sbuf_tensor("s_yr", [128, 2, 8, 512], BF16))
     ya = es.enter_context(nc.sbuf_tensor("s_ya", [128, 2, 8, 512], BF16))
     sg = es.enter_context(nc.sbuf_tensor("s_sg", [128, 2, 2, 512], BF16))
     mm = es.enter_context(nc.sbuf_tensor("s_mm", [128, 2, 2, 512], F32))
     mg = es.enter_context(nc.sbuf_tensor("s_mg", [128, 8, 512], BF16))
     tt = es.enter_context(nc.sbuf_tensor("s_c_tt", [128, 2, 512], F32))
     if True:
        load_w_cast("wgm", wgm, win_d, 0, 5120, 2048, 8)
        load_w_cast("wpr", wpr, wpr_d, 0, 0, D, 8)
        load_w_cast("wpa", wpa, wpa_d, 0, 0, D, 8)
        load_w_cast("wo", wo, wo_d, 0, 0, D, 8)
        xts = [xtA, xtB]

        def load_g(g):
            gp = g % 2
            k.dma("sp", "xt%d" % gp, xts[gp][:], x_d[g * 512:(g + 1) * 512, :].rearrange("(s p) d -> p s d", p=128), writes=[("xt", gp)])
            k.dma("sp", "yr%d" % gp, yr[:, gp, :, :], yrT_s.rearrange("j p t -> p j t")[:, :, g * 512:(g + 1) * 512], writes=[("yr", gp)])
            k.dma("sp", "ya%d" % gp, ya[:, gp, :, :], yaT_s.rearrange("j p t -> p j t")[:, :, g * 512:(g + 1) * 512], writes=[("ya", gp)])

        def np1_3a(g):
            norm_p1("n3a", xts[g % 2], ("xt", g % 2), 4, xn, junk, ssq, rstd)

        def np2_3a(g):
            norm_p2("n3a", sc1p, modT[:, 0:8], hTd[:, g % 2], ("hT", g % 2), 4, xn, [0, 1], "alt")

        load_g(0)
        np1_3a(0)
        np2_3a(0)
        for g in range(NG):
            if g + 1 < NG:
                load_g(g + 1)
            gp = g % 2
            xt = xts[gp]
            hT = hTd[:, gp]
            for j in range(8):
                if j == 4 and g + 1 < NG:
                    np1_3a(g + 1)
                jp = j % 2
                for br, (col0, bi) in enumerate(((0, 2), (1024, 3))):
                    for kc in range(8):
                        k.op("pe", lambda e, bi=bi, kc=kc, col0=col0, j=j: e.matmul(
                            banks[bi], lhsT=wgm[:, kc, col0 + j * 128:col0 + (j + 1) * 128], rhs=hT[:, kc, :], start=(kc == 0), stop=(kc == 7)),
                            reads=[(("hT", gp), kc)] + wkeys("wgm", col0 + j * 128, col0 + (j + 1) * 128), writes=[("bank", bi)], sig=(kc == 7))
                    k.op("act", lambda e, bi=bi, br=br, jp=jp: e.activation(out=sg[:, jp, br, :], in_=banks[bi], func=AF.Sigmoid),
                         reads=[("bank", bi)], writes=[("sg", jp, br)])
                for br, (wsb, wn, src, sk, bi) in enumerate(((wpr, "wpr", yr, "yr", 4), (wpa, "wpa", ya, "ya", 5))):
                    for kc in range(8):
                        k.op("pe", lambda e, bi=bi, kc=kc, j=j, wsb=wsb, src=src: e.matmul(
                            banks[bi], lhsT=wsb[:, kc, j * 128:(j + 1) * 128], rhs=src[:, gp, kc, :], start=(kc == 0), stop=(kc == 7)),
                            reads=[(sk, gp)] + wkeys(wn, j * 128, (j + 1) * 128), writes=[("bank", bi)], sig=(kc == 7))
                    k.op("dve", lambda e, bi=bi, br=br, jp=jp: e.tensor_tensor(out=mm[:, jp, br, :], in0=banks[bi], in1=sg[:, jp, br, :], op=ALU.mult),
                         reads=[("bank", bi), ("sg", jp, br)], writes=[("mm", jp, br)])
                k.op("pool", lambda e, j=j, jp=jp: e.tensor_tensor(out=mg[:, j, :], in0=mm[:, jp, 0, :], in1=mm[:, jp, 1, :], op=ALU.add),
                     reads=[("mm", jp, 0), ("mm", jp, 1)], writes=[("mg", j)])
            if g + 1 < NG:
                np2_3a(g + 1)
            for s in range(4):
                for half in range(2):
                    bi = 6 + half
                    hsl = slice(half * 512, (half + 1) * 512)
                    for kc in range(8):
                        k.op("pe", lambda e, bi=bi, kc=kc, s=s, hsl=hsl: e.matmul(
                            banks[bi], lhsT=mg[:, kc, s * 128:(s + 1) * 128], rhs=wo[:, kc, hsl], start=(kc == 0), stop=(kc == 7)),
                            reads=[("mg", kc)] + wkeys("wo", half * 512, (half + 1) * 512), writes=[("bank", bi)], sig=(kc == 7))
                    k.op("dve", lambda e, bi=bi, half=half, hsl=hsl: e.tensor_tensor(out=tt[:, half, :], in0=banks[bi], in1=g1_b[:, hsl], op=ALU.mult),
                         reads=[("bank", bi), "g1_b"], writes=[("tt", half)])
                    k.op("pool", lambda e, s=s, half=half, hsl=hsl, xt=xt: e.tensor_tensor(out=xt[:, s, hsl], in0=tt[:, half, :], in1=xt[:, s, hsl], op=ALU.add),
                         reads=[("tt", half), ("xt", gp)], writes=[("xt", gp)])
            k.dma("sp", "x1t%d" % gp, x1_s[g * 512:(g + 1) * 512, :].rearrange("(s p) d -> p s d", p=128), xt[:],
                  reads=[("xt", gp)], writes=[("x1_s", g)])
        k.barrier()

    if upto <= 4:
        k.barrier()
        return nc, k
    NG3 = S // 256
    with ExitStack() as es:
     wf1 = es.enter_context(nc.sbuf_tensor("s_wf1", [128, 8, DFF], BF16))
     wf2 = es.enter_context(nc.sbuf_tensor("s_wf2", [128, 32, D], BF16))
     x1A = es.enter_context(nc.sbuf_tensor("s_x1A", [128, 2, D], F32))
     x1B = es.enter_context(nc.sbuf_tensor("s_x1B", [128, 2, D], F32))
     xn = es.enter_context(nc.sbuf_tensor("s_d_xn", [128, 2, D], BF16))
     junk = es.enter_context(nc.sbuf_tensor("s_d_junk", [128, D], BF16))
     h2Td = es.enter_context(nc.sbuf_tensor("s_h2T", [128, 2, 8, 256], BF16))
     ssq = es.enter_context(nc.sbuf_tensor("s_d_ssq", [128, 4], F32))
     rstd = es.enter_context(nc.sbuf_tensor("s_d_rstd", [128, 4], F32))
     sq3 = es.enter_context(nc.sbuf_tensor("s_sq3", [128, 2, 256], F32))
     uT = es.enter_context(nc.sbuf_tensor("s_uT", [128, 32, 256], BF16))
     tt = es.enter_context(nc.sbuf_tensor("s_d_tt", [128, 2, 512], F32))
     if True:
        load_w_cast("wf1", wf1, wf1_d, 0, 0, DFF, 8)
        for fg in range(4):
            k.dma_group("pool", "w_wf2_%d" % fg, [(wf2[:, fk, :], wf2_d[fk * 128:(fk + 1) * 128, :]) for fk in range(fg * 8, fg * 8 + 8)], [("wf2", fg)])
        x1s = [x1A, x1B]

        def load_x1(g):
            k.dma("sp", "x1l%d" % (g % 2), x1s[g % 2][:], x1_s[g * 256:(g + 1) * 256, :].rearrange("(s p) d -> p s d", p=128), writes=[("x1", g % 2)])

        def np1_3b(g):
            norm_p1("n3b", x1s[g % 2], ("x1", g % 2), 2, xn, junk, ssq, rstd)

        def np2_3b(g):
            norm_p2("n3b", sc2p, modT[:, 24:32], h2Td[:, g % 2], ("h2T", g % 2), 2, xn, [0, 1, 7], "alt")

        load_x1(0)
        np1_3b(0)
        np2_3b(0)
        if NG3 > 1:
            load_x1(1)
        for g in range(NG3):
            gp = g % 2
            x1 = x1s[gp]
            h2T = h2Td[:, gp]
            for f in range(32):
                bi = 2 + (f % 3)
                fp = f % 2
                for kc in range(8):
                    k.op("pe", lambda e, bi=bi, kc=kc, f=f: e.matmul(
                        banks[bi][:, 0:256], lhsT=wf1[:, kc, f * 128:(f + 1) * 128], rhs=h2T[:, kc, :], start=(kc == 0), stop=(kc == 7)),
                        reads=[(("h2T", gp), kc)] + wkeys("wf1", f * 128, (f + 1) * 128), writes=[("bank", bi)], sig=(kc == 7))
                k.op("act", lambda e, bi=bi, fp=fp: e.activation(out=sq3[:, fp, :], in_=banks[bi][:, 0:256], func=AF.Square),
                     reads=[("bank", bi)], writes=[("sq3", fp)])
                k.op("dve", lambda e, bi=bi, fp=fp, f=f: e.scalar_tensor_tensor(out=uT[:, f, :], in0=banks[bi][:, 0:256], scalar=0.0, in1=sq3[:, fp, :],
                                                                              op0=ALU.is_gt, op1=ALU.mult),
                     reads=[("bank", bi), ("sq3", fp)], writes=[("uT", f)])
                if f == 15 and g + 1 < NG3:
                    np1_3b(g + 1)
            if g + 1 < NG3:
                np2_3b(g + 1)
            for s in range(2):
                for half in range(2):
                    bi = 5 + ((s * 2 + half) % 2)
                    hsl = slice(half * 512, (half + 1) * 512)
                    for f in range(32):
                        k.op("pe", lambda e, bi=bi, f=f, s=s, hsl=hsl: e.matmul(
                            banks[bi], lhsT=uT[:, f, s * 128:(s + 1) * 128], rhs=wf2[:, f, hsl], start=(f == 0), stop=(f == 31)),
                            reads=[("uT", f), ("wf2", f // 8)], writes=[("bank", bi)], sig=(f == 31))
                    k.op("dve", lambda e, bi=bi, half=half, hsl=hsl: e.tensor_tensor(out=tt[:, half, :], in0=banks[bi], in1=g2_b[:, hsl], op=ALU.mult),
                         reads=[("bank", bi), "g2_b"], writes=[("tt", half)])
                    k.op("pool", lambda e, s=s, half=half, hsl=hsl, x1=x1: e.tensor_tensor(out=x1[:, s, hsl], in0=tt[:, half, :], in1=x1[:, s, hsl], op=ALU.add),
                         reads=[("tt", half), ("x1", gp)], writes=[("x1", gp)])
            k.dma("sp", "outst%d" % gp, out_d[g * 256:(g + 1) * 256, :].rearrange("(s p) d -> p s d", p=128), x1[:],
                  reads=[("x1", gp)], writes=[("out", g)])
            if g + 2 < NG3:
                load_x1(g + 2)
        k.barrier()
    return nc, k


_CACHE = {}


def _prep_inputs(S, x, c, positions, w_ada, b_ada, w_in, conv_w, conv_b, rglru_wa, rglru_ba, rglru_wx, rglru_bx,
                 rglru_lambda, q_norm_gain, k_norm_gain, lambda_q1, lambda_k1, lambda_q2, lambda_k2, subln_gain,
                 w_proj_rnn, w_proj_attn, w_out, w_ff1, w_ff2):
    f = lambda a: np.ascontiguousarray(np.asarray(a, dtype=np.float32))
    B = x.shape[0]
    NT = S // 128
    col8 = lambda v: f(np.asarray(v).reshape(8, 128).T)
    invf = (10000.0 ** (-np.arange(0, 64, 2, dtype=np.float32) / 64.0)).astype(np.float32)
    shared = {
        "invf": f(np.broadcast_to(invf[None, :], (128, 32))),
        "w_ada": f(w_ada[0]),
        "b_adaT": f(np.asarray(b_ada[0]).reshape(48, 128).T),
        "w_in": f(w_in[0]),
        "conv_wT": f(np.asarray(conv_w[0]).reshape(4, 8, 128).transpose(2, 1, 0).reshape(128, 32)),
        "conv_bT": col8(conv_b[0]),
        "rglru_wa": f(rglru_wa[0]),
        "rglru_wx": f(rglru_wx[0]),
        "baT": col8(rglru_ba[0]),
        "bxT": col8(rglru_bx[0]),
        "lamT": col8(rglru_lambda[0]),
        "gq": f(np.asarray(q_norm_gain[0]).reshape(1, 64)),
        "gk": f(np.asarray(k_norm_gain[0]).reshape(1, 64)),
        "lq1": f(np.asarray(lambda_q1[0]).reshape(1, 64)),
        "lk1": f(np.asarray(lambda_k1[0]).reshape(1, 64)),
        "lq2": f(np.asarray(lambda_q2[0]).reshape(1, 64)),
        "lk2": f(np.asarray(lambda_k2[0]).reshape(1, 64)),
        "subg": f(np.asarray(subln_gain[0]).reshape(1, 128)),
        "w_proj_rnn": f(w_proj_rnn[0]),
        "w_proj_attn": f(w_proj_attn[0]),
        "w_out": f(w_out[0]),
        "w_ff1": f(w_ff1[0]),
        "w_ff2": f(w_ff2[0]),
    }
    in_maps = []
    for b in range(B):
        m = dict(shared)
        m["x"] = f(x[b])
        m["cT"] = f(np.asarray(c[b]).reshape(8, 128).T)
        m["posT"] = np.ascontiguousarray(np.asarray(positions[b], dtype=np.int32).reshape(NT, 128).T)
        in_maps.append(m)
    return in_maps


def kernel(**inputs):
    x = np.asarray(inputs["x"])
    B, S, _ = x.shape
    if S not in _CACHE:
        _CACHE[S] = build(S)[0]
    nc = _CACHE[S]
    in_maps = _prep_inputs(S, **inputs)
    res = run_bass_kernel_spmd(nc, in_maps, core_ids=list(range(B)))
    return np.stack([np.asarray(r["out"], dtype=np.float32).reshape(S, D) for r in res.results], axis=0)
```

```python
import math
from contextlib import ExitStack
import numpy as np
import concourse.bass as bass
import concourse.mybir as mybir
from concourse.bass_utils import run_bass_kernel_spmd

F32 = mybir.dt.float32
BF16 = mybir.dt.bfloat16
I32 = mybir.dt.int32
AF = mybir.ActivationFunctionType
ALU = mybir.AluOpType
AX = mybir.AxisListType

D = 1024
NCH = 8
NH = 8
DFF = 4096
EPS = 1e-6
LAM_INIT = 0.8 - 0.6 * math.exp(-0.3 * 0)
TWO_PI = 2.0 * math.pi
C1_2PI = 6.28125
C2_2PI = TWO_PI - C1_2PI
PI_SAFE = 3.1415925


class K:
    def __init__(self, nc):
        self.nc = nc
        self.engs = {"pe": nc.tensor, "act": nc.scalar, "dve": nc.vector, "pool": nc.gpsimd, "sp": nc.sync}
        self.sem = {}
        self.cnt = {}
        self.waited = {e: {} for e in self.engs}
        self._ctx = []
        for e in self.engs:
            cm = nc.semaphore("s_" + e)
            self.sem[e] = cm.__enter__()
            self._ctx.append(cm)
            self.cnt[e] = 0
        self.dsem = {}
        self.dcnt = {}
        self.lastw = {}
        self.readers = {}
        self.nins = 0

    def _wait(self, e, deps):
        for d in deps:
            if d is None:
                continue
            kind, key, val = d
            if kind == "e" and key == e and e == "pe":
                continue
            wk = (kind, key)
            if self.waited[e].get(wk, 0) >= val:
                continue
            self.waited[e][wk] = val
            self.engs[e].wait_ge(self.sem[key] if kind == "e" else self.dsem[key], val)

    def _deps(self, reads, writes):
        deps = []
        for r in reads:
            if r in self.lastw:
                deps.append(self.lastw[r])
        for w in writes:
            if w in self.lastw:
                deps.append(self.lastw[w])
            deps.extend(self.readers.get(w, ()))
        return deps

    def _commit(self, tok, reads, writes):
        for r in reads:
            self.readers.setdefault(r, []).append(tok)
        for w in writes:
            self.lastw[w] = tok
            self.readers[w] = []

    def op(self, e, ins, reads=(), writes=(), sig=True):
        self._wait(e, self._deps(reads, writes))
        i = ins(self.engs[e])
        self.nins += 1
        if sig:
            self.cnt[e] += 1
            i.then_inc(self.sem[e], 1)
            tok = ("e", e, self.cnt[e])
        else:
            tok = ("e", e, self.cnt[e] + 1)
        self._commit(tok, reads, writes)
        return tok

    def dma(self, e, slot, out, in_, reads=(), writes=(), **kw):
        if slot not in self.dsem:
            cm = self.nc.semaphore("d_" + slot)
            self.dsem[slot] = cm.__enter__()
            self._ctx.append(cm)
            self.dcnt[slot] = 0
        self._wait(e, self._deps(reads, writes))
        self.engs[e].dma_start(out=out, in_=in_, **kw).then_inc(self.dsem[slot], 16)
        self.nins += 1
        self.dcnt[slot] += 16
        tok = ("d", slot, self.dcnt[slot])
        self._commit(tok, reads, writes)
        return tok

    def dma_group(self, e, slot, pairs, keys):
        deps = self._deps((), keys)
        tok = None
        for (o, i) in pairs:
            tok = self.dma(e, slot, o, i, reads=(), writes=())
            if deps:
                pass
        for kk in keys:
            self.lastw[kk] = tok
            self.readers[kk] = []
        return tok

    def barrier(self, engines=("pe", "act", "dve", "pool", "sp")):
        toks = [("e", e, self.cnt[e]) for e in self.engs if self.cnt[e] > 0]
        toks += [("d", s, c) for s, c in self.dcnt.items()]
        for e in engines:
            self._wait(e, toks)


def build(S, upto=5):
    NT = S // 128
    NG = S // 512
    nc = bass.Bass("TRN2", target_bir_lowering=False)
    k = K(nc)

    def din(name, shape, dt=F32):
        return nc.dram_tensor(name, shape, dt, kind="ExternalInput").ap()

    x_d = din("x", [S, D])
    cT_d = din("cT", [128, 8])
    pos_d = din("posT", [128, NT], I32)
    invf_d = din("invf", [128, 32])
    wada_d = din("w_ada", [D, 6 * D])
    bada_d = din("b_adaT", [128, 48])
    win_d = din("w_in", [D, 7168])
    cw_d = din("conv_wT", [128, 32])
    cb_d = din("conv_bT", [128, 8])
    wa_d = din("rglru_wa", [8, 128, 128])
    wx_d = din("rglru_wx", [8, 128, 128])
    ba_d = din("baT", [128, 8])
    bx_d = din("bxT", [128, 8])
    lam_d = din("lamT", [128, 8])
    gq_d = din("gq", [1, 64])
    gk_d = din("gk", [1, 64])
    lq1_d = din("lq1", [1, 64])
    lk1_d = din("lk1", [1, 64])
    lq2_d = din("lq2", [1, 64])
    lk2_d = din("lk2", [1, 64])
    subg_d = din("subg", [1, 128])
    wpr_d = din("w_proj_rnn", [D, D])
    wpa_d = din("w_proj_attn", [D, D])
    wo_d = din("w_out", [D, D])
    wf1_d = din("w_ff1", [D, DFF])
    wf2_d = din("w_ff2", [DFF, D])
    out_d = nc.dram_tensor("out", [S, D], F32, kind="ExternalOutput").ap()

    def dscr(name, shape, dt):
        return nc.dram_tensor(name, shape, dt, kind="Internal").ap()

    qT_s = dscr("qT_s", [NH, 128, S], BF16)
    kT_s = dscr("kT_s", [NH, 128, S], BF16)
    v_s = dscr("v_s", [S, D], BF16)
    yrT_s = dscr("yrT_s", [NCH, 128, S], BF16)
    yaT_s = dscr("yaT_s", [NH, 128, S], BF16)
    x1_s = dscr("x1_s", [S, D], F32)
    gate_s = dscr("gate_s", [1, 2 * D], F32)

    sb = lambda n, s, d=F32: nc.alloc_sbuf_tensor("s_" + n, s, d)

    ident = sb("ident", [128, 128], BF16)
    trimask = sb("trimask", [128, 128], BF16)
    cT = sb("cT", [128, 8])
    badaT = sb("badaT", [128, 48])
    modT = sb("modT", [128, 48])
    sc1p = sb("sc1p", [128, 8])
    sc2p = sb("sc2p", [128, 8])
    cw = sb("cw", [128, 32])
    cb = sb("cb", [128, 8])
    hba = sb("hba", [128, 8])
    hbx = sb("hbx", [128, 8])
    lam = sb("lam", [128, 8])
    c1 = sb("c1", [128, 8])
    c2 = sb("c2", [128, 8])
    gq_b = sb("gq_b", [128, 64])
    gk_b = sb("gk_b", [128, 64])
    ngq_b = sb("ngq_b", [128, 64])
    ngk_b = sb("ngk_b", [128, 64])
    lq1_b = sb("lq1_b", [128, 64]); lk1_b = sb("lk1_b", [128, 64])
    lq2_b = sb("lq2_b", [128, 64]); lk2_b = sb("lk2_b", [128, 64])
    neg_lam = sb("neg_lam", [128, 1])
    subg_b = sb("subg_b", [128, 128])
    g1_b = sb("g1_b", [128, D])
    g2_b = sb("g2_b", [128, D])
    mhalf = sb("mhalf", [128, 64])
    hstate = sb("hstate", [128, 8])

    for nm, t, d in [("cT", cT, cT_d), ("badaT", badaT, bada_d), ("cw", cw, cw_d), ("cb", cb, cb_d),
                     ("hba", hba, ba_d), ("hbx", hbx, bx_d), ("lam", lam, lam_d)]:
        k.dma("sp", "c_" + nm, t[:], d, writes=[nm])
    for nm, t, d in [("gq_b", gq_b, gq_d), ("gk_b", gk_b, gk_d), ("lq1_b", lq1_b, lq1_d), ("lk1_b", lk1_b, lk1_d),
                     ("lq2_b", lq2_b, lq2_d), ("lk2_b", lk2_b, lk2_d), ("subg_b", subg_b, subg_d)]:
        k.dma("sp", "c_" + nm, t[:], d.partition_broadcast(128), writes=[nm])

    idf = sb("idf", [128, 128])
    k.op("pool", lambda e: e.memset(idf[:], 1.0), writes=["idf"])
    k.op("pool", lambda e: e.affine_select(out=idf[:], in_=idf[:], pattern=[[-1, 128]], compare_op=ALU.is_equal,
                                           fill=0.0, base=0, channel_multiplier=1), reads=["idf"], writes=["idf"])
    k.op("pool", lambda e: e.tensor_copy(out=ident[:], in_=idf[:]), reads=["idf"], writes=["ident"])
    k.op("pool", lambda e: e.memset(idf[:], 1.0), writes=["idf"])
    k.op("pool", lambda e: e.affine_select(out=idf[:], in_=idf[:], pattern=[[1, 128]], compare_op=ALU.is_ge,
                                           fill=0.0, base=0, channel_multiplier=-1), reads=["idf"], writes=["idf"])
    k.op("pool", lambda e: e.tensor_copy(out=trimask[:], in_=idf[:]), reads=["idf"], writes=["trimask"])
    k.op("pool", lambda e: e.memset(mhalf[:], -0.5), writes=["mhalf"])
    k.op("pool", lambda e: e.memset(hstate[:], 0.0), writes=["hstate"])

    k.op("dve", lambda e: e.tensor_scalar(out=hba[:], in0=hba[:], scalar1=0.5, scalar2=None, op0=ALU.mult), reads=["hba"], writes=["hba"])
    k.op("dve", lambda e: e.tensor_scalar(out=hbx[:], in0=hbx[:], scalar1=0.5, scalar2=None, op0=ALU.mult), reads=["hbx"], writes=["hbx"])
    k.op("dve", lambda e: e.tensor_scalar(out=ngq_b[:], in0=gq_b[:], scalar1=-1.0, scalar2=None, op0=ALU.mult), reads=["gq_b"], writes=["ngq_b"])
    k.op("dve", lambda e: e.tensor_scalar(out=ngk_b[:], in0=gk_b[:], scalar1=-1.0, scalar2=None, op0=ALU.mult), reads=["gk_b"], writes=["ngk_b"])
    k.op("dve", lambda e: e.tensor_scalar(out=subg_b[:], in0=subg_b[:], scalar1=1.0 - LAM_INIT, scalar2=None, op0=ALU.mult), reads=["subg_b"], writes=["subg_b"])
    t_ab = sb("t_ab", [128, 8]); t_mx = sb("t_mx", [128, 8]); t_e = sb("t_e", [128, 8])
    k.op("dve", lambda e: e.tensor_scalar(out=t_mx[:], in0=lam[:], scalar1=-1.0, scalar2=None, op0=ALU.mult), reads=["lam"], writes=["t_mx"])
    k.op("dve", lambda e: e.tensor_tensor(out=t_ab[:], in0=lam[:], in1=t_mx[:], op=ALU.max), reads=["lam", "t_mx"], writes=["t_ab"])
    k.op("dve", lambda e: e.tensor_scalar(out=t_mx[:], in0=t_mx[:], scalar1=0.0, scalar2=None, op0=ALU.max), reads=["t_mx", "t_ab"], writes=["t_mx"])
    k.op("act", lambda e: e.activation(out=t_e[:], in_=t_ab[:], func=AF.Exp, scale=-1.0), reads=["t_ab"], writes=["t_e"])
    k.op("act", lambda e: e.activation(out=t_e[:], in_=t_e[:], func=AF.Ln, bias=1.0), reads=["t_e"], writes=["t_e"])
    k.op("dve", lambda e: e.tensor_tensor(out=t_mx[:], in0=t_mx[:], in1=t_e[:], op=ALU.add), reads=["t_mx", "t_e"], writes=["t_mx"])
    k.op("dve", lambda e: e.tensor_scalar(out=c1[:], in0=t_mx[:], scalar1=-4.0, scalar2=None, op0=ALU.mult), reads=["t_mx"], writes=["c1"])
    k.op("dve", lambda e: e.tensor_scalar(out=c2[:], in0=t_mx[:], scalar1=-8.0, scalar2=None, op0=ALU.mult), reads=["t_mx"], writes=["c2"])
    t_p = sb("t_p", [128, 64]); t_s = sb("t_s", [128, 2])
    k.op("dve", lambda e: e.tensor_tensor(out=t_p[:], in0=lq1_b[:], in1=lk1_b[:], op=ALU.mult), reads=["lq1_b", "lk1_b"], writes=["t_p"])
    k.op("dve", lambda e: e.tensor_reduce(out=t_s[:, 0:1], in_=t_p[:], axis=AX.X, op=ALU.add), reads=["t_p"], writes=["t_s0"])
    k.op("dve", lambda e: e.tensor_tensor(out=t_p[:], in0=lq2_b[:], in1=lk2_b[:], op=ALU.mult), reads=["lq2_b", "lk2_b", "t_s0"], writes=["t_p"])
    k.op("dve", lambda e: e.tensor_reduce(out=t_s[:, 1:2], in_=t_p[:], axis=AX.X, op=ALU.add), reads=["t_p"], writes=["t_s1"])
    k.op("act", lambda e: e.activation(out=t_s[:], in_=t_s[:], func=AF.Exp), reads=["t_s0", "t_s1"], writes=["t_s"])
    k.op("dve", lambda e: e.scalar_tensor_tensor(out=neg_lam[:], in0=t_s[:, 1:2], scalar=-LAM_INIT, in1=t_s[:, 0:1],
                                                 op0=ALU.add, op1=ALU.subtract), reads=["t_s"], writes=["neg_lam"])

    c_act2 = sb("c_act2", [128, 8, 2])
    k.op("act", lambda e: e.activation(out=c_act2[:, :, 0], in_=cT[:], func=AF.Silu), reads=["cT"], writes=["ca0"])
    k.op("act", lambda e: e.activation(out=c_act2[:, :, 1], in_=cT[:], func=AF.Silu), reads=["cT"], writes=["ca1"])
    psall = nc.alloc_psum_tensor("psall", [128, 8, 512], F32)
    banks = [psall[:, i, :] for i in range(8)]
    ps_mod = banks[7][:, 0:96].rearrange("p (j t) -> p j t", t=2)
    wada_v = wada_d.rearrange("(c p) n -> p c n", p=128)
    wq_cm = nc.sbuf_tensor("s_wqkv", [128, 8, 3072], BF16)
    wqkv = wq_cm.__enter__()
    k.dma_group("pool", "w_wqkv", [(wqkv[:, kc, c0:c0 + 1024], win_d[kc * 128:(kc + 1) * 128, 2048 + c0:2048 + c0 + 1024])
                                   for kc in range(8) for c0 in range(0, 3072, 1024)], ["wqkv"])
    with ExitStack() as es:
     wada0 = es.enter_context(nc.sbuf_tensor("s_wada0", [128, 8, 1024], F32))
     wada1 = es.enter_context(nc.sbuf_tensor("s_wada1", [128, 8, 1024], F32))
     if True:
        wadas = [wada0, wada1]
        for jg in range(2):
            wsb = wadas[jg % 2]
            k._wait("sp", k._deps((), [("wada", jg % 2)]))
            k.dma_group("sp", "wada%d" % (jg % 2), [(wsb[:, kc, :], wada_d[kc * 128:(kc + 1) * 128, jg * 1024:(jg + 1) * 1024]) for kc in range(8)],
                        [("wada", jg % 2)])
            for jj in range(8):
                j = jg * 8 + jj
                for kc in range(8):
                    k.op("pe", lambda e, wsb=wsb, kc=kc, jj=jj, j=j: e.matmul(
                        ps_mod[:, j, :], lhsT=wsb[:, kc, jj * 128:(jj + 1) * 128], rhs=c_act2[:, kc, :],
                        start=(kc == 0), stop=(kc == 7)),
                        reads=[("wada", jg % 2), "ca0", "ca1"], writes=["ps_mod"], sig=(kc == 7))
        k.op("dve", lambda e: e.tensor_tensor(out=modT[:, 0:16], in0=ps_mod[:, 0:16, 0], in1=badaT[:, 0:16], op=ALU.add),
             reads=["ps_mod", "badaT"], writes=["modT"])
        k.barrier()
    k.op("dve", lambda e: e.tensor_scalar(out=sc1p[:], in0=modT[:, 8:16], scalar1=1.0, scalar2=None, op0=ALU.add), reads=["modT"], writes=["sc1p"])


    def bank_bf(i):
        return banks[i].bitcast(BF16)

    def norm_p1(tag, src, srckey, nsub, xn, junk, ssq, rstd, rstd_mode="pool"):
        for s in range(nsub):
            k.op("act", lambda e, s=s: e.activation(out=junk[:], in_=src[:, s, :], func=AF.Square, accum_out=ssq[:, s:s + 1]),
                 reads=[srckey], writes=[(tag, "ssq", s), (tag, "junk")])
        if rstd_mode == "act":
            k.op("act", lambda e: e.activation(out=rstd[:, 0:nsub], in_=ssq[:, 0:nsub], func=AF.Ln, scale=1.0 / D, bias=EPS),
                 reads=[(tag, "ssq", s) for s in range(nsub)], writes=[(tag, "rstd")])
            k.op("act", lambda e: e.activation(out=rstd[:, 0:nsub], in_=rstd[:, 0:nsub], func=AF.Exp, scale=-0.5),
                 reads=[(tag, "rstd")], writes=[(tag, "rstd")])
        else:
            k.op("pool", lambda e: e.tensor_scalar(out=rstd[:, 0:nsub], in0=ssq[:, 0:nsub], scalar1=1.0 / D, scalar2=EPS, op0=ALU.mult, op1=ALU.add),
                 reads=[(tag, "ssq", s) for s in range(nsub)], writes=[(tag, "rstd")])
            k.op("pool", lambda e: e.tensor_tensor(out=rstd[:, 0:nsub], in0=rstd[:, 0:nsub], in1=mhalf[:, 0:nsub], op=ALU.pow),
                 reads=[(tag, "rstd"), "mhalf"], writes=[(tag, "rstd")])
        for s in range(nsub):
            eng = "dve" if s % 2 == 0 else "pool"
            k.op(eng, lambda e, s=s: e.tensor_scalar(out=xn[:, s, :], in0=src[:, s, :], scalar1=rstd[:, s:s + 1], scalar2=0.0, op0=ALU.mult, op1=ALU.add),
                 reads=[srckey, (tag, "rstd")], writes=[(tag, "xn", s)])

    def norm_p2(tag, scp, shift, hT, hTkey, nsub, xn, tbanks, evac_eng):
        for c in range(8):
            bi = tbanks[c % len(tbanks)]
            pT = bank_bf(bi)
            for s in range(nsub):
                k.op("pe", lambda e, s=s, c=c, pT=pT: e.transpose(pT[:, s * 128:(s + 1) * 128], xn[:, s, c * 128:(c + 1) * 128], ident[:]),
                     reads=[(tag, "xn", s), "ident"], writes=[("bank", bi)], sig=(s == nsub - 1))
            ee = evac_eng if evac_eng != "alt" else ("act" if c % 2 == 0 else "dve")
            if ee == "act":
                k.op("act", lambda e, c=c, pT=pT: e.activation(out=hT[:, c, :], in_=pT[:, 0:nsub * 128], func=AF.Identity,
                                                               scale=scp[:, c:c + 1], bias=shift[:, c:c + 1]),
                     reads=[("bank", bi), "modT", "sc1p", "sc2p"], writes=[(hTkey, c)])
            else:
                k.op("dve", lambda e, c=c, pT=pT: e.tensor_scalar(out=hT[:, c, :], in0=pT[:, 0:nsub * 128], scalar1=scp[:, c:c + 1],
                                                                  scalar2=shift[:, c:c + 1], op0=ALU.mult, op1=ALU.add),
                     reads=[("bank", bi), "modT", "sc1p", "sc2p"], writes=[(hTkey, c)])

    def norm_T(tag, src, srckey, scp, shift, hT, hTkey, nsub, xn, junk, ssq, rstd, tbanks, evac_eng):
        norm_p1(tag, src, srckey, nsub, xn, junk, ssq, rstd)
        norm_p2(tag, scp, shift, hT, hTkey, nsub, xn, tbanks, evac_eng)

    def load_w_cast(name, wsb, wd, row0, col0, ncols, nk, piece=1024):
        for c0 in range(0, ncols, piece):
            c1_ = min(ncols, c0 + piece)
            pairs = [(wsb[:, kc, c0:c1_], wd[row0 + kc * 128:row0 + (kc + 1) * 128, col0 + c0:col0 + c1_]) for kc in range(nk)]
            k.dma_group("pool", "w_%s_%d" % (name, c0), pairs, [(name, c0)])

    def wkeys(name, lo, hi, piece=1024):
        return [(name, c0) for c0 in range((lo // piece) * piece, hi, piece)]

    with ExitStack() as es:
     cos_t = es.enter_context(nc.sbuf_tensor("s_cos_t", [128, NT, 32], F32))
     sin_t = es.enter_context(nc.sbuf_tensor("s_sin_t", [128, NT, 32], F32))
     if True:
      with ExitStack() as es:
       pos_i = es.enter_context(nc.sbuf_tensor("s_pos_i", [128, NT], I32))
       pos_f = es.enter_context(nc.sbuf_tensor("s_pos_f", [128, NT], F32))
       invf = es.enter_context(nc.sbuf_tensor("s_invf", [128, 32], F32))
       ang = es.enter_context(nc.sbuf_tensor("s_ang", [128, NT, 32], F32))
       rk = es.enter_context(nc.sbuf_tensor("s_rk", [128, NT, 32], F32))
       ki = es.enter_context(nc.sbuf_tensor("s_ki", [128, NT, 32], I32))
       dd = es.enter_context(nc.sbuf_tensor("s_dd", [128, NT, 32], F32))
       dc = es.enter_context(nc.sbuf_tensor("s_dc", [128, NT, 32], F32))
       if True:
            k.dma("sp", "pos", pos_i[:], pos_d, writes=["pos_i"])
            k.dma("sp", "invf", invf[:], invf_d, writes=["invf"])
            k.op("dve", lambda e: e.tensor_copy(out=pos_f[:], in_=pos_i[:]), reads=["pos_i"], writes=["pos_f"])
            k.op("dve", lambda e: e.tensor_tensor(out=ang[:], in0=pos_f[:].unsqueeze(2).to_broadcast([128, NT, 32]),
                                                  in1=invf[:].unsqueeze(1).to_broadcast([128, NT, 32]), op=ALU.mult),
                 reads=["pos_f", "invf"], writes=["ang"])
            k.op("dve", lambda e: e.tensor_scalar(out=rk[:], in0=ang[:], scalar1=1.0 / TWO_PI, scalar2=None, op0=ALU.mult), reads=["ang"], writes=["rk"])
            k.op("dve", lambda e: e.tensor_copy(out=ki[:], in_=rk[:]), reads=["rk"], writes=["ki"])
            k.op("dve", lambda e: e.tensor_copy(out=rk[:], in_=ki[:]), reads=["ki"], writes=["rk"])
            k.op("dve", lambda e: e.scalar_tensor_tensor(out=dd[:], in0=rk[:], scalar=-C1_2PI, in1=ang[:], op0=ALU.mult, op1=ALU.add),
                 reads=["rk", "ang"], writes=["dd"])
            k.op("dve", lambda e: e.scalar_tensor_tensor(out=dd[:], in0=rk[:], scalar=-C2_2PI, in1=dd[:], op0=ALU.mult, op1=ALU.add),
                 reads=["rk", "dd"], writes=["dd"])
            k.op("dve", lambda e: e.tensor_scalar(out=dd[:], in0=dd[:], scalar1=PI_SAFE, scalar2=-PI_SAFE, op0=ALU.min, op1=ALU.max),
                 reads=["dd"], writes=["dd"])
            k.op("act", lambda e: e.activation(out=sin_t[:], in_=dd[:], func=AF.Sin), reads=["dd"], writes=["sin_t"])
            k.op("dve", lambda e: e.tensor_scalar(out=dc[:], in0=dd[:], scalar1=math.pi / 2, scalar2=None, op0=ALU.add), reads=["dd"], writes=["dc"])
            k.op("dve", lambda e: e.tensor_scalar(out=rk[:], in0=dc[:], scalar1=math.pi, scalar2=None, op0=ALU.is_gt), reads=["dc"], writes=["rk"])
            k.op("dve", lambda e: e.scalar_tensor_tensor(out=dc[:], in0=rk[:], scalar=-TWO_PI, in1=dc[:], op0=ALU.mult, op1=ALU.add),
                 reads=["rk", "dc"], writes=["dc"])
            k.op("dve", lambda e: e.tensor_scalar(out=dc[:], in0=dc[:], scalar1=PI_SAFE, scalar2=-PI_SAFE, op0=ALU.min, op1=ALU.max),
                 reads=["dc"], writes=["dc"])
            k.op("act", lambda e: e.activation(out=cos_t[:], in_=dc[:], func=AF.Sin), reads=["dc"], writes=["cos_t"])
            k.barrier()
      with ExitStack() as es:
       xt1 = es.enter_context(nc.sbuf_tensor("s_b_xt", [128, 4, D], F32))
       xn = es.enter_context(nc.sbuf_tensor("s_b_xn", [128, 4, D], BF16))
       junk = es.enter_context(nc.sbuf_tensor("s_b_junk", [128, D], BF16))
       hTd = es.enter_context(nc.sbuf_tensor("s_b_hT", [128, 2, 8, 512], BF16))
       ssq = es.enter_context(nc.sbuf_tensor("s_b_ssq", [128, 4], F32))
       rstd = es.enter_context(nc.sbuf_tensor("s_b_rstd", [128, 4], F32))
       Tq = es.enter_context(nc.sbuf_tensor("s_Tq", [128, 4, 2, 64], F32))
       Tk = es.enter_context(nc.sbuf_tensor("s_Tk", [128, 4, 2, 64], F32))
       sqj = es.enter_context(nc.sbuf_tensor("s_sqj", [128, 2, D], BF16))
       gss = es.enter_context(nc.sbuf_tensor("s_gss", [128, 2, 16], F32))
       grs = es.enter_context(nc.sbuf_tensor("s_grs", [128, 2, 16], F32))
       m1 = es.enter_context(nc.sbuf_tensor("s_m1", [128, 2, 2, D], F32))
       m2 = es.enter_context(nc.sbuf_tensor("s_m2", [128, 2, 2, D], F32))
       ob = es.enter_context(nc.sbuf_tensor("s_ob", [128, 2, 2, D], BF16))
       qTst = es.enter_context(nc.sbuf_tensor("s_qTst", [128, 2, 8, 512], BF16))
       kTst = es.enter_context(nc.sbuf_tensor("s_kTst", [128, 2, 8, 512], BF16))
       vst = es.enter_context(nc.sbuf_tensor("s_vst", [128, 2, D], BF16))
       if True:
        def load_x(g):
            k.dma("sp", "b_xt", xt1[:], x_d[g * 512:(g + 1) * 512, :].rearrange("(s p) d -> p s d", p=128), writes=[("xt", 0)])

        def np1(g):
            norm_p1("n1b", xt1, ("xt", 0), 4, xn, junk, ssq, rstd, rstd_mode="act")

        def np2(g):
            norm_p2("n1b", sc1p, modT[:, 0:8], hTd[:, g % 2], ("hT", g % 2), 4, xn, [0, 1], "alt")

        def tables(g):
            for (T, gb, ngb, nm) in ((Tq, gq_b, ngq_b, "Tq"), (Tk, gk_b, ngk_b, "Tk")):
                cs = cos_t[:, 4 * g:4 * g + 4, :]
                sn = sin_t[:, 4 * g:4 * g + 4, :]
                bc = lambda a, lo: a[:, lo:lo + 32].unsqueeze(1).to_broadcast([128, 4, 32])
                k.op("pool", lambda e: e.tensor_tensor(out=T[:, :, 0, 0:32], in0=cs, in1=bc(gb, 0), op=ALU.mult), reads=["cos_t", "gq_b", "gk_b"], writes=[(nm, 0)])
                k.op("pool", lambda e: e.tensor_tensor(out=T[:, :, 0, 32:64], in0=cs, in1=bc(gb, 32), op=ALU.mult), reads=["cos_t", "gq_b", "gk_b"], writes=[(nm, 1)])
                k.op("pool", lambda e: e.tensor_tensor(out=T[:, :, 1, 0:32], in0=sn, in1=bc(ngb, 32), op=ALU.mult), reads=["sin_t", "ngq_b", "ngk_b"], writes=[(nm, 2)])
                k.op("pool", lambda e: e.tensor_tensor(out=T[:, :, 1, 32:64], in0=sn, in1=bc(gb, 0), op=ALU.mult), reads=["sin_t", "gq_b", "gk_b"], writes=[(nm, 3)])

        def mm_sub(g, s):
            hT = hTd[:, g % 2]
            pairs = {}
            for wi, (nm, col0) in enumerate((("q", 0), ("k", 1024), ("v", 2048))):
                idx = (s * 3 + wi) % 3
                b0 = 2 + 2 * idx
                pairs[nm] = b0
                for half in (0, 1):
                    bi = b0 + half
                    for kc in range(8):
                        k.op("pe", lambda e, bi=bi, kc=kc, col0=col0, half=half: e.matmul(
                            banks[bi], lhsT=hT[:, kc, s * 128:(s + 1) * 128],
                            rhs=wqkv[:, kc, col0 + half * 512:col0 + (half + 1) * 512], start=(kc == 0), stop=(kc == 7)),
                            reads=[(("hT", g % 2), kc), "wqkv"], writes=[("bank", bi)], sig=(kc == 7))
            return pairs

        def chains(g, s, pairs):
            gp = g % 2
            sp_ = s % 2
            info = []
            for qi, nm in enumerate(("q", "k")):
                b0 = pairs[nm]
                bkeys = [("bank", b0), ("bank", b0 + 1)]
                px = psall[:, b0:b0 + 2, :]
                k.op("act", lambda e: e.activation(out=sqj[:, qi, :].rearrange("p (b n) -> p b n", b=2), in_=px, func=AF.Square),
                     reads=(), writes=[("sqj", qi)] + bkeys)
                info.append((qi, nm, b0, bkeys, px))
            for (qi, nm, b0, bkeys, px) in info:
                T = Tq if nm == "q" else Tk
                Tn = "Tq" if nm == "q" else "Tk"
                px3 = px.rearrange("p b (g d) -> p (b g) d", d=64)
                px4 = px.rearrange("p b (g t d) -> p (b g) t d", t=2, d=32)
                m1v = m1[:, qi, sp_, :].rearrange("p (g d) -> p g d", d=64)
                m2v = m2[:, qi, sp_, :].rearrange("p (g t d) -> p g t d", t=2, d=32)
                tb = lambda t, lo: T[:, s, t, lo:lo + 32].unsqueeze(1).to_broadcast([128, 16, 32])
                tkeys = [(Tn, i) for i in range(4)]
                k.op("dve", lambda e: e.tensor_tensor(out=m1v, in0=px3, in1=T[:, s, 0, :].unsqueeze(1).to_broadcast([128, 16, 64]), op=ALU.mult),
                     reads=bkeys + tkeys, writes=[("m1", qi, sp_)])
                k.op("dve", lambda e: e.tensor_tensor(out=m2v[:, :, 0, :], in0=px4[:, :, 1, :], in1=tb(1, 0), op=ALU.mult),
                     reads=bkeys + tkeys, writes=[("m2a", qi, sp_)])
                k.op("dve", lambda e: e.tensor_tensor(out=m2v[:, :, 1, :], in0=px4[:, :, 0, :], in1=tb(1, 32), op=ALU.mult),
                     reads=bkeys + tkeys, writes=[("m2b", qi, sp_)])
                k.op("dve", lambda e: e.tensor_reduce(out=gss[:, qi, :], in_=sqj[:, qi, :].rearrange("p (g d) -> p g d", d=64), axis=AX.X, op=ALU.add),
                     reads=[("sqj", qi)], writes=[("gss", qi)])
                k.op("pool", lambda e: e.tensor_tensor(out=m1[:, qi, sp_, :], in0=m1[:, qi, sp_, :], in1=m2[:, qi, sp_, :], op=ALU.add),
                     reads=[("m1", qi, sp_), ("m2a", qi, sp_), ("m2b", qi, sp_)], writes=[("m1", qi, sp_)])
            b0 = pairs["v"]
            k.op("act", lambda e: e.activation(out=vst[:, s % 2, :].rearrange("p (b n) -> p b n", b=2), in_=psall[:, b0:b0 + 2, :], func=AF.Copy),
                 reads=[("bank", b0), ("bank", b0 + 1)], writes=[("vst", s % 2)])
            k.dma("sp", "vst%d" % (s % 2), v_s[g * 512 + s * 128:g * 512 + (s + 1) * 128, :], vst[:, s % 2, :],
                  reads=[("vst", s % 2)], writes=[("v_s", g, s)])
            for (qi, nm, b0, bkeys, px) in info:
                m1v = m1[:, qi, sp_, :].rearrange("p (g d) -> p g d", d=64)
                k.op("act", lambda e: e.activation(out=grs[:, qi, :], in_=gss[:, qi, :], func=AF.Ln, scale=1.0 / 64, bias=EPS),
                     reads=[("gss", qi)], writes=[("grs", qi)])
                k.op("act", lambda e: e.activation(out=grs[:, qi, :], in_=grs[:, qi, :], func=AF.Exp, scale=-0.5),
                     reads=[("grs", qi)], writes=[("grs", qi)])
                k.op("pool", lambda e: e.tensor_tensor(out=ob[:, qi, s % 2, :].rearrange("p (g d) -> p g d", d=64), in0=m1v,
                                                       in1=grs[:, qi, :].unsqueeze(2).to_broadcast([128, 16, 64]), op=ALU.mult),
                     reads=[("m1", qi, sp_), ("grs", qi)], writes=[("ob", qi, s % 2)])

        def make_deferred(g, s):
            gp = g % 2

            def run():
                for qi, nm in enumerate(("q", "k")):
                    pT = bank_bf(qi)
                    for h in range(8):
                        k.op("pe", lambda e, h=h: e.transpose(pT[:, h * 128:(h + 1) * 128], ob[:, qi, s % 2, h * 128:(h + 1) * 128], ident[:]),
                             reads=[("ob", qi, s % 2), "ident"], writes=[("bank", qi)], sig=(h == 7))
                    st = qTst if nm == "q" else kTst
                    k.op("act", lambda e: e.activation(out=st[:, gp, :, s * 128:(s + 1) * 128],
                                                       in_=pT.rearrange("p (h t) -> p h t", t=128), func=AF.Copy),
                         reads=[("bank", qi)], writes=[(nm + "st", gp, s)])
                if s == 3:
                    k.dma("sp", "qst%d" % gp, qT_s.rearrange("h p t -> p h t")[:, :, g * 512:(g + 1) * 512], qTst[:, gp, :, :],
                          reads=[("qst", gp, s_) for s_ in range(4)], writes=[("qT_s", g)])
                    k.dma("sp", "kst%d" % gp, kT_s.rearrange("h p t -> p h t")[:, :, g * 512:(g + 1) * 512], kTst[:, gp, :, :],
                          reads=[("kst", gp, s_) for s_ in range(4)], writes=[("kT_s", g)])
            return run

        load_x(0)
        np1(0)
        if NG > 1:
            load_x(1)
        np2(0)
        dq = []
        for g in range(NG):
            tables(g)
            for s in range(4):
                if len(dq) >= 2:
                    dq.pop(0)()
                pairs = mm_sub(g, s)
                chains(g, s, pairs)
                if s == 0 and g + 1 < NG:
                    np1(g + 1)
                    if g + 2 < NG:
                        load_x(g + 2)
                if s == 1 and g + 1 < NG:
                    np2(g + 1)
                dq.append(make_deferred(g, s))
        while dq:
            dq.pop(0)()
        k.barrier()

    wq_cm.__exit__(None, None, None)
    if upto <= 1:
        k.barrier()
        return nc, k
    JB = 4
    with ExitStack() as es:
     wrg = es.enter_context(nc.sbuf_tensor("s_wrg", [128, 8, 2048], BF16))
     wab = es.enter_context(nc.sbuf_tensor("s_wab", [128, 8, 128], BF16))
     wxb = es.enter_context(nc.sbuf_tensor("s_wxb", [128, 8, 128], BF16))
     xt = es.enter_context(nc.sbuf_tensor("s_a_xt", [128, 4, D], F32))
     xn = es.enter_context(nc.sbuf_tensor("s_a_xn", [128, 4, D], BF16))
     junk = es.enter_context(nc.sbuf_tensor("s_a_junk", [128, D], BF16))
     hTd = es.enter_context(nc.sbuf_tensor("s_a_hT", [128, 2, 8, 512], BF16))
     ssq = es.enter_context(nc.sbuf_tensor("s_a_ssq", [128, 4], F32))
     rstd = es.enter_context(nc.sbuf_tensor("s_a_rstd", [128, 4], F32))
     xrb = es.enter_context(nc.sbuf_tensor("s_xrb", [128, 8, 516], BF16))
     Wd = es.enter_context(nc.sbuf_tensor("s_Wd", [128, 8, 4, 128], BF16))
     xcb = es.enter_context(nc.sbuf_tensor("s_xcb", [128, 2, JB, 512], BF16))
     gg = es.enter_context(nc.sbuf_tensor("s_gg", [128, 2, JB, 512], BF16))
     trr = es.enter_context(nc.sbuf_tensor("s_trr", [128, 2, JB, 512], BF16))
     tii = es.enter_context(nc.sbuf_tensor("s_tii", [128, 2, JB, 512], BF16))
     aa = es.enter_context(nc.sbuf_tensor("s_aa", [128, 2, JB, 512], F32))
     sqv = es.enter_context(nc.sbuf_tensor("s_sqv", [128, 2, JB, 512], F32))
     uu = es.enter_context(nc.sbuf_tensor("s_uu", [128, JB, 512], F32))
     hs = es.enter_context(nc.sbuf_tensor("s_hs", [128, 2, 512], F32))
     yst = es.enter_context(nc.sbuf_tensor("s_yst", [128, 8, 512], BF16))
     if True:
        load_w_cast("wrg", wrg, win_d, 0, 0, 2048, 8)
        k.dma("pool", "w_wab", wab[:], wa_d.rearrange("n k j -> k n j"), writes=["wab"])
        k.dma("pool", "w_wxb", wxb[:], wx_d.rearrange("n k j -> k n j"), writes=["wxb"])
        k.op("pool", lambda e: e.memset(xrb[:, :, 0:3], 0.0), writes=[("xrh", j) for j in range(8)])
        for j in range(8):
            for tap in range(4):
                k.op("pool" if (j * 4 + tap) % 2 else "dve", lambda e, j=j, tap=tap: e.tensor_scalar(
                    out=Wd[:, j, tap, :], in0=ident[:], scalar1=cw[:, j * 4 + tap:j * 4 + tap + 1], scalar2=0.0, op0=ALU.mult, op1=ALU.add),
                    reads=["ident", "cw"], writes=[("Wd", j, tap)])

        def load_x(g):
            k.dma("sp", "a_xt", xt[:], x_d[g * 512:(g + 1) * 512, :].rearrange("(s p) d -> p s d", p=128), writes=[("xt", 0)])

        def np1(g):
            norm_p1("n1a", xt, ("xt", 0), 4, xn, junk, ssq, rstd)

        def np2(g):
            norm_p2("n1a", sc1p, modT[:, 0:8], hTd[:, g % 2], ("hT", g % 2), 4, xn, [0, 1], "dve")

        def stageA1(b):
            g, jb, par = b // 2, (b % 2) * JB, b % 2
            hT = hTd[:, g % 2]
            for jj in range(JB):
                j = jb + jj
                bx_, bg_ = 2 + (jj % 2), 4 + (jj % 2)
                for (bi, col0) in ((bx_, 0), (bg_, 1024)):
                    for kc in range(8):
                        k.op("pe", lambda e, bi=bi, kc=kc, col0=col0: e.matmul(
                            banks[bi], lhsT=wrg[:, kc, col0 + j * 128:col0 + (j + 1) * 128], rhs=hT[:, kc, :],
                            start=(kc == 0), stop=(kc == 7)),
                            reads=[(("hT", g % 2), kc)] + wkeys("wrg", col0 + j * 128, col0 + (j + 1) * 128), writes=[("bank", bi)], sig=(kc == 7))
                k.op("act", lambda e: e.activation(out=xrb[:, j, 3:515], in_=banks[bx_], func=AF.Copy),
                     reads=[("bank", bx_)], writes=[("xr", j)])
                k.op("act", lambda e: e.activation(out=gg[:, par, jj, :], in_=banks[bg_], func=AF.Gelu_apprx_tanh),
                     reads=[("bank", bg_)], writes=[("gg", par, jj)])

        def stageA2(b):
            g, jb, par = b // 2, (b % 2) * JB, b % 2
            for jj in range(JB):
                j = jb + jj
                cvb = 6 + (jj % 2)
                for tap in range(4):
                    k.op("pe", lambda e, tap=tap: e.matmul(banks[cvb], lhsT=Wd[:, j, tap, :], rhs=xrb[:, j, tap:tap + 512],
                                                          start=(tap == 0), stop=(tap == 3)),
                         reads=[("xr", j), ("xrh", j), ("Wd", j, tap)], writes=[("bank", cvb)], sig=(tap == 3))
                k.op("dve", lambda e: e.tensor_scalar(out=xcb[:, par, jj, :], in0=banks[cvb], scalar1=cb[:, j:j + 1], scalar2=None, op0=ALU.add),
                     reads=[("bank", cvb), "cb"], writes=[("xcb", par, jj)])
                k.op("pool", lambda e: e.tensor_copy(out=xrb[:, j, 0:3], in_=xrb[:, j, 512:515]), reads=[("xr", j)], writes=[("xrh", j)])
            for jj in range(JB):
                j = jb + jj
                bx_ = 2 + (jj % 2)
                br_ = 4 + (jj % 2)
                k.op("pe", lambda e: e.matmul(banks[br_], lhsT=wab[:, j, :], rhs=xcb[:, par, jj, :], start=True, stop=True),
                     reads=[("xcb", par, jj), "wab"], writes=[("bank", br_)])
                k.op("act", lambda e: e.activation(out=trr[:, par, jj, :], in_=banks[br_], func=AF.Tanh, scale=0.5, bias=hba[:, j:j + 1]),
                     reads=[("bank", br_), "hba"], writes=[("trr", par, jj)])
                k.op("pe", lambda e: e.matmul(banks[bx_], lhsT=wxb[:, j, :], rhs=xcb[:, par, jj, :], start=True, stop=True),
                     reads=[("xcb", par, jj), "wxb"], writes=[("bank", bx_)])
                k.op("act", lambda e: e.activation(out=tii[:, par, jj, :], in_=banks[bx_], func=AF.Tanh, scale=0.5, bias=hbx[:, j:j + 1]),
                     reads=[("bank", bx_), "hbx"], writes=[("tii", par, jj)])

        def stageBC(b):
            g, jb, par = b // 2, (b % 2) * JB, b % 2
            for jj in range(JB):
                j = jb + jj
                k.op("act", lambda e: e.activation(out=aa[:, par, jj, :], in_=trr[:, par, jj, :], func=AF.Exp, scale=c1[:, j:j + 1], bias=c1[:, j:j + 1]),
                     reads=[("trr", par, jj), "c1"], writes=[("aa", par, jj)])
                k.op("act", lambda e: e.activation(out=sqv[:, par, jj, :], in_=trr[:, par, jj, :], func=AF.Exp, scale=c2[:, j:j + 1], bias=c2[:, j:j + 1]),
                     reads=[("trr", par, jj), "c2"], writes=[("sqv", par, jj)])
            for jj in range(JB):
                k.op("act", lambda e: e.activation(out=sqv[:, par, jj, :], in_=sqv[:, par, jj, :], func=AF.Sqrt, scale=-0.25, bias=0.25),
                     reads=[("sqv", par, jj)], writes=[("sqv", par, jj)])

        def stageD(b):
            g, jb, par = b // 2, (b % 2) * JB, b % 2
            for jj in range(JB):
                j = jb + jj
                ui = jj % 2
                k.op("dve", lambda e: e.scalar_tensor_tensor(out=uu[:, jj, :], in0=tii[:, par, jj, :], scalar=1.0, in1=xcb[:, par, jj, :],
                                                             op0=ALU.add, op1=ALU.mult),
                     reads=[("tii", par, jj), ("xcb", par, jj)], writes=[("uu", jj)])
                k.op("pool", lambda e: e.tensor_tensor(out=uu[:, jj, :], in0=uu[:, jj, :], in1=sqv[:, par, jj, :], op=ALU.mult),
                     reads=[("uu", jj), ("sqv", par, jj)], writes=[("uu", jj)])
            for jj in range(JB):
                j = jb + jj
                ui = jj % 2
                k.op("dve", lambda e: e.tensor_tensor_scan(out=hs[:, ui, :], data0=aa[:, par, jj, :], data1=uu[:, jj, :],
                                                           initial=hstate[:, j:j + 1], op0=ALU.mult, op1=ALU.add),
                     reads=[("aa", par, jj), ("uu", jj), ("hstate", j), "hstate"], writes=[("hs", ui)])
                k.op("dve", lambda e: e.tensor_copy(out=hstate[:, j:j + 1], in_=hs[:, ui, 511:512]),
                     reads=[("hs", ui)], writes=[("hstate", j)])
                k.op("pool", lambda e: e.tensor_tensor(out=yst[:, j, :], in0=hs[:, ui, :], in1=gg[:, par, jj, :], op=ALU.mult),
                     reads=[("hs", ui), ("gg", par, jj)], writes=[("yst", j)])
            if b % 2 == 1:
                k.dma("sp", "yst", yrT_s.rearrange("j p t -> p j t")[:, :, g * 512:(g + 1) * 512], yst[:],
                      reads=[("yst", j) for j in range(8)], writes=[("yrT_s", g)])

        NB2 = 2 * NG
        load_x(0)
        np1(0)
        if NG > 1:
            load_x(1)
        np2(0)
        stageA1(0)
        stageA2(0)
        for b in range(NB2):
            g = b // 2
            nxt = (b % 2 == 0 and g + 1 < NG)
            if nxt:
                np1(g + 1)
                if g + 2 < NG:
                    load_x(g + 2)
            if b + 1 < NB2:
                stageA1(b + 1)
            if nxt:
                np2(g + 1)
            stageBC(b)
            if b + 1 < NB2:
                stageA2(b + 1)
            stageD(b)
        k.barrier()

    if upto <= 2:
        k.barrier()
        return nc, k
    with ExitStack() as es:
     kTh = es.enter_context(nc.sbuf_tensor("s_kTh", [128, 2, S], BF16))
     qTh = es.enter_context(nc.sbuf_tensor("s_qTh", [128, 2, S], BF16))
     vh = es.enter_context(nc.sbuf_tensor("s_vh", [128, 2, NT, 130], BF16))
     Pb = es.enter_context(nc.sbuf_tensor("s_Pb", [128, 3, 2, 512], BF16))
     accs = es.enter_context(nc.sbuf_tensor("s_accs", [128, 2, 3, 390], F32))
     rc = es.enter_context(nc.sbuf_tensor("s_rc", [128, 2, 8], F32))
     t0 = es.enter_context(nc.sbuf_tensor("s_t0", [128, 2, 128], F32))
     yv = es.enter_context(nc.sbuf_tensor("s_yv", [128, 4, 128], F32))
     yj = es.enter_context(nc.sbuf_tensor("s_yj", [128, 128], F32))
     ss2 = es.enter_context(nc.sbuf_tensor("s_ss2", [128, 4], F32))
     rs2 = es.enter_context(nc.sbuf_tensor("s_rs2", [128, 4], F32))
     ynb = es.enter_context(nc.sbuf_tensor("s_ynb", [128, 2, 4, 128], BF16))
     yTst = es.enter_context(nc.sbuf_tensor("s_yTst", [128, 2, 512], BF16))
     wadc = es.enter_context(nc.sbuf_tensor("s_wadc", [128, 2, 8, 128], F32))
     if True:
        k.op("pool", lambda e: e.memset(vh[:, :, :, 128:130], 1.0), writes=[("vones",)])

        def load_head(h):
            hp = h % 2
            k.dma("sp", "kTh%d" % hp, kTh[:, hp, :], kT_s[h], writes=[("kTh", hp)])
            k.dma("sp", "qTh%d" % hp, qTh[:, hp, :], qT_s[h], writes=[("qTh", hp)])
            k.dma("sp", "vh%d" % hp, vh[:, hp, :, 0:128], v_s[:, h * 128:(h + 1) * 128].rearrange("(t p) d -> p t d", p=128),
                  writes=[("vh", hp)])

        def acc_loc(c, i):
            a = c * 4 + i
            return a // 3, (a % 3) * 130

        tb7 = banks[7].bitcast(BF16)
        ps_mod2 = banks[7][:, 256:352].rearrange("p (j t) -> p j t", t=2)
        NMC = 32
        mc_state = {"next": 0}

        def mod_chunk_load(i):
            j = 16 + i
            k.dma("sp", "wadc%d" % (i % 2), wadc[:, i % 2], wada_d[:, j * 128:(j + 1) * 128].rearrange("(c p) n -> p c n", p=128),
                  writes=[("wadc", i % 2)])

        def mod_chunk_mm(i):
            j = 16 + i
            for kc in range(8):
                k.op("pe", lambda e, kc=kc: e.matmul(ps_mod2[:, j, :], lhsT=wadc[:, i % 2, kc, :], rhs=c_act2[:, kc, :],
                                                     start=(kc == 0), stop=(kc == 7)),
                     reads=[("wadc", i % 2), "ca0", "ca1"], writes=[("bank", 7)], sig=(kc == 7))

        def mod_chunk_step():
            i = mc_state["next"]
            if i >= NMC:
                return
            mod_chunk_mm(i)
            if i + 2 < NMC:
                mod_chunk_load(i + 2)
            mc_state["next"] = i + 1

        mod_chunk_load(0)
        mod_chunk_load(1)
        steps = [(h, qg, kt) for h in range(NH) for qg in range(NG) for kt in range(4 * qg + 4)]
        NSTEP = len(steps)

        def emit_qk(n):
            h, qg, kt = steps[n]
            hp = h % 2
            q0 = qg * 512
            j = kt - 4 * qg
            lo = max(j, 0) * 128
            sb_ = n % 2
            pb_ = n % 3
            sbanks = (2 * sb_, 2 * sb_ + 1)
            for c in (0, 1):
                k.op("pe", lambda e, c=c: e.matmul(
                    banks[sbanks[c]][:, lo:512], lhsT=kTh[c * 64:(c + 1) * 64, hp, kt * 128:(kt + 1) * 128],
                    rhs=qTh[c * 64:(c + 1) * 64, hp, q0 + lo:q0 + 512], start=True, stop=True),
                    reads=[("kTh", hp), ("qTh", hp)], writes=[("bank", sbanks[c])], sig=(c == 1))
            k.op("act", lambda e: e.activation(
                out=Pb[:, pb_, :, lo:512], in_=psall[:, 2 * sb_:2 * sb_ + 2, lo:512], func=AF.Exp, scale=0.125),
                reads=[("bank", sbanks[0]), ("bank", sbanks[1])], writes=[("Pb", pb_)])
            if j >= 0:
                k.op("pool", lambda e: e.tensor_tensor(
                    out=Pb[:, pb_, :, lo:lo + 128], in0=Pb[:, pb_, :, lo:lo + 128],
                    in1=trimask[:].unsqueeze(1).to_broadcast([128, 2, 128]), op=ALU.mult),
                    reads=[("Pb", pb_), "trimask"], writes=[("Pb", pb_)])

        started = set()

        def emit_pv(n):
            h, qg, kt = steps[n]
            hp = h % 2
            j = kt - 4 * qg
            pb_ = n % 3
            if kt == 0:
                started.clear()
            for c in (0, 1):
                for i in range(max(j, 0), 4):
                    bo, off = acc_loc(c, i)
                    bk = 4 + bo
                    st = bk not in started
                    started.add(bk)
                    last = (c == 1 and i == 3)
                    k.op("pe", lambda e, bk=bk, off=off, c=c, i=i, st=st: e.matmul(
                        banks[bk][:, off:off + 129], lhsT=Pb[:, pb_, c, i * 128:(i + 1) * 128], rhs=vh[:, hp, kt, 0:129],
                        start=st, stop=(kt == 4 * qg + i), skip_group_check=True),
                        reads=[("Pb", pb_), ("vh", hp), ("vones",)], writes=[("bank", bk)], sig=last)

        def epilogue(h, qg):
            q0 = qg * 512
            gpar = (h * NG + qg) % 2
            for bo in range(3):
                k.op("dve", lambda e, bo=bo: e.tensor_copy(out=accs[:, gpar, bo, :], in_=banks[4 + bo][:, 0:390]),
                     reads=[("bank", 4 + bo)], writes=[("accs", gpar, bo)])
            for i in range(4):
                b0_, o0 = acc_loc(0, i)
                b1_, o1 = acc_loc(1, i)
                ip = i % 2
                k.op("dve", lambda e, ip=ip, b0_=b0_, o0=o0: e.reciprocal(out=rc[:, ip, 0:1], in_=accs[:, gpar, b0_, o0 + 128:o0 + 129]),
                     reads=[("accs", gpar, b0_)], writes=[("rc0", ip)])
                k.op("dve", lambda e, ip=ip, b1_=b1_, o1=o1: e.reciprocal(out=rc[:, ip, 1:2], in_=accs[:, gpar, b1_, o1 + 128:o1 + 129]),
                     reads=[("accs", gpar, b1_)], writes=[("rc1", ip)])
                k.op("dve", lambda e, ip=ip: e.tensor_tensor(out=rc[:, ip, 2:3], in0=rc[:, ip, 1:2], in1=neg_lam[:], op=ALU.mult),
                     reads=[("rc1", ip), "neg_lam"], writes=[("rc2", ip)])
                k.op("dve", lambda e, ip=ip, b0_=b0_, o0=o0: e.tensor_scalar(out=t0[:, ip, :], in0=accs[:, gpar, b0_, o0:o0 + 128], scalar1=rc[:, ip, 0:1],
                                                                           scalar2=None, op0=ALU.mult),
                     reads=[("accs", gpar, b0_), ("rc0", ip)], writes=[("t0", ip)])
                k.op("dve", lambda e, i=i, ip=ip, b1_=b1_, o1=o1: e.scalar_tensor_tensor(out=yv[:, i, :], in0=accs[:, gpar, b1_, o1:o1 + 128], scalar=rc[:, ip, 2:3],
                                                                                     in1=t0[:, ip, :], op0=ALU.mult, op1=ALU.add),
                     reads=[("accs", gpar, b1_), ("rc2", ip), ("t0", ip)], writes=[("yv", i)])
                k.op("dve", lambda e, i=i: e.scalar_tensor_tensor(out=yj[:], in0=yv[:, i, :], scalar=1.0, in1=yv[:, i, :], op0=ALU.mult, op1=ALU.mult,
                                                                  accum_out=ss2[:, i:i + 1]),
                     reads=[("yv", i)], writes=[("ss2", i), "yj"])
            k.op("pool", lambda e: e.tensor_scalar(out=rs2[:], in0=ss2[:], scalar1=1.0 / 128, scalar2=EPS, op0=ALU.mult, op1=ALU.add),
                 reads=[("ss2", i) for i in range(4)], writes=["rs2"])
            k.op("pool", lambda e: e.tensor_tensor(out=rs2[:], in0=rs2[:], in1=mhalf[:, 0:4], op=ALU.pow), reads=["rs2", "mhalf"], writes=["rs2"])
            for i in range(4):
                k.op("dve", lambda e, i=i: e.scalar_tensor_tensor(out=ynb[:, gpar, i, :], in0=yv[:, i, :], scalar=rs2[:, i:i + 1], in1=subg_b[:],
                                                                  op0=ALU.mult, op1=ALU.mult),
                     reads=[("yv", i), "rs2", "subg_b"], writes=[("ynb", gpar, i)])

            def fin():
                for i in range(4):
                    k.op("pe", lambda e, i=i: e.transpose(tb7[:, i * 128:(i + 1) * 128], ynb[:, gpar, i, :], ident[:]),
                         reads=[("ynb", gpar, i), "ident"], writes=[("bank", 7)], sig=(i == 3))
                k.op("dve", lambda e: e.tensor_copy(out=yTst[:, gpar, :], in_=tb7[:, 0:512]), reads=[("bank", 7)], writes=[("yTst", gpar)])
                k.dma("sp", "yTst%d" % gpar, yaT_s[h][:, q0:q0 + 512], yTst[:, gpar, :], reads=[("yTst", gpar)], writes=[("yaT_s", h, qg)])
                mod_chunk_step()

            return fin

        load_head(0)
        emit_qk(0)
        if NSTEP > 1:
            emit_qk(1)
        pending = None
        age = 0
        for n in range(NSTEP):
            h, qg, kt = steps[n]
            if qg == 0 and kt == 0 and h + 1 < NH:
                load_head(h + 1)
            if n + 2 < NSTEP:
                emit_qk(n + 2)
            emit_pv(n)
            age += 1
            last = (kt == 4 * qg + 3)
            if pending is not None and (age >= 10 or last):
                pending()
                pending = None
            if last:
                pending = epilogue(h, qg)
                age = 0
        if pending is not None:
            pending()
            pending = None
        while mc_state["next"] < NMC:
            mod_chunk_step()
        k.op("dve", lambda e: e.tensor_tensor(out=modT[:, 16:48], in0=ps_mod2[:, 16:48, 0], in1=badaT[:, 16:48], op=ALU.add),
             reads=[("bank", 7), "badaT"], writes=["modT2"])
        k.op("dve", lambda e: e.tensor_scalar(out=sc2p[:], in0=modT[:, 32:40], scalar1=1.0, scalar2=None, op0=ALU.add), reads=["modT2"], writes=["sc2p"])
        for j in range(8):
            k.dma("sp", "gst", gate_s[0:1, j * 128:(j + 1) * 128].rearrange("o p -> p o"), modT[:, 16 + j:17 + j], reads=["modT2"], writes=[("gate_s", j)])
            k.dma("sp", "gst", gate_s[0:1, D + j * 128:D + (j + 1) * 128].rearrange("o p -> p o"), modT[:, 40 + j:41 + j], reads=["modT2"], writes=[("gate_s", 8 + j)])
        k.barrier(engines=("sp",))
        k.dma("sp", "g1b", g1_b[:], gate_s[0:1, 0:D].partition_broadcast(128), writes=["g1_b"])
        k.dma("sp", "g2b", g2_b[:], gate_s[0:1, D:2 * D].partition_broadcast(128), writes=["g2_b"])
        k.barrier()

    if upto <= 3:
        k.barrier()
        return nc, k
    with ExitStack() as es:
     wgm = es.enter_context(nc.sbuf_tensor("s_wgm", [128, 8, 2048], BF16))
     wpr = es.enter_context(nc.sbuf_tensor("s_wpr", [128, 8, D], BF16))
     wpa = es.enter_context(nc.sbuf_tensor("s_wpa", [128, 8, D], BF16))
     wo = es.enter_context(nc.sbuf_tensor("s_wo", [128, 8, D], BF16))
     xtA = es.enter_context(nc.sbuf_tensor("s_c_xtA", [128, 4, D], F32))
     xtB = es.enter_context(nc.sbuf_tensor("s_c_xtB", [128, 4, D], F32))
     xn = es.enter_context(nc.sbuf_tensor("s_c_xn", [128, 4, D], BF16))
     junk = es.enter_context(nc.sbuf_tensor("s_c_junk", [128, D], BF16))
     hTd = es.enter_context(nc.sbuf_tensor("s_c_hT", [128, 2, 8, 512], BF16))
     ssq = es.enter_context(nc.sbuf_tensor("s_c_ssq", [128, 4], F32))
     rstd = es.enter_context(nc.sbuf_tensor("s_c_rstd", [128, 4], F32))
     yr = es.enter_context(nc.sbuf_tensor("s_yr", [128, 2, 8, 512], BF16))
     ya = es.enter_context(nc.sbuf_tensor("s_ya", [128, 2, 8, 512], BF16))
     sg = es.enter_context(nc.sbuf_tensor("s_sg", [128, 2, 2, 512], BF16))
     mm = es.enter_context(nc.sbuf_tensor("s_mm", [128, 2, 2, 512], F32))
     mg = es.enter_context(nc.sbuf_tensor("s_mg", [128, 8, 512], BF16))
     tt = es.enter_context(nc.sbuf_tensor("s_c_tt", [128, 2, 512], F32))
     if True:
        load_w_cast("wgm", wgm, win_d, 0, 5120, 2048, 8)
        load_w_cast("wpr", wpr, wpr_d, 0, 0, D, 8)
        load_w_cast("wpa", wpa, wpa_d, 0, 0, D, 8)
        load_w_cast("wo", wo, wo_d, 0, 0, D, 8)
        xts = [xtA, xtB]

        def load_g(g):
            gp = g % 2
            k.dma("sp", "xt%d" % gp, xts[gp][:], x_d[g * 512:(g + 1) * 512, :].rearrange("(s p) d -> p s d", p=128), writes=[("xt", gp)])
            k.dma("sp", "yr%d" % gp, yr[:, gp, :, :], yrT_s.rearrange("j p t -> p j t")[:, :, g * 512:(g + 1) * 512], writes=[("yr", gp)])
            k.dma("sp", "ya%d" % gp, ya[:, gp, :, :], yaT_s.rearrange("j p t -> p j t")[:, :, g * 512:(g + 1) * 512], writes=[("ya", gp)])

        def np1_3a(g):
            norm_p1("n3a", xts[g % 2], ("xt", g % 2), 4, xn, junk, ssq, rstd)

        def np2_3a(g):
            norm_p2("n3a", sc1p, modT[:, 0:8], hTd[:, g % 2], ("hT", g % 2), 4, xn, [0, 1], "alt")

        load_g(0)
        np1_3a(0)
        np2_3a(0)
        for g in range(NG):
            if g + 1 < NG:
                load_g(g + 1)
            gp = g % 2
            xt = xts[gp]
            hT = hTd[:, gp]
            for j in range(8):
                if j == 4 and g + 1 < NG:
                    np1_3a(g + 1)
                jp = j % 2
                for br, (col0, bi) in enumerate(((0, 2), (1024, 3))):
                    for kc in range(8):
                        k.op("pe", lambda e, bi=bi, kc=kc, col0=col0, j=j: e.matmul(
                            banks[bi], lhsT=wgm[:, kc, col0 + j * 128:col0 + (j + 1) * 128], rhs=hT[:, kc, :], start=(kc == 0), stop=(kc == 7)),
                            reads=[(("hT", gp), kc)] + wkeys("wgm", col0 + j * 128, col0 + (j + 1) * 128), writes=[("bank", bi)], sig=(kc == 7))
                    k.op("act", lambda e, bi=bi, br=br, jp=jp: e.activation(out=sg[:, jp, br, :], in_=banks[bi], func=AF.Sigmoid),
                         reads=[("bank", bi)], writes=[("sg", jp, br)])
                for br, (wsb, wn, src, sk, bi) in enumerate(((wpr, "wpr", yr, "yr", 4), (wpa, "wpa", ya, "ya", 5))):
                    for kc in range(8):
                        k.op("pe", lambda e, bi=bi, kc=kc, j=j, wsb=wsb, src=src: e.matmul(
                            banks[bi], lhsT=wsb[:, kc, j * 128:(j + 1) * 128], rhs=src[:, gp, kc, :], start=(kc == 0), stop=(kc == 7)),
                            reads=[(sk, gp)] + wkeys(wn, j * 128, (j + 1) * 128), writes=[("bank", bi)], sig=(kc == 7))
                    k.op("dve", lambda e, bi=bi, br=br, jp=jp: e.tensor_tensor(out=mm[:, jp, br, :], in0=banks[bi], in1=sg[:, jp, br, :], op=ALU.mult),
                         reads=[("bank", bi), ("sg", jp, br)], writes=[("mm", jp, br)])
                k.op("pool", lambda e, j=j, jp=jp: e.tensor_tensor(out=mg[:, j, :], in0=mm[:, jp, 0, :], in1=mm[:, jp, 1, :], op=ALU.add),
                     reads=[("mm", jp, 0), ("mm", jp, 1)], writes=[("mg", j)])
            if g + 1 < NG:
                np2_3a(g + 1)
            for s in range(4):
                for half in range(2):
                    bi = 6 + half
                    hsl = slice(half * 512, (half + 1) * 512)
                    for kc in range(8):
                        k.op("pe", lambda e, bi=bi, kc=kc, s=s, hsl=hsl: e.matmul(
                            banks[bi], lhsT=mg[:, kc, s * 128:(s + 1) * 128], rhs=wo[:, kc, hsl], start=(kc == 0), stop=(kc == 7)),
                            reads=[("mg", kc)] + wkeys("wo", half * 512, (half + 1) * 512), writes=[("bank", bi)], sig=(kc == 7))
                    k.op("dve", lambda e, bi=bi, half=half, hsl=hsl: e.tensor_tensor(out=tt[:, half, :], in0=banks[bi], in1=g1_b[:, hsl], op=ALU.mult),
                         reads=[("bank", bi), "g1_b"], writes=[("tt", half)])
                    k.op("pool", lambda e, s=s, half=half, hsl=hsl, xt=xt: e.tensor_tensor(out=xt[:, s, hsl], in0=tt[:, half, :], in1=xt[:, s, hsl], op=ALU.add),
                         reads=[("tt", half), ("xt", gp)], writes=[("xt", gp)])
            k.dma("sp", "x1t%d" % gp, x1_s[g * 512:(g + 1) * 512, :].rearrange("(s p) d -> p s d", p=128), xt[:],
                  reads=[("xt", gp)], writes=[("x1_s", g)])
        k.barrier()

    if upto <= 4:
        k.barrier()
        return nc, k
    NG3 = S // 256
    with ExitStack() as es:
     wf1 = es.enter_context(nc.sbuf_tensor("s_wf1", [128, 8, DFF], BF16))
     wf2 = es.enter_context(nc.sbuf_tensor("s_wf2", [128, 32, D], BF16))
     x1A = es.enter_context(nc.sbuf_tensor("s_x1A", [128, 2, D], F32))
     x1B = es.enter_context(nc.sbuf_tensor("s_x1B", [128, 2, D], F32))
     xn = es.enter_context(nc.sbuf_tensor("s_d_xn", [128, 2, D], BF16))
     junk = es.enter_context(nc.sbuf_tensor("s_d_junk", [128, D], BF16))
     h2Td = es.enter_context(nc.sbuf_tensor("s_h2T", [128, 2, 8, 256], BF16))
     ssq = es.enter_context(nc.sbuf_tensor("s_d_ssq", [128, 4], F32))
     rstd = es.enter_context(nc.sbuf_tensor("s_d_rstd", [128, 4], F32))
     sq3 = es.enter_context(nc.sbuf_tensor("s_sq3", [128, 2, 256], F32))
     uT = es.enter_context(nc.sbuf_tensor("s_uT", [128, 32, 256], BF16))
     tt = es.enter_context(nc.sbuf_tensor("s_d_tt", [128, 2, 512], F32))
     if True:
        load_w_cast("wf1", wf1, wf1_d, 0, 0, DFF, 8)
        for fg in range(4):
            k.dma_group("pool", "w_wf2_%d" % fg, [(wf2[:, fk, :], wf2_d[fk * 128:(fk + 1) * 128, :]) for fk in range(fg * 8, fg * 8 + 8)], [("wf2", fg)])
        x1s = [x1A, x1B]

        def load_x1(g):
            k.dma("sp", "x1l%d" % (g % 2), x1s[g % 2][:], x1_s[g * 256:(g + 1) * 256, :].rearrange("(s p) d -> p s d", p=128), writes=[("x1", g % 2)])

        def np1_3b(g):
            norm_p1("n3b", x1s[g % 2], ("x1", g % 2), 2, xn, junk, ssq, rstd)

        def np2_3b(g):
            norm_p2("n3b", sc2p, modT[:, 24:32], h2Td[:, g % 2], ("h2T", g % 2), 2, xn, [0, 1, 7], "alt")

        load_x1(0)
        np1_3b(0)
        np2_3b(0)
        if NG3 > 1:
            load_x1(1)
        for g in range(NG3):
            gp = g % 2
            x1 = x1s[gp]
            h2T = h2Td[:, gp]
            for f in range(32):
                bi = 2 + (f % 3)
                fp = f % 2
                for kc in range(8):
                    k.op("pe", lambda e, bi=bi, kc=kc, f=f: e.matmul(
                        banks[bi][:, 0:256], lhsT=wf1[:, kc, f * 128:(f + 1) * 128], rhs=h2T[:, kc, :], start=(kc == 0), stop=(kc == 7)),
                        reads=[(("h2T", gp), kc)] + wkeys("wf1", f * 128, (f + 1) * 128), writes=[("bank", bi)], sig=(kc == 7))
                k.op("act", lambda e, bi=bi, fp=fp: e.activation(out=sq3[:, fp, :], in_=banks[bi][:, 0:256], func=AF.Square),
                     reads=[("bank", bi)], writes=[("sq3", fp)])
                k.op("dve", lambda e, bi=bi, fp=fp, f=f: e.scalar_tensor_tensor(out=uT[:, f, :], in0=banks[bi][:, 0:256], scalar=0.0, in1=sq3[:, fp, :],
                                                                              op0=ALU.is_gt, op1=ALU.mult),
                     reads=[("bank", bi), ("sq3", fp)], writes=[("uT", f)])
                if f == 15 and g + 1 < NG3:
                    np1_3b(g + 1)
            for s in range(2):
                for half in range(2):
                    if s == 0 and half == 1 and g + 1 < NG3:
                        np2_3b(g + 1)
                    bi = 5 + ((s * 2 + half) % 2)
                    hsl = slice(half * 512, (half + 1) * 512)
                    for f in range(32):
                        k.op("pe", lambda e, bi=bi, f=f, s=s, hsl=hsl: e.matmul(
                            banks[bi], lhsT=uT[:, f, s * 128:(s + 1) * 128], rhs=wf2[:, f, hsl], start=(f == 0), stop=(f == 31)),
                            reads=[("uT", f), ("wf2", f // 8)], writes=[("bank", bi)], sig=(f == 31))
                    k.op("dve", lambda e, bi=bi, half=half, hsl=hsl: e.tensor_tensor(out=tt[:, half, :], in0=banks[bi], in1=g2_b[:, hsl], op=ALU.mult),
                         reads=[("bank", bi), "g2_b"], writes=[("tt", half)])
                    k.op("pool", lambda e, s=s, half=half, hsl=hsl, x1=x1: e.tensor_tensor(out=x1[:, s, hsl], in0=tt[:, half, :], in1=x1[:, s, hsl], op=ALU.add),
                         reads=[("tt", half), ("x1", gp)], writes=[("x1", gp)])
            k.dma("sp", "outst%d" % gp, out_d[g * 256:(g + 1) * 256, :].rearrange("(s p) d -> p s d", p=128), x1[:],
                  reads=[("x1", gp)], writes=[("out", g)])
            if g + 2 < NG3:
                load_x1(g + 2)
        k.barrier()
    return nc, k


_CACHE = {}


def _prep_inputs(S, x, c, positions, w_ada, b_ada, w_in, conv_w, conv_b, rglru_wa, rglru_ba, rglru_wx, rglru_bx,
                 rglru_lambda, q_norm_gain, k_norm_gain, lambda_q1, lambda_k1, lambda_q2, lambda_k2, subln_gain,
                 w_proj_rnn, w_proj_attn, w_out, w_ff1, w_ff2):
    f = lambda a: np.ascontiguousarray(np.asarray(a, dtype=np.float32))
    B = x.shape[0]
    NT = S // 128
    col8 = lambda v: f(np.asarray(v).reshape(8, 128).T)
    invf = (10000.0 ** (-np.arange(0, 64, 2, dtype=np.float32) / 64.0)).astype(np.float32)
    shared = {
        "invf": f(np.broadcast_to(invf[None, :], (128, 32))),
        "w_ada": f(w_ada[0]),
        "b_adaT": f(np.asarray(b_ada[0]).reshape(48, 128).T),
        "w_in": f(w_in[0]),
        "conv_wT": f(np.asarray(conv_w[0]).reshape(4, 8, 128).transpose(2, 1, 0).reshape(128, 32)),
        "conv_bT": col8(conv_b[0]),
        "rglru_wa": f(rglru_wa[0]),
        "rglru_wx": f(rglru_wx[0]),
        "baT": col8(rglru_ba[0]),
        "bxT": col8(rglru_bx[0]),
        "lamT": col8(rglru_lambda[0]),
        "gq": f(np.asarray(q_norm_gain[0]).reshape(1, 64)),
        "gk": f(np.asarray(k_norm_gain[0]).reshape(1, 64)),
        "lq1": f(np.asarray(lambda_q1[0]).reshape(1, 64)),
        "lk1": f(np.asarray(lambda_k1[0]).reshape(1, 64)),
        "lq2": f(np.asarray(lambda_q2[0]).reshape(1, 64)),
        "lk2": f(np.asarray(lambda_k2[0]).reshape(1, 64)),
        "subg": f(np.asarray(subln_gain[0]).reshape(1, 128)),
        "w_proj_rnn": f(w_proj_rnn[0]),
        "w_proj_attn": f(w_proj_attn[0]),
        "w_out": f(w_out[0]),
        "w_ff1": f(w_ff1[0]),
        "w_ff2": f(w_ff2[0]),
    }
    in_maps = []
    for b in range(B):
        m = dict(shared)
        m["x"] = f(x[b])
        m["cT"] = f(np.asarray(c[b]).reshape(8, 128).T)
        m["posT"] = np.ascontiguousarray(np.asarray(positions[b], dtype=np.int32).reshape(NT, 128).T)
        in_maps.append(m)
    return in_maps


def kernel(**inputs):
    x = np.asarray(inputs["x"])
    B, S, _ = x.shape
    if S not in _CACHE:
        _CACHE[S] = build(S)[0]
    nc = _CACHE[S]
    in_maps = _prep_inputs(S, **inputs)
    res = run_bass_kernel_spmd(nc, in_maps, core_ids=list(range(B)))
    return np.stack([np.asarray(r["out"], dtype=np.float32).reshape(S, D) for r in res.results], axis=0)
```
